# Optimizing a Trainium2 kernel written in Bass

```python
import math
import jax, jax.numpy as jnp
from jax import lax
import numpy as np

D_MODEL = 2048
BATCH = 4
SEQ = 2048
DEPTH = 1
DEC_BATCH = 128
DEC_SEQ = 8
PAST_LEN = 16384
PAGE_SIZE = 128

N_META = 16
CONV_W = 3
D_CONV = D_MODEL // 2
GLA_HEADS = 4
GLA_DK = (D_MODEL // 2) // GLA_HEADS
GLA_DV = D_MODEL // GLA_HEADS
GATE_RANK = 16
GATE_TAU = 16.0
GLA_CHUNK = 16
D_FF = 5632
EPS = 1e-6

SPLIT_SIZES = (D_CONV, D_CONV, D_CONV,
               GLA_HEADS * GLA_DK, GLA_HEADS * GLA_DK,
               GLA_HEADS * GLA_DV, GLA_HEADS * GLA_DV,
               GATE_RANK, D_MODEL, D_MODEL)
IN_COLS = sum(SPLIT_SIZES)
SPLIT_POINTS = tuple(int(p) for p in np.cumsum(SPLIT_SIZES)[:-1])

kernel_name = "hybrid_conv_gla_convffn_step"


def rmsnorm(x, g):
    xf = x.astype(jnp.float32)
    y = xf * lax.rsqrt(jnp.mean(xf * xf, axis=-1, keepdims=True) + EPS)
    return (y * g.astype(jnp.float32)).astype(x.dtype)


def causal_dwconv(x, prev, w):
    L = x.shape[1]
    xp = jnp.concatenate([prev.astype(x.dtype), x], axis=1)
    y = w[0] * xp[:, 0:L] + w[1] * xp[:, 1:L + 1] + w[2] * xp[:, 2:L + 2]
    return y, xp[:, -(CONV_W - 1):]


def gla_chunked(q, k, v, logg, s0, chunk):
    f32 = jnp.float32
    Bsz, L, H, DK = q.shape
    DV = v.shape[-1]
    n = L // chunk
    q = q.astype(f32).reshape(Bsz, n, chunk, H, DK)
    k = k.astype(f32).reshape(Bsz, n, chunk, H, DK)
    v = v.astype(f32).reshape(Bsz, n, chunk, H, DV)
    logg = logg.astype(f32).reshape(Bsz, n, chunk, H, DK)
    b = jnp.cumsum(logg, axis=2)
    b_last = b[:, :, -1]
    qe = q * jnp.exp(b)
    ke = k * jnp.exp(-b)
    kd = k * jnp.exp(b_last[:, :, None] - b)
    mask = jnp.tril(jnp.ones((chunk, chunk), dtype=bool))
    att = jnp.einsum('bnihk,bnjhk->bnhij', qe, ke)
    att = jnp.where(mask, att, 0.0)
    o_intra = jnp.einsum('bnhij,bnjhv->bnihv', att, v)

    def step(S, inp):
        qe_c, kd_c, v_c, dl_c = inp
        o = jnp.einsum('bihk,bhkv->bihv', qe_c, S)
        S = S * dl_c[..., None] + jnp.einsum('bjhk,bjhv->bhkv', kd_c, v_c)
        return S, o

    xs = (jnp.moveaxis(qe, 1, 0), jnp.moveaxis(kd, 1, 0), jnp.moveaxis(v, 1, 0),
          jnp.moveaxis(jnp.exp(b_last), 1, 0))
    S, o_inter = lax.scan(step, s0.astype(f32), xs)
    o = o_intra + jnp.moveaxis(o_inter, 0, 1)
    return o.reshape(Bsz, L, H, DV), S


def layer(x, conv_prev, gla_prev, ffn_prev, chunk,
          norm_mix_g, w_in, conv_mix_w, w_conv_out, w_gate_up, b_gate, gla_norm_g,
          w_gla_out, w_o, norm_ffn_g, w_ffn_up, ffn_conv_w, ffn_conv_b, w_ffn_down):
    Bsz, L, _ = x.shape
    n = rmsnorm(x, norm_mix_g)
    proj = n @ w_in
    cb, cc, ch, q, k, v, g, a_lr, gate_a, gate_b = jnp.split(proj, SPLIT_POINTS, axis=-1)
    uc, conv_new = causal_dwconv(cc * ch, conv_prev, conv_mix_w)
    y_a = (cb * uc) @ w_conv_out
    logit = (a_lr @ w_gate_up + b_gate).astype(jnp.float32)
    logg = jax.nn.log_sigmoid(logit) / GATE_TAU
    o, S = gla_chunked((q * GLA_DK ** -0.5).reshape(Bsz, L, GLA_HEADS, GLA_DK),
                       k.reshape(Bsz, L, GLA_HEADS, GLA_DK),
                       v.reshape(Bsz, L, GLA_HEADS, GLA_DV),
                       logg.reshape(Bsz, L, GLA_HEADS, GLA_DK), gla_prev, chunk)
    o = rmsnorm(o, gla_norm_g).astype(x.dtype).reshape(Bsz, L, GLA_HEADS * GLA_DV)
    y_b = (o * jax.nn.silu(g)) @ w_gla_out
    merged = jax.nn.sigmoid(gate_a) * y_a + jax.nn.sigmoid(gate_b) * y_b
    h = x + merged @ w_o
    n2 = rmsnorm(h, norm_ffn_g)
    a, gt = jnp.split(n2 @ w_ffn_up, 2, axis=-1)
    gc, ffn_new = causal_dwconv(gt, ffn_prev, ffn_conv_w)
    ff = (jax.nn.silu(gc + ffn_conv_b) * a) @ w_ffn_down
    return h + ff, conv_new, S, ffn_new


def setup_inputs(seed: int = 0) -> dict:
    key = jax.random.key(seed)
    ks = jax.random.split(key, 24)
    nrm = jax.random.normal
    f32 = jnp.float32
    return {
        "x_prompt": nrm(ks[0], (BATCH, SEQ, D_MODEL), f32),
        "x_sample": nrm(ks[1], (DEC_BATCH, DEC_SEQ, D_MODEL), f32),
        "state_conv": nrm(ks[2], (DEPTH, DEC_BATCH, CONV_W - 1, D_CONV), f32) * 0.5,
        "state_gla": nrm(ks[3], (DEPTH, DEC_BATCH, GLA_HEADS, GLA_DK, GLA_DV), f32) * 0.05,
        "state_ffn_conv": nrm(ks[4], (DEPTH, DEC_BATCH, CONV_W - 1, D_FF), f32),
        "meta_tokens": nrm(ks[5], (N_META, D_MODEL), f32),
        "norm_mix_g": 1.0 + 0.02 * nrm(ks[6], (DEPTH, D_MODEL), f32),
        "w_in": nrm(ks[7], (DEPTH, D_MODEL, IN_COLS), f32) * D_MODEL ** -0.5,
        "conv_mix_w": nrm(ks[8], (DEPTH, CONV_W, D_CONV), f32) * CONV_W ** -0.5,
        "w_conv_out": nrm(ks[9], (DEPTH, D_CONV, D_MODEL), f32) * D_CONV ** -0.5,
        "w_gate_up": nrm(ks[10], (DEPTH, GATE_RANK, GLA_HEADS * GLA_DK), f32) * GATE_RANK ** -0.5,
        "b_gate": 0.1 * nrm(ks[11], (DEPTH, GLA_HEADS * GLA_DK), f32),
        "gla_norm_g": 1.0 + 0.02 * nrm(ks[12], (DEPTH, GLA_DV), f32),
        "w_gla_out": nrm(ks[13], (DEPTH, GLA_HEADS * GLA_DV, D_MODEL), f32) * (GLA_HEADS * GLA_DV) ** -0.5,
        "w_o": nrm(ks[14], (DEPTH, D_MODEL, D_MODEL), f32) * D_MODEL ** -0.5,
        "norm_ffn_g": 1.0 + 0.02 * nrm(ks[15], (DEPTH, D_MODEL), f32),
        "w_ffn_up": nrm(ks[16], (DEPTH, D_MODEL, 2 * D_FF), f32) * D_MODEL ** -0.5,
        "ffn_conv_w": nrm(ks[17], (DEPTH, CONV_W, D_FF), f32) * CONV_W ** -0.5,
        "ffn_conv_b": 0.02 * nrm(ks[18], (DEPTH, D_FF), f32),
        "w_ffn_down": nrm(ks[19], (DEPTH, D_FF, D_MODEL), f32) * D_FF ** -0.5,
        "final_norm_g": 1.0 + 0.02 * nrm(ks[20], (D_MODEL,), f32),
    }


def reference(x_prompt, x_sample, state_conv, state_gla, state_ffn_conv, meta_tokens,
              norm_mix_g, w_in, conv_mix_w, w_conv_out, w_gate_up, b_gate, gla_norm_g,
              w_gla_out, w_o, norm_ffn_g, w_ffn_up, ffn_conv_w, ffn_conv_b, w_ffn_down,
              final_norm_g):
    bp = x_prompt.shape[0]
    meta = jnp.broadcast_to(meta_tokens.astype(x_prompt.dtype)[None], (bp, N_META, D_MODEL))
    hp = jnp.concatenate([meta, x_prompt], axis=1)
    hs = x_sample
    chunk_p = math.gcd(hp.shape[1], GLA_CHUNK)
    chunk_s = math.gcd(hs.shape[1], GLA_CHUNK)
    conv_p, gla_p, ffn_p, conv_s, gla_s, ffn_s = [], [], [], [], [], []
    for l in range(DEPTH):
        pl = (norm_mix_g[l], w_in[l], conv_mix_w[l], w_conv_out[l], w_gate_up[l], b_gate[l],
              gla_norm_g[l], w_gla_out[l], w_o[l], norm_ffn_g[l], w_ffn_up[l], ffn_conv_w[l],
              ffn_conv_b[l], w_ffn_down[l])
        zc = jnp.zeros((bp, CONV_W - 1, D_CONV), hp.dtype)
        zs = jnp.zeros((bp, GLA_HEADS, GLA_DK, GLA_DV), jnp.float32)
        zf = jnp.zeros((bp, CONV_W - 1, D_FF), hp.dtype)
        hp, c1, s1, f1 = layer(hp, zc, zs, zf, chunk_p, *pl)
        hs, c2, s2, f2 = layer(hs, state_conv[l], state_gla[l], state_ffn_conv[l], chunk_s, *pl)
        conv_p.append(c1)
        gla_p.append(s1.astype(x_prompt.dtype))
        ffn_p.append(f1)
        conv_s.append(c2.astype(state_conv.dtype))
        gla_s.append(s2.astype(state_gla.dtype))
        ffn_s.append(f2.astype(state_ffn_conv.dtype))
    y_prompt = rmsnorm(hp[:, N_META:], final_norm_g)
    y_sample = rmsnorm(hs, final_norm_g)
    return (y_prompt, y_sample, jnp.stack(conv_p), jnp.stack(gla_p), jnp.stack(ffn_p),
            jnp.stack(conv_s), jnp.stack(gla_s), jnp.stack(ffn_s))
```

```python
import contextlib
import numpy as np
import concourse.bass as bass
import concourse.mybir as mybir
from concourse.bass_utils import run_bass_kernel_spmd

F32 = mybir.dt.float32
BF16 = mybir.dt.bfloat16
AF = mybir.ActivationFunctionType
ALU = mybir.AluOpType
AX = mybir.AxisListType

D = 2048
KC = 16
DFF = 5632
FC = 44
DCV = 1024
NPO = 1036
NS = 128
T = NPO + NS
TP = 1028
EPS = 1e-6
NT_OWN = [(0, 388), (388, 388), (776, 388)]
NT_PRE = [(0, 343), (343, 343), (686, 342)]
CH_OWN = [(i * 128, 128) for i in range(8)] + [(1024, 12)]
CH_PRE = [(i * 128, 128) for i in range(8)] + [(1024, 4)]
SAMPLE = (NPO, NS)
TILES_OWN = CH_OWN + [SAMPLE]
EXT = 2 + NPO + 160
NSLOT = 2
SLOT = 4096
NCOL = 34

O_CB, O_CC, O_CH, O_Q, O_K, O_V, O_G, O_ALR, O_GA, O_GB = 0, 1024, 2048, 3072, 4096, 5120, 7168, 9216, 9232, 11280

C_GMIX, C_GFFN, C_GGLA, C_CW, C_FW, C_FB = 0, 16, 32, 36, 60, 192
NCST = 236


NOSYNC_SAME = ("pe",)


class _Op:
    __slots__ = ("eng", "fn", "deps", "sig", "sigcnt", "lane", "lanecnt", "is_dma", "ph")

    def __init__(self, eng, fn, is_dma=False, lane=None):
        self.eng = eng
        self.fn = fn
        self.deps = []
        self.sig = False
        self.sigcnt = 0
        self.lane = lane
        self.lanecnt = 0
        self.is_dma = is_dma


class Prog:
    ENGS = ("pe", "act", "dve", "pool", "sp")
    COMPUTE = ("pe", "act", "dve", "pool")

    def __init__(self, nc):
        self.nc = nc
        self.ops = {e: [] for e in self.ENGS}
        self.last_w = {}
        self.readers = {}
        self.lane_cnt = {}
        self.lane_last = {}
        self.phase = "init"
        self.scopes = False

    def _track(self, o, reads, writes):
        o.ph = self.phase
        deps = {}

        def add(d):
            if d is not None and d is not o:
                deps[id(d)] = d
        for r in reads:
            add(self.last_w.get(r))
        for w in writes:
            add(self.last_w.get(w))
            for rd in self.readers.get(w, {}).values():
                add(rd)
        o.deps = list(deps.values())
        for r in reads:
            key = ("dma", id(o)) if o.is_dma else o.eng
            self.readers.setdefault(r, {})[key] = o
        for w in writes:
            self.last_w[w] = o
            self.readers[w] = {}

    def op(self, eng, fn, reads=(), writes=()):
        o = _Op(eng, fn)
        self._track(o, reads, writes)
        self.ops[eng].append(o)
        return o

    def dma(self, eng, fn, lane, reads=(), writes=()):
        o = _Op(eng, fn, is_dma=True, lane=lane)
        self.lane_cnt[lane] = self.lane_cnt.get(lane, 0) + 1
        o.lanecnt = self.lane_cnt[lane]
        self.lane_last[lane] = o
        self._track(o, reads, writes)
        self.ops[eng].append(o)
        return o

    def barrier(self):
        lasts = []
        for e in self.ENGS:
            for o in reversed(self.ops[e]):
                if not o.is_dma and o.fn is not None:
                    lasts.append(o)
                    break
        dmas = list(self.lane_last.values())
        for e in self.ENGS:
            o = _Op(e, None)
            o.ph = self.phase
            o.deps = lasts + dmas
            self.ops[e].append(o)
        self.last_w = {}
        self.readers = {}

    def emit(self):
        nc = self.nc
        for e in self.ENGS:
            for o in self.ops[e]:
                for d in o.deps:
                    if d.is_dma:
                        continue
                    if d.eng == o.eng and not o.is_dma and d.eng in NOSYNC_SAME:
                        continue
                    d.sig = True
        for e in self.ENGS:
            c = 0
            for o in self.ops[e]:
                if o.sig:
                    c += 1
                    o.sigcnt = c
        lanes = sorted(self.lane_cnt.keys(), key=str)
        with contextlib.ExitStack() as st:
            sem_e = {e: st.enter_context(nc.semaphore("s_" + e)) for e in self.COMPUTE}
            sem_l = {l: st.enter_context(nc.semaphore("l%d" % i)) for i, l in enumerate(lanes)}
            block = st.enter_context(nc.Block())
            handles = {"pe": block.tensor, "act": block.scalar, "dve": block.vector,
                       "pool": block.gpsimd, "sp": block.sync}

            def make(e):
                ops = self.ops[e]

                def body(eng):
                    known = {}
                    cur = [None, None]
                    for o in ops:
                        if self.scopes and o.ph != cur[0]:
                            if cur[0] is not None:
                                nc.leave_named_scope(cur[0], cur[1], False)
                            cur[0] = o.ph
                            cur[1] = nc.enter_named_scope(o.ph, False)[0]
                        need = {}
                        for d in o.deps:
                            if d.is_dma:
                                k, v = ("l", d.lane), 16 * d.lanecnt
                            else:
                                if d.eng == e and not o.is_dma and e in NOSYNC_SAME:
                                    continue
                                if not d.sig:
                                    continue
                                k, v = ("e", d.eng), d.sigcnt
                            if need.get(k, 0) < v:
                                need[k] = v
                        for k, v in need.items():
                            if known.get(k, 0) >= v:
                                continue
                            known[k] = v
                            eng.wait_ge(sem_l[k[1]] if k[0] == "l" else sem_e[k[1]], v)
                        if o.fn is None:
                            continue
                        ins = o.fn(eng)
                        if o.is_dma:
                            ins.then_inc(sem_l[o.lane], 16)
                        elif o.sig:
                            ins.then_inc(sem_e[e], 1)
                    if e == "sp":
                        for l in lanes:
                            eng.wait_ge(sem_l[l], 16 * self.lane_cnt[l])
                    if self.scopes and cur[0] is not None:
                        nc.leave_named_scope(cur[0], cur[1], False)
                return body

            for e in self.ENGS:
                handles[e](make(e))


def weight_plan():
    pl = []
    pl.append(("w_in", 0, 16, O_ALR, 16))
    for h in range(4):
        pl.append(("w_in", 0, 16, O_V + 512 * h, 256))
        pl.append(("w_in", 0, 16, O_V + 512 * h + 256, 256))
        pl.append(("w_in", 0, 16, O_K + 256 * h, 256))
    pl.append(("w_in", 0, 16, O_ALR, 16))
    for h in range(4):
        pl.append(("w_in", 0, 16, O_V + 512 * h, 256))
        pl.append(("w_in", 0, 16, O_V + 512 * h + 256, 256))
        pl.append(("w_in", 0, 16, O_K + 256 * h, 256))
        pl.append(("w_in", 0, 16, O_Q + 256 * h, 256))
        pl.append(("w_in", 0, 16, O_G + 512 * h, 256))
        pl.append(("w_in", 0, 16, O_G + 512 * h + 256, 256))
    for c2 in range(8):
        pl.append(("w_in", 0, 16, O_GB + 256 * c2, 256))
        pl.append(("w_gla_out", 0, 16, 256 * c2, 256))
    for c2 in range(4):
        pl.append(("w_in", 0, 16, O_CC + 256 * c2, 256))
        pl.append(("w_in", 0, 16, O_CH + 256 * c2, 256))
        pl.append(("w_in", 0, 16, O_CB + 256 * c2, 256))
    for c4 in range(4):
        pl.append(("w_in", 0, 16, O_GA + 512 * c4, 256))
        pl.append(("w_in", 0, 16, O_GA + 512 * c4 + 256, 256))
        pl.append(("w_conv_out", 0, 8, 512 * c4, 512))
    for c2 in range(22):
        pl.append(("w_ffn_up", 0, 16, 256 * c2, 256))
        pl.append(("w_ffn_up", 0, 16, DFF + 256 * c2, 256))
    for c in range(16):
        pl.append(("w_ffn_down", 0, 22, 128 * c, 128))
        pl.append(("w_ffn_down", 22, 22, 128 * c, 128))
    return pl


def plan_offsets(pl):
    offs, o = [], 0
    for (_, _, kcn, _, ncol) in pl:
        offs.append(o)
        o += 128 * kcn * ncol
    return offs, o


def build_nc(scopes=False):
    nc = bass.Bass("TRN2", target_bir_lowering=False)
    pl = weight_plan()
    offs, nw = plan_offsets(pl)

    def din(name, shape):
        return nc.dram_tensor(name, shape, F32, kind="ExternalInput").ap()

    def dout(name, shape):
        return nc.dram_tensor(name, shape, F32, kind="ExternalOutput").ap()

    xo = din("xo", [T, D])
    xp = din("xp", [TP, D])
    wflat = din("wflat", [nw])
    wo_r = din("wo_r", [4, 128, 16 * 512])
    wg_d = din("wg", [17, 1024])
    sgla = din("sgla", [16, 4, 2, 128, 512])
    sconv = din("sconv", [32, DCV])
    sffn = din("sffn", [32, DFF])
    cst_d = din("cst", [128, NCST])
    gfin_d = din("gfin", [128, D])
    ident_d = din("ident", [128, 128])
    maskp_d = din("maskp", [128, 128])
    masks_d = din("masks", [128, 128])
    rmo_d = din("rmo", [128, T])
    seqmb_d = din("seqmb", [128, 16 * 128])
    seqmt_d = din("seqmt", [128, 16])
    rmp_d = din("rmp", [128, TP])

    y = dout("y", [T, D])
    o_convp = dout("o_convp", [2, DCV])
    o_ffnp = dout("o_ffnp", [2, DFF])
    o_glap = dout("o_glap", [4, 2, 128, 512])
    o_convs = dout("o_convs", [32, DCV])
    o_ffns = dout("o_ffns", [32, DFF])
    o_glas = dout("o_glas", [16, 4, 2, 128, 512])
    h_scr = nc.dram_tensor("h_scr", [T, D], F32).ap()
    ff_scr = nc.dram_tensor("ff_scr", [T, D], F32).ap()

    P = Prog(nc)
    P.scopes = scopes
    es = contextlib.ExitStack()

    def sb(name, shape, dt=F32):
        return es.enter_context(nc.sbuf_tensor("sb_sb_" + name, shape, dt))

    def mm(out, lhsT, rhs, start, stop, r, w):
        P.op("pe", lambda e: e.matmul(out, lhsT=lhsT, rhs=rhs, start=start, stop=stop), reads=r, writes=w)

    def tr(out, in_, ident, r, w):
        P.op("pe", lambda e: e.transpose(out, in_, ident), reads=r, writes=w)

    def act(out, in_, func, r, w, bias=None, scale=None, accum=None):
        kw = {}
        if bias is not None:
            kw["bias"] = bias
        if scale is not None:
            kw["scale"] = scale
        if accum is not None:
            kw["accum_out"] = accum
        P.op("act", lambda e: e.activation(out=out, in_=in_, func=func, **kw), reads=r, writes=w)

    def tt(out, in0, in1, op, r, w, eng="dve"):
        P.op(eng, lambda e: e.tensor_tensor(out=out, in0=in0, in1=in1, op=op), reads=r, writes=w)

    def ts(out, in0, s1, op0, r, w, s2=None, op1=None, eng="dve"):
        if op1 is None:
            P.op(eng, lambda e: e.tensor_scalar(out=out, in0=in0, scalar1=s1, scalar2=None, op0=op0), reads=r, writes=w)
        else:
            P.op(eng, lambda e: e.tensor_scalar(out=out, in0=in0, scalar1=s1, scalar2=s2, op0=op0, op1=op1),
                 reads=r, writes=w)

    def stt(out, in0, scalar, in1, op0, op1, r, w):
        P.op("dve", lambda e: e.scalar_tensor_tensor(out=out, in0=in0, scalar=scalar, in1=in1, op0=op0, op1=op1),
             reads=r, writes=w)

    def cp(out, in_, r, w, eng="dve"):
        P.op(eng, lambda e: e.tensor_copy(out=out, in_=in_), reads=r, writes=w)

    def memset(ap, val, w, eng="pool"):
        P.op(eng, lambda e: e.memset(ap, val), writes=w)

    ulane = {"n": 0}

    def dma(out, in_, lane, r=(), w=(), eng="sp"):
        if lane is None:
            ulane["n"] += 1
            lane = ("u", ulane["n"])
        P.dma(eng, lambda e: e.dma_start(out=out, in_=in_), lane=lane, reads=r, writes=w)

    ps = es.enter_context(nc.psum_tensor("ps", [128, 8, 512], F32))
    psb = ps[:].bitcast(BF16)
    ring = sb("ring", [128, NSLOT, SLOT], BF16)
    cst = sb("cst", [128, NCST])
    identf = sb("identf", [128, 128])
    identb = sb("identb", [128, 128], BF16)
    maskp = sb("maskp", [128, 128])
    masks = sb("masks", [128, 128])
    ucol = sb("ucol", [128, 8, NCOL])
    sprev_u = sb("sprev_u", [128, 8, 32])

    seqmt = sb("seqmt", [128, 16])
    PSB = lambda b: ("ps", b)

    arena_start = (nc.sbuf_base + 63) // 64 * 64
    AR = nc.sbuf_top - arena_start - 128
    es.enter_context(nc.sbuf_tensor("sb_fence", [128, AR // 4], F32))
    cnt = {"n": 0}

    def AT(name, shape, dt, off):
        nbytes = int(np.prod(shape[1:])) * (4 if dt == F32 else 2)
        assert off % 32 == 0 and off + nbytes <= AR, (name, off, nbytes, AR)
        cnt["n"] += 1
        return nc.alloc_sbuf_tensor_at("a%d_%s" % (cnt["n"], name), shape, dt, offset=arena_start + off)

    class Bump:
        def __init__(self, off, limit=None):
            self.off = off
            self.limit = limit

        def __call__(self, name, shape, dt=F32):
            nbytes = int(np.prod(shape[1:])) * (4 if dt == F32 else 2)
            o = (self.off + 63) // 64 * 64
            self.off = o + nbytes
            if self.limit is not None:
                assert self.off <= self.limit, (name, self.off, self.limit)
            return AT(name, shape, dt, o)

    BIGB = 16 * T * 2
    R0, R1, R2 = 0, BIGB, 2 * BIGB
    O_SST = R2
    O_WG = O_SST + 16384
    O_P23 = O_WG + 4096
    Sst = AT("Sst", [128, 4, 2, 512], F32, O_SST)
    wg = AT("wg", [32, 1024], F32, O_WG)

    dma(cst[:], cst_d, None, w=["cst"])
    dma(identf[:], ident_d, None, w=["identf"])
    dma(maskp[:], maskp_d, None, w=["maskp"])
    dma(masks[:], masks_d, None, w=["masks"])
    dma(wg[0:17, :], wg_d, None, w=["wg"])
    cp(identb[:], identf[:], ["identf"], ["identb"])
    memset(Sst[:], 0.0, [("S", h_, d_) for h_ in range(4) for d_ in range(2)])
    dma(seqmt[:], seqmt_d, None, w=["seqmt"])

    wstate = {"next_dma": 0, "next_use": 0, "released": 0}

    def issue_block_dma():
        i = wstate["next_dma"]
        if i >= len(pl):
            return
        assert i - NSLOT < wstate["released"], "weight ring overrun"
        wstate["next_dma"] += 1
        (_, _, kcn, _, ncol) = pl[i]
        n = kcn * ncol
        s = i % NSLOT
        src = wflat[offs[i]:offs[i] + 128 * n].rearrange("(p n) -> p n", p=128)
        for c0 in range(0, n, 2048):
            c1 = min(n, c0 + 2048)
            dma(ring[:, s, c0:c1], src[:, c0:c1], ("w", s), w=[("ring", s)], eng="pool")

    def next_block():
        i = wstate["next_use"]
        wstate["next_use"] += 1
        while wstate["next_dma"] <= i:
            issue_block_dma()
        (_, _, kcn, _, ncol) = pl[i]
        s = i % NSLOT
        v = ring[:, s, 0:kcn * ncol].rearrange("p (k n) -> p k n", k=kcn)
        return v, ("ring", s)

    def prefetch():
        wstate["released"] = wstate["next_use"]
        while wstate["next_dma"] < min(len(pl), wstate["next_use"] + NSLOT):
            issue_block_dma()

    for _ in range(NSLOT):
        issue_block_dma()

    def mm_a(blk, rk, j, M, kcn, actT, act_key, ntiles, banks, first=True, last=True, kc_off=0):
        for kc in range(kcn):
            for i, (t0, tn) in enumerate(ntiles):
                mm(ps[0:M, banks[i], 0:tn], blk[:, kc, j * M:(j + 1) * M], actT[:, kc_off + kc, t0:t0 + tn],
                   start=(first and kc == 0), stop=(last and kc == kcn - 1),
                   r=[rk, act_key], w=[PSB(banks[i])])

    bank_set = {"i": 0}

    def next_banks():
        b = bank_set["i"]
        bank_set["i"] ^= 1
        return [3 * b, 3 * b + 1, 3 * b + 2]

    def norm_tiles(*a, **kw):
        for _ in norm_tiles_gen(*a, **kw):
            pass

    def norm_tiles_gen(src, tiles, dstT, dst_key, gcol0, tag, bump, add_fn=None, store_h=None):
        xt = [bump("xt", [128, D], F32) for i in range(2)]
        xn = [bump("xn", [128, D], BF16) for i in range(2)]
        sq = bump("sq", [128, D], BF16)
        st = bump("st", [128, 2, 4], F32)

        def load(ti):
            if ti < len(tiles):
                t0, n = tiles[ti]
                dma(xt[ti % 2][0:n, :], src[t0:t0 + n, :], ("x", ti % 2), w=[("xt", tag, ti % 2)])

        def stage1(ti):
            t0, n = tiles[ti]
            b = ti % 2
            kx, kn = ("xt", tag, b), ("xn", tag, b)
            if add_fn is not None:
                add_fn(ti, t0, n, xt[b], kx)
            if store_h is not None:
                dma(store_h[t0:t0 + n, :], xt[b][0:n, :], ("hs", b), r=[kx])
            act(sq[0:n, :], xt[b][0:n, :], AF.Square, [kx], ["sq" + tag, ("st", tag, b)], accum=st[0:n, b, 0:1])
            act(st[0:n, b, 1:2], st[0:n, b, 0:1], AF.Ln, [("st", tag, b)], [("st1", tag, b)], scale=1.0 / D, bias=EPS)
            act(st[0:n, b, 2:3], st[0:n, b, 1:2], AF.Exp, [("st1", tag, b)], [("st2", tag, b)], scale=-0.5)
            ts(xn[b][0:n, :], xt[b][0:n, :], st[0:n, b, 2:3], ALU.mult, [kx, ("st2", tag, b)], [kn])

        def stage2(ti):
            t0, n = tiles[ti]
            b = ti % 2
            kn = ("xn", tag, b)
            for half in range(2):
                bank = 6 + half
                for k8 in range(8):
                    kc = half * 8 + k8
                    tr(psb[:, bank, k8 * 128:k8 * 128 + n], xn[b][0:n, kc * 128:(kc + 1) * 128],
                       identb[0:n, 0:n], [kn, "identb"], [PSB(bank)])
                gview = cst[:, gcol0 + half * 8:gcol0 + half * 8 + 8].unsqueeze(2).to_broadcast([128, 8, n])
                pview = psb[:, bank, 0:1024].rearrange("p (k t) -> p k t", k=8)[:, :, 0:n]
                tt(dstT[:, half * 8:half * 8 + 8, t0:t0 + n], pview, gview, ALU.mult,
                   [PSB(bank), "cst"], [dst_key])

        load(0)
        load(1)
        for ti in range(len(tiles)):
            stage1(ti)
            if ti >= 1:
                stage2(ti - 1)
            load(ti + 2)
            yield
        stage2(len(tiles) - 1)
        yield

    P.phase = "P1a"
    if True:
        npT = AT("npT", [128, KC, TP], BF16, R1)
        norm_tiles(xp, CH_PRE, npT, "npT", C_GMIX, "p", Bump(O_P23))
        P.barrier()

        P.phase = "P2"
        if True:
            b2 = Bump(O_P23)
            alr = b2("alrp", [32, TP], F32)
            rm = b2("rmp", [128, TP], BF16)
            A = b2("Ap", [128, 2, TP], F32)
            B = b2("Bp", [128, 2, TP], F32)
            KD = b2("KDp", [128, 2, TP], BF16)
            vt = b2("vtp", [128, 9, 512], BF16)
            kdta = b2("kdtp", [128, 9, 256], BF16)
            vTs = b2("vTsp", [128, 4, TP], BF16)
            nT = AT("nT", [128, KC, T], BF16, R0)
            g1b = norm_tiles_gen(xo, TILES_OWN, nT, "nT", C_GMIX, "o", Bump(b2.off))
            dma(rm[:], rmp_d, None, w=["rm"], eng="pool")
            memset(alr[:], 1.0, ["alr"])
            blk, rk = next_block()
            mm_a(blk, rk, 0, 16, 16, npT, "npT", NT_PRE, [0, 1, 2])
            prefetch()
            for i, (t0, tn) in enumerate(NT_PRE):
                cp(alr[0:16, t0:t0 + tn], ps[0:16, i, 0:tn], [PSB(i)], ["alr"])
            for h in range(4):
                for dk in range(2):
                    banks = next_banks()
                    for i, (t0, tn) in enumerate(NT_PRE):
                        mm(ps[:, banks[i], 0:tn], wg[0:17, h * 256 + dk * 128:h * 256 + (dk + 1) * 128],
                           alr[0:17, t0:t0 + tn], True, True, ["wg", "alr"], [PSB(banks[i])])
                        act(A[:, dk, t0:t0 + tn], ps[:, banks[i], 0:tn], AF.Exp, [PSB(banks[i])], [("A", dk)], scale=-1.0)

                def batch(dk, A=A, B=B, rm=rm):
                    act(A[:, dk, :], A[:, dk, :], AF.Ln, [("A", dk)], [("A", dk)], bias=1.0)
                    P.op("dve", lambda e: e.tensor_tensor_scan(
                        out=B[:, dk, :], data0=rm[:], data1=A[:, dk, :], initial=0.0, op0=ALU.mult, op1=ALU.add),
                        reads=[("A", dk), "rm"], writes=[("B", dk)])
                    full = B[:, dk, 0:1024].rearrange("p (c t) -> p c t", c=8)
                    tt(A[:, dk, 0:1024].rearrange("p (c t) -> p c t", c=8), full,
                       full[:, :, 127:128].to_broadcast([128, 8, 128]), ALU.subtract, [("B", dk)], [("A", dk)])
                    tt(A[:, dk, 1024:TP], B[:, dk, 1024:TP], B[:, dk, TP - 1:TP].to_broadcast([128, 4]),
                       ALU.subtract, [("B", dk)], [("A", dk)])
                    act(A[:, dk, :], A[:, dk, :], AF.Exp, [("A", dk)], [("A", dk)], scale=1.0 / 16)
                    ends = full[:, :, 127:128]
                    act(ends, ends, AF.Exp, [("B", dk)], [("B", dk)], scale=-1.0 / 16)
                    act(B[:, dk, TP - 1:TP], B[:, dk, TP - 1:TP], AF.Exp, [("B", dk)], [("B", dk)], scale=-1.0 / 16)

                for g2 in range(2):
                    vblk, vk = next_block()
                    for j in range(2):
                        c = 2 * g2 + j
                        banks = next_banks()
                        mm_a(vblk, vk, j, 128, 16, npT, "npT", NT_PRE, banks)
                        for i, (t0, tn) in enumerate(NT_PRE):
                            act(vTs[:, c, t0:t0 + tn], ps[:, banks[i], 0:tn], AF.Copy, [PSB(banks[i])], ["vTs"])
                        if c == 0:
                            batch(0)
                        if c == 1:
                            batch(1)
                    prefetch()
                for ci, (t0, n) in enumerate(CH_PRE):
                    bank = 6 + (ci % 2)
                    for c in range(4):
                        tr(psb[0:n, bank, c * 128:(c + 1) * 128], vTs[:, c, t0:t0 + n], identb[:, :],
                           ["vTs", "identb"], [PSB(bank)])
                    act(vt[0:n, ci, :], psb[0:n, bank, 0:512], AF.Copy, [PSB(bank)], [("vt", ci)])
                for _ in range(3 if h < 3 else 2):
                    next(g1b, None)
                blk, rk = next_block()
                for dk in range(2):
                    banks = next_banks()
                    mm_a(blk, rk, dk, 128, 16, npT, "npT", NT_PRE, banks)
                    for i, (t0, tn) in enumerate(NT_PRE):
                        tt(KD[:, dk, t0:t0 + tn], ps[:, banks[i], 0:tn], A[:, dk, t0:t0 + tn], ALU.mult,
                           [PSB(banks[i]), ("A", dk)], [("KD", dk)])
                prefetch()
                for ci, (t0, n) in enumerate(CH_PRE):
                    bank = 6 + (ci // 4) % 2
                    q4 = ci % 4
                    for dk in range(2):
                        tr(psb[0:n, bank, q4 * 256 + dk * 128:q4 * 256 + (dk + 1) * 128], KD[:, dk, t0:t0 + n], identb[:, :],
                           [("KD", dk), "identb"], [PSB(bank)])
                    act(kdta[0:n, ci, :], psb[0:n, bank, q4 * 256:(q4 + 1) * 256], AF.Copy, [PSB(bank)], [("kdt", ci)])
                for ci, (t0, n) in enumerate(CH_PRE):
                    b = ci % 2
                    for dk in range(2):
                        pb = 2 * b + dk
                        mm(ps[:, pb, :], kdta[0:n, ci, dk * 128:(dk + 1) * 128], vt[0:n, ci, :], True, True,
                           [("kdt", ci), ("vt", ci)], [PSB(pb)])
                        stt(Sst[:, h, dk, :], Sst[:, h, dk, :], B[:, dk, t0 + n - 1:t0 + n], ps[:, pb, :],
                            ALU.mult, ALU.add, [("S", h, dk), ("B", dk), PSB(pb)], [("S", h, dk)])
            for _ in g1b:
                pass
            P.barrier()
    P.phase = "P1b"
    b1 = Bump(O_P23)
    if True:
        stl = b1("stl", [32, DCV], F32)
        dma(stl[:, :], sconv, None, w=["stl"])
        for c in range(8):
            tr(ps[:, 0, c * 32:(c + 1) * 32], stl[0:32, c * 128:(c + 1) * 128], identf[0:32, 0:32],
               ["stl", "identf"], [PSB(0)])
        cp(sprev_u[:], ps[:, 0, 0:256].rearrange("p (c s) -> p c s", c=8), [PSB(0)], ["sprev_u"])
        P.barrier()

    P.phase = "P3"
    ogT = AT("ogT", [128, KC, T], BF16, R1)
    if True:
        t3 = Bump(O_P23)
        alr = t3("alro", [32, T])
        rm = t3("rmo", [128, T], BF16)
        seqmb = t3("seqmb", [128, 16, 128], BF16)
        o_A = (t3.off + 63) // 64 * 64
        A = t3("Ao", [128, 2, T])
        sgh = AT("sgh", [128, 4, T], BF16, o_A)
        B = t3("Bo", [128, 2, T])
        QE = t3("QE", [128, 2, T], BF16)
        o_KE = (t3.off + 63) // 64 * 64
        KE = t3("KE", [128, 2, T], BF16)
        vTs = AT("vTs", [128, 4, T], BF16, o_KE)
        KD = t3("KD", [128, 2, T], BF16)
        vt = t3("vto", [128, 10, 512], BF16)
        kdta = t3("kdta", [128, 10, 256], BF16)
        attma = t3("attma", [128, 10, 128], BF16)
        Sbf = [t3("Sbf%d" % i, [128, 2, 512], BF16) for i in range(2)]
        og = [t3("og%d" % i, [128, 512], BF16) for i in range(2)]
        ost = t3("ost", [128, 2, 4])
        S0 = [t3("S0_%d" % i, [128, 2, 512]) for i in range(4)]
        S0b = [t3("S0b_%d" % i, [128, 2, 512], BF16) for i in range(4)]
        QXs = [t3("QXs%d" % i, [128, 2, 128], BF16) for i in range(2)]
        KXs = [t3("KXs%d" % i, [128, 256], BF16) for i in range(2)]
        dma(rm[:], rmo_d, None, w=["rm"], eng="pool")
        dma(seqmb[:].rearrange("p s t -> p (s t)"), seqmb_d, None, w=["seqmb"], eng="pool")
        memset(alr[:], 1.0, ["alr"])
        blk, rk = next_block()
        mm_a(blk, rk, 0, 16, 16, nT, "nT", NT_OWN, [0, 1, 2])
        prefetch()
        for i, (t0, tn) in enumerate(NT_OWN):
            cp(alr[0:16, t0:t0 + tn], ps[0:16, i, 0:tn], [PSB(i)], ["alr"])

        def chunk_views(buf, dk):
            return (buf[:, dk, 0:1024].rearrange("p (c t) -> p c t", c=8), buf[:, dk, 1024:NPO],
                    buf[:, dk, NPO:T].rearrange("p (s t) -> p s t", s=16))
        AK = [("A", 0), ("A", 1)]

        for h in range(4):
            for dk in range(2):
                banks = next_banks()
                for i, (t0, tn) in enumerate(NT_OWN):
                    mm(ps[:, banks[i], 0:tn], wg[0:17, h * 256 + dk * 128:h * 256 + (dk + 1) * 128],
                       alr[0:17, t0:t0 + tn], True, True, ["wg", "alr"], [PSB(banks[i])])
                    act(A[:, dk, t0:t0 + tn], ps[:, banks[i], 0:tn], AF.Exp, [PSB(banks[i])], [("A", dk)], scale=-1.0)

            def batch(dk, A=A, B=B, rm=rm):
                act(A[:, dk, :], A[:, dk, :], AF.Ln, [("A", dk)], [("A", dk)], bias=1.0)
                P.op("dve", lambda e: e.tensor_tensor_scan(
                    out=B[:, dk, :], data0=rm[:], data1=A[:, dk, :], initial=0.0, op0=ALU.mult, op1=ALU.add),
                    reads=[("A", dk), "rm"], writes=[("B", dk)])
                act(A[:, dk, :], B[:, dk, :], AF.Exp, [("B", dk)], [("A", dk)], scale=1.0 / 16)

            KK = [("KE", 0), ("KE", 1), ("KD", 0), ("KD", 1)]
            for g2 in range(2):
                vblk, vk = next_block()
                for j in range(2):
                    c = 2 * g2 + j
                    banks = next_banks()
                    mm_a(vblk, vk, j, 128, 16, nT, "nT", NT_OWN, banks)
                    for i, (t0, tn) in enumerate(NT_OWN):
                        act(vTs[:, c, t0:t0 + tn], ps[:, banks[i], 0:tn], AF.Copy, [PSB(banks[i])] + KK, KK)
                    if c == 0:
                        batch(0)
                    if c == 1:
                        batch(1)
                prefetch()
            for ci, (t0, n) in enumerate(TILES_OWN):
                bank = 6 + (ci % 2)
                for c in range(4):
                    tr(psb[0:n, bank, c * 128:(c + 1) * 128], vTs[:, c, t0:t0 + n], identb[:, :],
                       KK + ["identb"], [PSB(bank)])
                act(vt[0:n, ci, :], psb[0:n, bank, 0:512], AF.Copy, [PSB(bank)], [("vt", ci)])
            blk, rk = next_block()
            for dk in range(2):
                banks = next_banks()
                mm_a(blk, rk, dk, 128, 16, nT, "nT", NT_OWN, banks)
                for i, (t0, tn) in enumerate(NT_OWN):
                    tt(KE[:, dk, t0:t0 + tn], ps[:, banks[i], 0:tn], A[:, dk, t0:t0 + tn], ALU.mult,
                       [PSB(banks[i]), ("A", dk)] + KK, [("KE", dk)])
                fb, sb_, smb = chunk_views(B, dk)
                fa, sa, sma = chunk_views(A, dk)
                tt(fa, fb, fb[:, :, 127:128].to_broadcast([128, 8, 128]), ALU.subtract, [("B", dk)], [("A", dk)])
                tt(sa, sb_, B[:, dk, NPO - 1:NPO].to_broadcast([128, 12]), ALU.subtract, [("B", dk)], [("A", dk)])
                tt(sma, smb, smb[:, :, 7:8].to_broadcast([128, 16, 8]), ALU.subtract, [("B", dk)], [("A", dk)])
                act(A[:, dk, :], A[:, dk, :], AF.Exp, [("A", dk)], [("A", dk)], scale=1.0 / 16)
                for i, (t0, tn) in enumerate(NT_OWN):
                    tt(KD[:, dk, t0:t0 + tn], ps[:, banks[i], 0:tn], A[:, dk, t0:t0 + tn], ALU.mult,
                       [PSB(banks[i]), ("A", dk)] + KK, [("KD", dk)])
                act(A[:, dk, :], B[:, dk, :], AF.Exp, [("B", dk), ("KD", dk)], [("A", dk)], scale=-1.0 / 16)
            prefetch()
            for ci, (t0, n) in enumerate(TILES_OWN):
                bank = 6 + (ci // 4) % 2
                q4 = ci % 4
                for dk in range(2):
                    tr(psb[0:n, bank, q4 * 256 + dk * 128:q4 * 256 + (dk + 1) * 128], KD[:, dk, t0:t0 + n], identb[:, :],
                       [("KD", dk), "identb"], [PSB(bank)])
                act(kdta[0:n, ci, :], psb[0:n, bank, q4 * 256:(q4 + 1) * 256], AF.Copy, [PSB(bank)], [("kdt", ci)])
            blk, rk = next_block()
            for dk in range(2):
                banks = next_banks()
                mm_a(blk, rk, dk, 128, 16, nT, "nT", NT_OWN, banks)
                for i, (t0, tn) in enumerate(NT_OWN):
                    stt(QE[:, dk, t0:t0 + tn], ps[:, banks[i], 0:tn], 1.0 / 16, A[:, dk, t0:t0 + tn], ALU.mult, ALU.mult,
                        [PSB(banks[i]), ("A", dk)], [("QE", dk)])
                fb, sb_, smb = chunk_views(B, dk)
                act(fb[:, :, 127:128], fb[:, :, 127:128], AF.Exp, [("B", dk), ("A", dk)], [("B", dk)], scale=-1.0 / 16)
                act(B[:, dk, NPO - 1:NPO], B[:, dk, NPO - 1:NPO], AF.Exp, [("B", dk)], [("B", dk)], scale=-1.0 / 16)
                act(smb[:, :, 7:8], smb[:, :, 7:8], AF.Exp, [("B", dk)], [("B", dk)], scale=-1.0 / 16)
            prefetch()
            for ci, (t0, n) in enumerate(TILES_OWN):
                bank = 6 + (ci // 4) % 2
                q4 = ci % 4
                for dk in range(2):
                    mm(ps[0:n, bank, q4 * 128:q4 * 128 + n], KE[:, dk, t0:t0 + n], QE[:, dk, t0:t0 + n], dk == 0, dk == 1,
                       [("KE", dk), ("QE", dk)], [PSB(bank)])
                msk = masks if ci == 9 else maskp
                tt(attma[0:n, ci, 0:n], ps[0:n, bank, q4 * 128:q4 * 128 + n], msk[0:n, 0:n], ALU.mult,
                   [PSB(bank), "maskp", "masks"], [("attm", ci)])
            for dk in range(2):
                act(Sbf[0][:, dk, :], Sst[:, h, dk, :], AF.Copy, [("S", h, dk)], [("Sbf", 0, dk)])

            def ep_a(ci, n, obank):
                b = ci % 2
                act(og[b][0:n, :], ps[0:n, obank, :], AF.Square, [PSB(obank)], [("og", b), ("ost", b)], accum=ost[0:n, b, 0:1])
                act(ost[0:n, b, 1:2], ost[0:n, b, 0:1], AF.Ln, [("ost", b)], [("ost1", b)], scale=1.0 / 512, bias=EPS)
                act(ost[0:n, b, 2:3], ost[0:n, b, 1:2], AF.Exp, [("ost1", b)], [("ost2", b)], scale=-0.5)
                ts(og[b][0:n, :], ps[0:n, obank, :], ost[0:n, b, 2:3], ALU.mult, [PSB(obank), ("ost2", b)], [("og", b)])

            def ep_b(ci, t0, n, h=h):
                b = ci % 2
                tb = 6 + b
                for c in range(4):
                    tr(psb[:, tb, 512 + c * 128:512 + c * 128 + n], og[b][0:n, c * 128:(c + 1) * 128], identb[0:n, 0:n],
                       [("og", b), "identb"], [PSB(tb)])
                pview = psb[:, tb, 512:1024].rearrange("p (c t) -> p c t", c=4)[:, :, 0:n]
                gview = cst[:, C_GGLA:C_GGLA + 4].unsqueeze(2).to_broadcast([128, 4, n])
                tt(ogT[:, 4 * h:4 * h + 4, t0:t0 + n], pview, gview, ALU.mult, [PSB(tb), "cst"], [("ogT", h)])

            def P_mm(ci):
                t0, n = CH_OWN[ci]
                for dk in range(2):
                    pb = 2 + 2 * (ci % 2) + dk
                    mm(ps[:, pb, :], kdta[0:n, ci, dk * 128:(dk + 1) * 128], vt[0:n, ci, :], True, True,
                       [("kdt", ci), ("vt", ci)], [PSB(pb)])

            def load_state(s, h=h):
                sbi = s % 4
                for dk in range(2):
                    dma(S0[sbi][:, dk, :], sgla[s, h, dk], ("s0", sbi, dk), w=[("S0", sbi, dk)])
            for s_ in range(3):
                load_state(s_)

            P_mm(0)
            for ci, (t0, n) in enumerate(CH_OWN):
                b = ci % 2
                sb_cur, sb_nxt = ci % 2, (ci + 1) % 2
                if ci + 1 < len(CH_OWN):
                    P_mm(ci + 1)
                ob = b
                mm(ps[0:n, ob, :], attma[0:n, ci, 0:n], vt[0:n, ci, :], True, False, [("attm", ci), ("vt", ci)], [PSB(ob)])
                for dk in range(2):
                    mm(ps[0:n, ob, :], QE[:, dk, t0:t0 + n], Sbf[sb_cur][:, dk, :], False, dk == 1,
                       [("QE", dk), ("Sbf", sb_cur, dk)], [PSB(ob)])
                for dk in range(2):
                    pb = 2 + 2 * b + dk
                    dl = B[:, dk, t0 + n - 1:t0 + n]
                    stt(Sbf[sb_nxt][:, dk, :], Sst[:, h, dk, :], dl, ps[:, pb, :], ALU.mult, ALU.add,
                        [("S", h, dk), ("B", dk), PSB(pb)], [("Sbf", sb_nxt, dk)])
                    stt(Sst[:, h, dk, :], Sst[:, h, dk, :], dl, ps[:, pb, :], ALU.mult, ALU.add,
                        [("S", h, dk), ("B", dk), PSB(pb)], [("S", h, dk)])
                ep_a(ci, n, ob)
                if ci >= 1:
                    ep_b(ci - 1, CH_OWN[ci - 1][0], CH_OWN[ci - 1][1])
            ep_b(len(CH_OWN) - 1, CH_OWN[-1][0], CH_OWN[-1][1])
            for dk in range(2):
                dma(o_glap[h, dk], Sst[:, h, dk, :], ("gp", dk), r=[("S", h, dk)])

            def g_stream():
                gbanks = [5, 6, 7]
                for g2 in range(2):
                    blk, rk = next_block()
                    for j in range(2):
                        c = 2 * g2 + j
                        cnt_ = 0
                        for kc in range(KC):
                            for i, (t0_, tn) in enumerate(NT_OWN):
                                mm(ps[:, gbanks[i], 0:tn], blk[:, kc, j * 128:(j + 1) * 128], nT[:, kc, t0_:t0_ + tn],
                                   start=(kc == 0), stop=(kc == KC - 1), r=[rk, "nT"], w=[PSB(gbanks[i])])
                                cnt_ += 1
                                if cnt_ % 12 == 0 and cnt_ < 48:
                                    yield
                        for i, (t0_, tn) in enumerate(NT_OWN):
                            act(sgh[:, c, t0_:t0_ + tn], ps[:, gbanks[i], 0:tn], AF.Silu, [PSB(gbanks[i])] + AK, AK)
                        yield
                    prefetch()

            gs = g_stream()
            t0, n = SAMPLE
            ci = 9
            ob = 0
            mm(ps[:, ob, :], attma[:, ci, :], vt[:, ci, :], True, False, [("attm", ci), ("vt", ci)], [PSB(ob)])

            def prep(s):
                sbi, xb = s % 4, s % 2
                tt(QXs[xb][:, :, :], QE[:, :, t0:t0 + n], seqmb[:, s, :].unsqueeze(1).to_broadcast([128, 2, 128]),
                   ALU.mult, [("QE", 0), ("QE", 1), "seqmb"], [("QXs", xb)])
                ts(KXs[xb][:, :], kdta[:, ci, :], seqmt[:, s:s + 1], ALU.mult, [("kdt", ci), "seqmt"], [("KXs", xb)])
                act(S0b[sbi][:, :, :], S0[sbi][:, :, :], AF.Copy, [("S0", sbi, 0), ("S0", sbi, 1)], [("S0b", sbi)])
            prep(0)
            for s in range(16):
                sbi = s % 4
                xb = s % 2
                for dk in range(2):
                    mm(ps[:, ob, :], QXs[xb][:, dk, :], S0b[sbi][:, dk, :], False, (s == 15 and dk == 1),
                       [("QXs", xb), ("S0b", sbi)], [PSB(ob)])
                for dk in range(2):
                    pb = 1 + 2 * xb + dk
                    mm(ps[:, pb, :], KXs[xb][:, dk * 128:(dk + 1) * 128], vt[:, ci, :], True, True,
                       [("KXs", xb), ("vt", ci)], [PSB(pb)])
                if s + 1 < 16:
                    prep(s + 1)
                for dk in range(2):
                    pb = 1 + 2 * xb + dk
                    stt(S0[sbi][:, dk, :], S0[sbi][:, dk, :], B[:, dk, t0 + 8 * s + 7:t0 + 8 * s + 8], ps[:, pb, :],
                        ALU.mult, ALU.add, [("S0", sbi, dk), ("B", dk), PSB(pb)], [("S0", sbi, dk)])
                if s + 3 < 16:
                    load_state(s + 3)
                for dk in range(2):
                    dma(o_glas[s, h, dk], S0[sbi][:, dk, :], ("so", sbi, dk), r=[("S0", sbi, dk)])
                next(gs, None)
            for _ in gs:
                pass
            ep_a(ci, n, ob)
            ep_b(ci, t0, n)
            for c in range(4):
                tt(ogT[:, 4 * h + c, :], ogT[:, 4 * h + c, :], sgh[:, c, :], ALU.mult, [("ogT", h)] + AK, [("ogT", h)])
        P.barrier()

    P.phase = "P4"
    merged = AT("merged", [128, KC, T], BF16, R2)
    O_WOB = (AR - 32768) // 64 * 64
    woB = AT("woB", [128, 2, 16 * 512], BF16, O_WOB)
    t5 = Bump(3 * BIGB, O_WOB)
    sgt = [t5("sgt", [128, T], BF16) for i in range(4)]
    for c2 in range(8):
        gb, gbk = next_block()
        for j in range(2):
            banks = next_banks()
            mm_a(gb, gbk, j, 128, 16, nT, "nT", NT_OWN, banks)
            for i, (t0, tn) in enumerate(NT_OWN):
                act(sgt[j][:, t0:t0 + tn], ps[:, banks[i], 0:tn], AF.Sigmoid, [PSB(banks[i])], [("sgt", j)])
        prefetch()
        go, gok = next_block()
        for j in range(2):
            c = 2 * c2 + j
            banks = next_banks()
            mm_a(go, gok, j, 128, 16, ogT, "ogTall", NT_OWN, banks)
            for i, (t0, tn) in enumerate(NT_OWN):
                tt(merged[:, c, t0:t0 + tn], ps[:, banks[i], 0:tn], sgt[j][:, t0:t0 + tn], ALU.mult,
                   [PSB(banks[i]), ("sgt", j)], [("merged", c)])
        prefetch()
        g_, q_ = 2 + c2 // 4, c2 % 4
        dma(woB[:, g_ - 2, q_ * 2048:(q_ + 1) * 2048], wo_r[g_][:, q_ * 2048:(q_ + 1) * 2048], ("wo", g_),
            w=[("wo", g_)], eng="pool")
    P.barrier()

    P.phase = "P5"
    def conv3(dst, src, wcol, key_src, key_dst, tmp, key_tmp):
        regs = [(src[:, 0:EXT - 160], dst[:, 0:NPO], tmp[:, 0:NPO], NPO, None),
                (src[:, EXT - 160:EXT].rearrange("p (s t) -> p s t", s=16),
                 dst[:, NPO:T].rearrange("p (s t) -> p s t", s=16),
                 tmp[:, NPO:T].rearrange("p (s t) -> p s t", s=16), 8, 16)]
        for (sv, dv, tv, L, ns) in regs:
            def sl(k):
                return sv[:, k:k + L] if ns is None else sv[:, :, k:k + L]
            ts(tv, sl(0), cst[:, wcol:wcol + 1], ALU.mult, [key_src, "cst"], [key_tmp])
            stt(tv, sl(1), cst[:, wcol + 1:wcol + 2], tv, ALU.mult, ALU.add, [key_src, "cst", key_tmp], [key_tmp])
            stt(dv, sl(2), cst[:, wcol + 2:wcol + 3], tv, ALU.mult, ALU.add, [key_src, "cst", key_tmp], [key_dst])

    if True:
        cbuc = AT("cbuc", [128, 8, T], BF16, R1)
        ccs = [t5("ccs%d" % i, [128, T]) for i in range(2)]
        ue = [t5("ue%d" % i, [128, EXT]) for i in range(2)]
        uc = [t5("uc%d" % i, [128, T]) for i in range(2)]
        ctmp = t5("ctmp", [128, T])
        uo = t5("uo", [NCOL, DCV])
        for b in range(2):
            memset(ue[b][:, 0:2], 0.0, [("ue", b)])
        for c2 in range(4):
            blk, rk = next_block()
            for j in range(2):
                banks = next_banks()
                mm_a(blk, rk, j, 128, 16, nT, "nT", NT_OWN, banks)
                for i, (t0, tn) in enumerate(NT_OWN):
                    act(ccs[j][:, t0:t0 + tn], ps[:, banks[i], 0:tn], AF.Copy, [PSB(banks[i])], [("ccs", j)])
            prefetch()
            blk, rk = next_block()
            for j in range(2):
                c = 2 * c2 + j
                banks = next_banks()
                mm_a(blk, rk, j, 128, 16, nT, "nT", NT_OWN, banks)
                ues = ue[j][:, EXT - 160:EXT].rearrange("p (s t) -> p s t", s=16)
                cp(ues[:, :, 0:2], sprev_u[:, c, :].rearrange("p (s r) -> p s r", s=16), ["sprev_u"], [("ue", j)])
                for i, (t0, tn) in enumerate(NT_OWN):
                    pn = min(tn, NPO - t0)
                    tt(ue[j][:, 2 + t0:2 + t0 + pn], ps[:, banks[i], 0:pn], ccs[j][:, t0:t0 + pn], ALU.mult,
                       [PSB(banks[i]), ("ccs", j)], [("ue", j)])
                    if pn < tn:
                        tt(ues[:, :, 2:10], ps[:, banks[i], pn:tn].rearrange("p (s t) -> p s t", s=16),
                           ccs[j][:, NPO:T].rearrange("p (s t) -> p s t", s=16), ALU.mult,
                           [PSB(banks[i]), ("ccs", j)], [("ue", j)])
                conv3(uc[j], ue[j], C_CW + 3 * c, ("ue", j), ("uc", j), ctmp, "ctmp")
                cp(ucol[:, c, 0:2], ue[j][:, NPO:NPO + 2], [("ue", j)], ["ucol"])
                cp(ucol[:, c, 2:NCOL].rearrange("p (s r) -> p s r", s=16), ues[:, :, 8:10], [("ue", j)], ["ucol"])
            prefetch()
            blk, rk = next_block()
            for j in range(2):
                c = 2 * c2 + j
                banks = next_banks()
                mm_a(blk, rk, j, 128, 16, nT, "nT", NT_OWN, banks)
                for i, (t0, tn) in enumerate(NT_OWN):
                    tt(cbuc[:, c, t0:t0 + tn], ps[:, banks[i], 0:tn], uc[j][:, t0:t0 + tn], ALU.mult,
                       [PSB(banks[i]), ("uc", j)], ["cbuc"])
            prefetch()
        for c in range(8):
            tr(ps[0:NCOL, 6 + c // 4, (c % 4) * 128:(c % 4 + 1) * 128], ucol[:, c, :], identf[:, :],
               ["ucol", "identf"], [PSB(6 + c // 4)])
        for hb in range(2):
            cp(uo[:, hb * 512:(hb + 1) * 512], ps[0:NCOL, 6 + hb, :], [PSB(6 + hb)], ["uo"])
        dma(o_convp, uo[0:2, :], "oc", r=["uo"])
        dma(o_convs, uo[2:NCOL, :], "oc", r=["uo"])
        for c4 in range(4):
            for gh in range(2):
                gab, gak = next_block()
                for jj in range(2):
                    j4 = 2 * gh + jj
                    banks = next_banks()
                    mm_a(gab, gak, jj, 128, 16, nT, "nT", NT_OWN, banks)
                    for i, (t0, tn) in enumerate(NT_OWN):
                        act(sgt[j4][:, t0:t0 + tn], ps[:, banks[i], 0:tn], AF.Sigmoid, [PSB(banks[i])], [("sgt", j4)])
                prefetch()
            co, cok = next_block()
            for j4 in range(4):
                c = 4 * c4 + j4
                banks = next_banks()
                mm_a(co, cok, j4, 128, 8, cbuc, "cbuc", NT_OWN, banks)
                for i, (t0, tn) in enumerate(NT_OWN):
                    tt(ctmp[:, t0:t0 + tn], ps[:, banks[i], 0:tn], sgt[j4][:, t0:t0 + tn], ALU.mult,
                       [PSB(banks[i]), ("sgt", j4)], ["ctmp"])
                    tt(merged[:, c, t0:t0 + tn], merged[:, c, t0:t0 + tn], ctmp[:, t0:t0 + tn], ALU.add,
                       [("merged", c), "ctmp"], [("merged", c)])
            prefetch()
        P.barrier()

    P.phase = "P6"
    n2T = AT("n2T", [128, KC, T], BF16, R0)
    if True:
        woA = AT("woA", [128, 2, 16 * 512], BF16, R1)

        def wo_g(g):
            return woA[:, g, :] if g < 2 else woB[:, g - 2, :]
        for g in range(2):
            for q in range(4):
                dma(wo_g(g)[:, q * 2048:(q + 1) * 2048], wo_r[g][:, q * 2048:(q + 1) * 2048], ("wo", g),
                    w=[("wo", g)], eng="pool")

        def add_mo(ti, t0, n, xt_t, kx):
            for g in range(4):
                bank = (4 * ti + g) % 6
                for kc in range(KC):
                    mm(ps[0:n, bank, :], merged[:, kc, t0:t0 + n], wo_g(g)[:, kc * 512:(kc + 1) * 512],
                       kc == 0, kc == KC - 1, ["mergedall", ("wo", g)], [PSB(bank)])
                tt(xt_t[0:n, g * 512:(g + 1) * 512], xt_t[0:n, g * 512:(g + 1) * 512], ps[0:n, bank, :], ALU.add,
                   [kx, PSB(bank)], [kx])
        norm_tiles(xo, TILES_OWN, n2T, "n2T", C_GFFN, "h", Bump(3 * BIGB, O_WOB), add_fn=add_mo, store_h=h_scr)
        P.barrier()

    P.phase = "P7"
    O_ACT = BIGB
    O_GCOL = O_ACT + FC * T * 2
    O_P7 = (O_GCOL + FC * NCOL * 4 + 63) // 64 * 64
    actb = AT("actb", [128, FC, T], BF16, O_ACT)
    gcol = AT("gcol", [128, FC, NCOL], F32, O_GCOL)
    if True:
        t7 = Bump(O_P7)
        sprev_g = t7("sprev_g", [128, FC, 32])
        a_sb = [t7("a_sb%d" % i, [128, T], BF16) for i in range(2)]
        gte = [t7("gte%d" % i, [128, EXT]) for i in range(2)]
        o_shared = t7.off
        stl = t7("stl2", [32, 1408])
        for pc in range(4):
            dma(stl[:, :], sffn[:, pc * 1408:(pc + 1) * 1408], None, w=["stl2"])
            for cc in range(11):
                c = pc * 11 + cc
                bnk, off = c // 16, (c % 16) * 32
                tr(ps[:, bnk, off:off + 32], stl[0:32, cc * 128:(cc + 1) * 128], identf[0:32, 0:32],
                   ["stl2", "identf"], [PSB(bnk)])
        for bnk in range(3):
            c0 = bnk * 16
            cn = min(16, FC - c0)
            cp(sprev_g[:, c0:c0 + cn, :], ps[:, bnk, 0:cn * 32].rearrange("p (c s) -> p c s", c=cn),
               [PSB(bnk)], ["sprev_g"])
        P.barrier()
        t7 = Bump(o_shared)
        gc1 = t7("gc", [128, T])
        gc = [gc1, gc1]
        ctmp = t7("ctmp7", [128, T])
        for b in range(2):
            memset(gte[b][:, 0:2], 0.0, [("gte", b)])
        for c2 in range(22):
            ba, bak = next_block()
            for j in range(2):
                banks = next_banks()
                mm_a(ba, bak, j, 128, 16, n2T, "n2T", NT_OWN, banks)
                for i, (t0, tn) in enumerate(NT_OWN):
                    act(a_sb[j][:, t0:t0 + tn], ps[:, banks[i], 0:tn], AF.Copy, [PSB(banks[i])], [("a_sb", j)])
            prefetch()
            bg, bgk = next_block()
            for j in range(2):
                c = 2 * c2 + j
                banks = next_banks()
                mm_a(bg, bgk, j, 128, 16, n2T, "n2T", NT_OWN, banks)
                gts = gte[j][:, EXT - 160:EXT].rearrange("p (s t) -> p s t", s=16)
                cp(gts[:, :, 0:2], sprev_g[:, c, :].rearrange("p (s r) -> p s r", s=16), ["sprev_g"], [("gte", j)])
                for i, (t0, tn) in enumerate(NT_OWN):
                    pn = min(tn, NPO - t0)
                    act(gte[j][:, 2 + t0:2 + t0 + pn], ps[:, banks[i], 0:pn], AF.Copy, [PSB(banks[i])], [("gte", j)])
                    if pn < tn:
                        act(gts[:, :, 2:10], ps[:, banks[i], pn:tn].rearrange("p (s t) -> p s t", s=16), AF.Copy,
                            [PSB(banks[i])], [("gte", j)])
                conv3(gc[j], gte[j], C_FW + 3 * c, ("gte", j), "gc", ctmp, "ctmp7")
                cp(gcol[:, c, 0:2], gte[j][:, NPO:NPO + 2], [("gte", j)], ["gcol"])
                cp(gcol[:, c, 2:NCOL].rearrange("p (s r) -> p s r", s=16), gts[:, :, 8:10], [("gte", j)], ["gcol"])
                act(gc[j][:, :], gc[j][:, :], AF.Silu, ["gc", "cst"], ["gc"], bias=cst[:, C_FB + c:C_FB + c + 1])
                tt(actb[:, c, :], gc[j][:, :], a_sb[j][:, :], ALU.mult, ["gc", ("a_sb", j)], ["actb"])
            prefetch()
        P.barrier()

    P.phase = "P8"
    if True:
        t8 = Bump(0, BIGB)
        go = t8("go", [NCOL, DFF])
        fft = [t8("fft%d" % i, [128, T]) for i in range(2)]
        t8b = Bump(O_P7)
        ftm = [t8b("ftm%d" % i, [128, 10, 128]) for i in range(2)]
        hpc = [t8b("hpc%d" % i, [128, 10, 128]) for i in range(2)]

        def p8_hload(c):
            b = c % 2
            kh = ("hpc", b)
            dma(hpc[b][:, 0:8, :], h_scr[0:1024, c * 128:(c + 1) * 128].rearrange("(i p) m -> p i m", p=128),
                ("hp", b), w=[kh])
            dma(hpc[b][0:12, 8, :], h_scr[1024:NPO, c * 128:(c + 1) * 128], ("hp", b), w=[kh])
            dma(hpc[b][:, 9, :], h_scr[NPO:T, c * 128:(c + 1) * 128], ("hp", b), w=[kh])
        for c in range(FC):
            bnk = 6 + (c // 4) % 2
            tr(ps[0:NCOL, bnk, (c % 4) * 128:(c % 4 + 1) * 128], gcol[:, c, :], identf[:, :], ["gcol", "identf"], [PSB(bnk)])
            if c % 4 == 3:
                cp(go[:, (c - 3) * 128:(c + 1) * 128], ps[0:NCOL, bnk, :], [PSB(bnk)], ["go"])
        dma(o_ffnp, go[0:2, :], "oc", r=["go"])
        dma(o_ffns, go[2:NCOL, :], "oc", r=["go"])
        def p8_out(c):
            b = c % 2
            for ti, (t0, n) in enumerate(TILES_OWN):
                bnk = 6 + (ti // 4) % 2
                tr(ps[0:n, bnk, (ti % 4) * 128:(ti % 4 + 1) * 128], fft[b][:, t0:t0 + n], identf[:, :],
                   [("fft", b), "identf"], [PSB(bnk)])
                if ti % 4 == 3 or ti == 9:
                    t_lo = ti - (ti % 4)
                    for tj in range(t_lo, ti + 1):
                        nn = TILES_OWN[tj][1]
                        tt(ftm[b][0:nn, tj, :], ps[0:nn, bnk, (tj % 4) * 128:(tj % 4 + 1) * 128], hpc[b][0:nn, tj, :], ALU.add,
                           [PSB(bnk), ("hpc", b)], [("ftm", b)])
            dma(ff_scr[0:1024, c * 128:(c + 1) * 128].rearrange("(i p) m -> p i m", p=128), ftm[b][:, 0:8, :],
                ("fo", b), r=[("ftm", b)])
            dma(ff_scr[1024:NPO, c * 128:(c + 1) * 128], ftm[b][0:12, 8, :], ("fo", b), r=[("ftm", b)])
            dma(ff_scr[NPO:T, c * 128:(c + 1) * 128], ftm[b][:, 9, :], ("fo", b), r=[("ftm", b)])

        for c in range(16):
            b = c % 2
            p8_hload(c)
            bA, bAk = next_block()
            banks = next_banks()
            mm_a(bA, bAk, 0, 128, 22, actb, "actb", NT_OWN, banks, first=True, last=False, kc_off=0)
            prefetch()
            bB, bBk = next_block()
            mm_a(bB, bBk, 0, 128, 22, actb, "actb", NT_OWN, banks, first=False, last=True, kc_off=22)
            prefetch()
            for i, (t0, tn) in enumerate(NT_OWN):
                act(fft[b][:, t0:t0 + tn], ps[:, banks[i], 0:tn], AF.Copy, [PSB(banks[i])], [("fft", b)])
            if c >= 1:
                p8_out(c - 1)
        p8_out(15)
        P.barrier()

    P.phase = "P9"
    if True:
        t9 = Bump(0)
        gfin = t9("gfin", [128, D])
        fa = [t9("fa%d" % i, [128, D]) for i in range(2)]
        ha = [t9("ha%d" % i, [128, D]) for i in range(2)]
        ya = [t9("ya%d" % i, [128, D]) for i in range(2)]
        sq9 = t9("sq9", [128, D], BF16)
        st9 = t9("st9", [128, 2, 4])
        dma(gfin[:], gfin_d, None, w=["gfin"])

        def load9(ti):
            if ti < len(TILES_OWN):
                t0, n = TILES_OWN[ti]
                b = ti % 2
                dma(fa[b][0:n, :], ff_scr[t0:t0 + n, :], ("f9", b), w=[("fa", b)])
        load9(0)
        load9(1)
        for ti, (t0, n) in enumerate(TILES_OWN):
            b = ti % 2
            act(sq9[0:n, :], fa[b][0:n, :], AF.Square, [("fa", b)], ["sq9", ("st9", b)], accum=st9[0:n, b, 0:1])
            act(st9[0:n, b, 1:2], st9[0:n, b, 0:1], AF.Ln, [("st9", b)], [("st91", b)], scale=1.0 / D, bias=EPS)
            act(st9[0:n, b, 2:3], st9[0:n, b, 1:2], AF.Exp, [("st91", b)], [("st92", b)], scale=-0.5)
            act(ya[b][0:n, :], fa[b][0:n, :], AF.Copy, [("fa", b), ("st92", b)], [("ya", b)], scale=st9[0:n, b, 2:3])
            tt(ya[b][0:n, :], ya[b][0:n, :], gfin[0:n, :], ALU.mult, [("ya", b), "gfin"], [("ya", b)])
            dma(y[t0:t0 + n, :], ya[b][0:n, :], ("y9", b), r=[("ya", b)], eng="act")
            load9(ti + 2)
    P.emit()
    es.close()
    return nc


_NC_CACHE = {}


def _layout_block(Wm, kc0, kcn, col0, ncol):
    sub = Wm[kc0 * 128:(kc0 + kcn) * 128, col0:col0 + ncol]
    return np.ascontiguousarray(sub.reshape(kcn, 128, ncol).transpose(1, 0, 2)).reshape(128, kcn * ncol)


def kernel(x_prompt, x_sample, state_conv, state_gla, state_ffn_conv, meta_tokens,
           norm_mix_g, w_in, conv_mix_w, w_conv_out, w_gate_up, b_gate, gla_norm_g,
           w_gla_out, w_o, norm_ffn_g, w_ffn_up, ffn_conv_w, ffn_conv_b, w_ffn_down,
           final_norm_g):
    f32 = np.float32
    A_ = lambda a: np.asarray(a, dtype=f32)
    x_prompt, x_sample = A_(x_prompt), A_(x_sample)
    mats = {"w_in": A_(w_in)[0], "w_conv_out": A_(w_conv_out)[0], "w_gla_out": A_(w_gla_out)[0],
            "w_ffn_up": A_(w_ffn_up)[0], "w_ffn_down": A_(w_ffn_down)[0]}
    pl = weight_plan()
    offs, nw = plan_offsets(pl)
    wflat = np.empty(nw, dtype=f32)
    for (name, kc0, kcn, col0, ncol), o in zip(pl, offs):
        wflat[o:o + 128 * kcn * ncol] = _layout_block(mats[name], kc0, kcn, col0, ncol).reshape(-1)
    wo = A_(w_o)[0]
    wo_r = np.stack([_layout_block(wo, 0, 16, 512 * g, 512) for g in range(4)])
    wg = np.concatenate([A_(w_gate_up)[0], A_(b_gate)[0][None, :]], axis=0)

    def pp(v, nch):
        return np.ascontiguousarray(A_(v).reshape(nch, 128).T)
    cst = np.zeros((128, NCST), dtype=f32)
    cst[:, C_GMIX:C_GMIX + 16] = pp(norm_mix_g[0], 16)
    cst[:, C_GFFN:C_GFFN + 16] = pp(norm_ffn_g[0], 16)
    cst[:, C_GGLA:C_GGLA + 4] = pp(gla_norm_g[0], 4)
    cw = A_(conv_mix_w)[0]
    fw = A_(ffn_conv_w)[0]
    for tap in range(3):
        cst[:, C_CW + tap:C_CW + 24:3] = pp(cw[tap], 8)
        cst[:, C_FW + tap:C_FW + 132:3] = pp(fw[tap], 44)
    cst[:, C_FB:C_FB + 44] = pp(ffn_conv_b[0], 44)
    gfin = np.ascontiguousarray(np.broadcast_to(A_(final_norm_g)[None, :], (128, D)))
    ident = np.eye(128, dtype=f32)
    maskp = np.triu(np.ones((128, 128), dtype=f32))
    seq = np.arange(128) // 8
    same = (seq[:, None] == seq[None, :]).astype(f32)
    masks = maskp * same
    seqmt = (seq[:, None] == np.arange(16)[None, :]).astype(f32)
    seqmb = np.ascontiguousarray(np.broadcast_to(seqmt.T.reshape(1, 16 * 128), (128, 16 * 128)))
    rmo = np.ones((128, T), dtype=f32)
    for (t0, n) in CH_OWN:
        rmo[:, t0] = 0.0
    rmo[:, NPO:T:8] = 0.0
    rmp = np.ones((128, TP), dtype=f32)
    for (t0, n) in CH_PRE:
        rmp[:, t0] = 0.0

    meta = A_(meta_tokens)
    in_maps = []
    for c in range(8):
        b, half = c // 2, c % 2
        hp_pad = np.concatenate([np.zeros((1032, D), f32), meta, x_prompt[b]], axis=0)
        base = half * 1032
        xp_c = hp_pad[base:base + TP]
        own = hp_pad[base + TP:base + TP + NPO]
        xo_c = np.concatenate([own, x_sample[16 * c:16 * c + 16].reshape(NS, D)], axis=0)
        in_maps.append({
            "xo": np.ascontiguousarray(xo_c), "xp": np.ascontiguousarray(xp_c), "wflat": wflat, "wo_r": wo_r,
            "wg": wg, "sgla": np.ascontiguousarray(A_(state_gla)[0, 16 * c:16 * c + 16].reshape(16, 4, 2, 128, 512)),
            "sconv": np.ascontiguousarray(A_(state_conv)[0, 16 * c:16 * c + 16].reshape(32, DCV)),
            "sffn": np.ascontiguousarray(A_(state_ffn_conv)[0, 16 * c:16 * c + 16].reshape(32, DFF)),
            "cst": cst, "gfin": gfin, "ident": ident, "maskp": maskp, "masks": masks,
            "rmo": rmo, "rmp": rmp, "seqmb": seqmb, "seqmt": seqmt,
        })
    if _NC_CACHE.get("prep_only"):
        return in_maps
    if "nc" not in _NC_CACHE:
        _NC_CACHE["nc"] = build_nc()
    res = run_bass_kernel_spmd(_NC_CACHE["nc"], in_maps, core_ids=list(range(8))).results

    y_prompt = np.empty((4, 2048, D), f32)
    y_sample = np.empty((128, 8, D), f32)
    conv_p = np.empty((1, 4, 2, DCV), f32)
    gla_p = np.empty((1, 4, 4, 256, 512), f32)
    ffn_p = np.empty((1, 4, 2, DFF), f32)
    conv_s = np.empty((1, 128, 2, DCV), f32)
    gla_s = np.empty((1, 128, 4, 256, 512), f32)
    ffn_s = np.empty((1, 128, 2, DFF), f32)
    for c in range(8):
        r = res[c]
        b, half = c // 2, c % 2
        if half == 0:
            y_prompt[b, 0:1016] = r["y"][20:NPO]
        else:
            y_prompt[b, 1016:2048] = r["y"][4:NPO]
            conv_p[0, b] = r["o_convp"]
            ffn_p[0, b] = r["o_ffnp"]
            gla_p[0, b] = r["o_glap"].reshape(4, 256, 512)
        y_sample[16 * c:16 * c + 16] = r["y"][NPO:T].reshape(16, 8, D)
        conv_s[0, 16 * c:16 * c + 16] = r["o_convs"].reshape(16, 2, DCV)
        ffn_s[0, 16 * c:16 * c + 16] = r["o_ffns"].reshape(16, 2, DFF)
        gla_s[0, 16 * c:16 * c + 16] = r["o_glas"].reshape(16, 4, 256, 512)
    return (y_prompt, y_sample, conv_p, gla_p, ffn_p, conv_s, gla_s, ffn_s)
```

```python
import contextlib
import numpy as np
import concourse.bass as bass
import concourse.mybir as mybir
from concourse.bass_utils import run_bass_kernel_spmd

F32 = mybir.dt.float32
BF16 = mybir.dt.bfloat16
AF = mybir.ActivationFunctionType
ALU = mybir.AluOpType
AX = mybir.AxisListType

D = 2048
KC = 16
DFF = 5632
FC = 44
DCV = 1024
NPO = 1036
NS = 128
T = NPO + NS
TP = 1028
EPS = 1e-6
NT_OWN = [(0, 388), (388, 388), (776, 388)]
NT_PRE = [(0, 343), (343, 343), (686, 342)]
CH_OWN = [(i * 128, 128) for i in range(8)] + [(1024, 12)]
CH_PRE = [(i * 128, 128) for i in range(8)] + [(1024, 4)]
SAMPLE = (NPO, NS)
TILES_OWN = CH_OWN + [SAMPLE]
EXT = 2 + NPO + 160
NSLOT = 2
SLOT = 4096
NCOL = 34

O_CB, O_CC, O_CH, O_Q, O_K, O_V, O_G, O_ALR, O_GA, O_GB = 0, 1024, 2048, 3072, 4096, 5120, 7168, 9216, 9232, 11280

C_GMIX, C_GFFN, C_GGLA, C_CW, C_FW, C_FB = 0, 16, 32, 36, 60, 192
NCST = 236


NOSYNC_SAME = ("pe",)


class _Op:
    __slots__ = ("eng", "fn", "deps", "sig", "sigcnt", "lane", "lanecnt", "is_dma", "ph")

    def __init__(self, eng, fn, is_dma=False, lane=None):
        self.eng = eng
        self.fn = fn
        self.deps = []
        self.sig = False
        self.sigcnt = 0
        self.lane = lane
        self.lanecnt = 0
        self.is_dma = is_dma


class Prog:
    ENGS = ("pe", "act", "dve", "pool", "sp")
    COMPUTE = ("pe", "act", "dve", "pool")

    def __init__(self, nc):
        self.nc = nc
        self.ops = {e: [] for e in self.ENGS}
        self.last_w = {}
        self.readers = {}
        self.lane_cnt = {}
        self.lane_last = {}
        self.phase = "init"
        self.scopes = False

    def _track(self, o, reads, writes):
        o.ph = self.phase
        deps = {}

        def add(d):
            if d is not None and d is not o:
                deps[id(d)] = d
        for r in reads:
            add(self.last_w.get(r))
        for w in writes:
            add(self.last_w.get(w))
            for rd in self.readers.get(w, {}).values():
                add(rd)
        o.deps = list(deps.values())
        for r in reads:
            key = ("dma", id(o)) if o.is_dma else o.eng
            self.readers.setdefault(r, {})[key] = o
        for w in writes:
            self.last_w[w] = o
            self.readers[w] = {}

    def op(self, eng, fn, reads=(), writes=()):
        o = _Op(eng, fn)
        self._track(o, reads, writes)
        self.ops[eng].append(o)
        return o

    def dma(self, eng, fn, lane, reads=(), writes=()):
        o = _Op(eng, fn, is_dma=True, lane=lane)
        self.lane_cnt[lane] = self.lane_cnt.get(lane, 0) + 1
        o.lanecnt = self.lane_cnt[lane]
        self.lane_last[lane] = o
        self._track(o, reads, writes)
        self.ops[eng].append(o)
        return o

    def barrier(self):
        lasts = []
        for e in self.ENGS:
            for o in reversed(self.ops[e]):
                if not o.is_dma and o.fn is not None:
                    lasts.append(o)
                    break
        dmas = list(self.lane_last.values())
        for e in self.ENGS:
            o = _Op(e, None)
            o.ph = self.phase
            o.deps = lasts + dmas
            self.ops[e].append(o)
        self.last_w = {}
        self.readers = {}

    def emit(self):
        nc = self.nc
        for e in self.ENGS:
            for o in self.ops[e]:
                for d in o.deps:
                    if d.is_dma:
                        continue
                    if d.eng == o.eng and not o.is_dma and d.eng in NOSYNC_SAME:
                        continue
                    d.sig = True
        for e in self.ENGS:
            c = 0
            for o in self.ops[e]:
                if o.sig:
                    c += 1
                    o.sigcnt = c
        lanes = sorted(self.lane_cnt.keys(), key=str)
        with contextlib.ExitStack() as st:
            sem_e = {e: st.enter_context(nc.semaphore("s_" + e)) for e in self.COMPUTE}
            sem_l = {l: st.enter_context(nc.semaphore("l%d" % i)) for i, l in enumerate(lanes)}
            block = st.enter_context(nc.Block())
            handles = {"pe": block.tensor, "act": block.scalar, "dve": block.vector,
                       "pool": block.gpsimd, "sp": block.sync}

            def make(e):
                ops = self.ops[e]

                def body(eng):
                    known = {}
                    cur = [None, None]
                    for o in ops:
                        if self.scopes and o.ph != cur[0]:
                            if cur[0] is not None:
                                nc.leave_named_scope(cur[0], cur[1], False)
                            cur[0] = o.ph
                            cur[1] = nc.enter_named_scope(o.ph, False)[0]
                        need = {}
                        for d in o.deps:
                            if d.is_dma:
                                k, v = ("l", d.lane), 16 * d.lanecnt
                            else:
                                if d.eng == e and not o.is_dma and e in NOSYNC_SAME:
                                    continue
                                if not d.sig:
                                    continue
                                k, v = ("e", d.eng), d.sigcnt
                            if need.get(k, 0) < v:
                                need[k] = v
                        for k, v in need.items():
                            if known.get(k, 0) >= v:
                                continue
                            known[k] = v
                            eng.wait_ge(sem_l[k[1]] if k[0] == "l" else sem_e[k[1]], v)
                        if o.fn is None:
                            continue
                        ins = o.fn(eng)
                        if o.is_dma:
                            ins.then_inc(sem_l[o.lane], 16)
                        elif o.sig:
                            ins.then_inc(sem_e[e], 1)
                    if e == "sp":
                        for l in lanes:
                            eng.wait_ge(sem_l[l], 16 * self.lane_cnt[l])
                    if self.scopes and cur[0] is not None:
                        nc.leave_named_scope(cur[0], cur[1], False)
                return body

            for e in self.ENGS:
                handles[e](make(e))


def weight_plan():
    pl = []
    pl.append(("w_in", 0, 16, O_ALR, 16))
    for h in range(4):
        pl.append(("w_in", 0, 16, O_V + 512 * h, 256))
        pl.append(("w_in", 0, 16, O_V + 512 * h + 256, 256))
        pl.append(("w_in", 0, 16, O_K + 256 * h, 256))
    pl.append(("w_in", 0, 16, O_ALR, 16))
    for h in range(4):
        pl.append(("w_in", 0, 16, O_V + 512 * h, 256))
        pl.append(("w_in", 0, 16, O_V + 512 * h + 256, 256))
        pl.append(("w_in", 0, 16, O_K + 256 * h, 256))
        pl.append(("w_in", 0, 16, O_Q + 256 * h, 256))
        pl.append(("w_in", 0, 16, O_G + 512 * h, 256))
        pl.append(("w_in", 0, 16, O_G + 512 * h + 256, 256))
    for c2 in range(8):
        pl.append(("w_in", 0, 16, O_GB + 256 * c2, 256))
        pl.append(("w_gla_out", 0, 16, 256 * c2, 256))
    for c2 in range(4):
        pl.append(("w_in", 0, 16, O_CC + 256 * c2, 256))
        pl.append(("w_in", 0, 16, O_CH + 256 * c2, 256))
        pl.append(("w_in", 0, 16, O_CB + 256 * c2, 256))
    for c4 in range(4):
        pl.append(("w_in", 0, 16, O_GA + 512 * c4, 256))
        pl.append(("w_in", 0, 16, O_GA + 512 * c4 + 256, 256))
        pl.append(("w_conv_out", 0, 8, 512 * c4, 512))
    for c2 in range(22):
        pl.append(("w_ffn_up", 0, 16, 256 * c2, 256))
        pl.append(("w_ffn_up", 0, 16, DFF + 256 * c2, 256))
    for c in range(16):
        pl.append(("w_ffn_down", 0, 22, 128 * c, 128))
        pl.append(("w_ffn_down", 22, 22, 128 * c, 128))
    return pl


def plan_offsets(pl):
    offs, o = [], 0
    for (_, _, kcn, _, ncol) in pl:
        offs.append(o)
        o += 128 * kcn * ncol
    return offs, o


def build_nc(scopes=False):
    nc = bass.Bass("TRN2", target_bir_lowering=False)
    pl = weight_plan()
    offs, nw = plan_offsets(pl)

    def din(name, shape):
        return nc.dram_tensor(name, shape, F32, kind="ExternalInput").ap()

    def dout(name, shape):
        return nc.dram_tensor(name, shape, F32, kind="ExternalOutput").ap()

    xo = din("xo", [T, D])
    xp = din("xp", [TP, D])
    wflat = din("wflat", [nw])
    wo_r = din("wo_r", [4, 128, 16 * 512])
    wg_d = din("wg", [17, 1024])
    sgla = din("sgla", [16, 4, 2, 128, 512])
    sconv = din("sconv", [32, DCV])
    sffn = din("sffn", [32, DFF])
    cst_d = din("cst", [128, NCST])
    gfin_d = din("gfin", [128, D])
    ident_d = din("ident", [128, 128])
    maskp_d = din("maskp", [128, 128])
    masks_d = din("masks", [128, 128])
    rmo_d = din("rmo", [128, T])
    seqmb_d = din("seqmb", [128, 16 * 128])
    seqmt_d = din("seqmt", [128, 16])
    rmp_d = din("rmp", [128, TP])

    y = dout("y", [T, D])
    o_convp = dout("o_convp", [2, DCV])
    o_ffnp = dout("o_ffnp", [2, DFF])
    o_glap = dout("o_glap", [4, 2, 128, 512])
    o_convs = dout("o_convs", [32, DCV])
    o_ffns = dout("o_ffns", [32, DFF])
    o_glas = dout("o_glas", [16, 4, 2, 128, 512])
    h_scr = nc.dram_tensor("h_scr", [T, D], F32).ap()
    ff_scr = nc.dram_tensor("ff_scr", [T, D], F32).ap()

    P = Prog(nc)
    P.scopes = scopes
    es = contextlib.ExitStack()

    def sb(name, shape, dt=F32):
        return es.enter_context(nc.sbuf_tensor("sb_sb_" + name, shape, dt))

    def mm(out, lhsT, rhs, start, stop, r, w):
        P.op("pe", lambda e: e.matmul(out, lhsT=lhsT, rhs=rhs, start=start, stop=stop), reads=r, writes=w)

    def tr(out, in_, ident, r, w):
        P.op("pe", lambda e: e.transpose(out, in_, ident), reads=r, writes=w)

    def act(out, in_, func, r, w, bias=None, scale=None, accum=None):
        kw = {}
        if bias is not None:
            kw["bias"] = bias
        if scale is not None:
            kw["scale"] = scale
        if accum is not None:
            kw["accum_out"] = accum
        P.op("act", lambda e: e.activation(out=out, in_=in_, func=func, **kw), reads=r, writes=w)

    def tt(out, in0, in1, op, r, w, eng="dve"):
        P.op(eng, lambda e: e.tensor_tensor(out=out, in0=in0, in1=in1, op=op), reads=r, writes=w)

    def ts(out, in0, s1, op0, r, w, s2=None, op1=None, eng="dve"):
        if op1 is None:
            P.op(eng, lambda e: e.tensor_scalar(out=out, in0=in0, scalar1=s1, scalar2=None, op0=op0), reads=r, writes=w)
        else:
            P.op(eng, lambda e: e.tensor_scalar(out=out, in0=in0, scalar1=s1, scalar2=s2, op0=op0, op1=op1),
                 reads=r, writes=w)

    def stt(out, in0, scalar, in1, op0, op1, r, w):
        P.op("dve", lambda e: e.scalar_tensor_tensor(out=out, in0=in0, scalar=scalar, in1=in1, op0=op0, op1=op1),
             reads=r, writes=w)

    def cp(out, in_, r, w, eng="dve"):
        P.op(eng, lambda e: e.tensor_copy(out=out, in_=in_), reads=r, writes=w)

    def memset(ap, val, w, eng="pool"):
        P.op(eng, lambda e: e.memset(ap, val), writes=w)

    ulane = {"n": 0}

    def dma(out, in_, lane, r=(), w=(), eng="sp"):
        if lane is None:
            ulane["n"] += 1
            lane = ("u", ulane["n"])
        P.dma(eng, lambda e: e.dma_start(out=out, in_=in_), lane=lane, reads=r, writes=w)

    ps = es.enter_context(nc.psum_tensor("ps", [128, 8, 512], F32))
    psb = ps[:].bitcast(BF16)
    ring = sb("ring", [128, NSLOT, SLOT], BF16)
    cst = sb("cst", [128, NCST])
    identf = sb("identf", [128, 128])
    identb = sb("identb", [128, 128], BF16)
    maskp = sb("maskp", [128, 128])
    masks = sb("masks", [128, 128])
    ucol = sb("ucol", [128, 8, NCOL])
    sprev_u = sb("sprev_u", [128, 8, 32])

    seqmt = sb("seqmt", [128, 16])
    PSB = lambda b: ("ps", b)

    arena_start = (nc.sbuf_base + 63) // 64 * 64
    AR = nc.sbuf_top - arena_start - 128
    es.enter_context(nc.sbuf_tensor("sb_fence", [128, AR // 4], F32))
    cnt = {"n": 0}

    def AT(name, shape, dt, off):
        nbytes = int(np.prod(shape[1:])) * (4 if dt == F32 else 2)
        assert off % 32 == 0 and off + nbytes <= AR, (name, off, nbytes, AR)
        cnt["n"] += 1
        return nc.alloc_sbuf_tensor_at("a%d_%s" % (cnt["n"], name), shape, dt, offset=arena_start + off)

    class Bump:
        def __init__(self, off, limit=None):
            self.off = off
            self.limit = limit

        def __call__(self, name, shape, dt=F32):
            nbytes = int(np.prod(shape[1:])) * (4 if dt == F32 else 2)
            o = (self.off + 63) // 64 * 64
            self.off = o + nbytes
            if self.limit is not None:
                assert self.off <= self.limit, (name, self.off, self.limit)
            return AT(name, shape, dt, o)

    BIGB = 16 * T * 2
    R0, R1, R2 = 0, BIGB, 2 * BIGB
    O_SST = R2
    O_WG = O_SST + 16384
    O_P23 = O_WG + 4096
    Sst = AT("Sst", [128, 4, 2, 512], F32, O_SST)
    wg = AT("wg", [32, 1024], F32, O_WG)

    dma(cst[:], cst_d, None, w=["cst"])
    dma(identf[:], ident_d, None, w=["identf"])
    dma(maskp[:], maskp_d, None, w=["maskp"])
    dma(masks[:], masks_d, None, w=["masks"])
    dma(wg[0:17, :], wg_d, None, w=["wg"])
    cp(identb[:], identf[:], ["identf"], ["identb"])
    memset(Sst[:], 0.0, [("S", h_, d_) for h_ in range(4) for d_ in range(2)])
    dma(seqmt[:], seqmt_d, None, w=["seqmt"])

    wstate = {"next_dma": 0, "next_use": 0, "released": 0}

    def issue_block_dma():
        i = wstate["next_dma"]
        if i >= len(pl):
            return
        assert i - NSLOT < wstate["released"], "weight ring overrun"
        wstate["next_dma"] += 1
        (_, _, kcn, _, ncol) = pl[i]
        n = kcn * ncol
        s = i % NSLOT
        src = wflat[offs[i]:offs[i] + 128 * n].rearrange("(p n) -> p n", p=128)
        for c0 in range(0, n, 2048):
            c1 = min(n, c0 + 2048)
            dma(ring[:, s, c0:c1], src[:, c0:c1], ("w", s), w=[("ring", s)], eng="pool")

    def next_block():
        i = wstate["next_use"]
        wstate["next_use"] += 1
        while wstate["next_dma"] <= i:
            issue_block_dma()
        (_, _, kcn, _, ncol) = pl[i]
        s = i % NSLOT
        v = ring[:, s, 0:kcn * ncol].rearrange("p (k n) -> p k n", k=kcn)
        return v, ("ring", s)

    def prefetch():
        wstate["released"] = wstate["next_use"]
        while wstate["next_dma"] < min(len(pl), wstate["next_use"] + NSLOT):
            issue_block_dma()

    for _ in range(NSLOT):
        issue_block_dma()

    def mm_a(blk, rk, j, M, kcn, actT, act_key, ntiles, banks, first=True, last=True, kc_off=0):
        for kc in range(kcn):
            for i, (t0, tn) in enumerate(ntiles):
                mm(ps[0:M, banks[i], 0:tn], blk[:, kc, j * M:(j + 1) * M], actT[:, kc_off + kc, t0:t0 + tn],
                   start=(first and kc == 0), stop=(last and kc == kcn - 1),
                   r=[rk, act_key], w=[PSB(banks[i])])

    bank_set = {"i": 0}

    def next_banks():
        b = bank_set["i"]
        bank_set["i"] ^= 1
        return [3 * b, 3 * b + 1, 3 * b + 2]

    def norm_tiles(*a, **kw):
        for _ in norm_tiles_gen(*a, **kw):
            pass

    def norm_tiles_gen(src, tiles, dstT, dst_key, gcol0, tag, bump, add_fn=None, store_h=None):
        xt = [bump("xt", [128, D], F32) for i in range(2)]
        xn = [bump("xn", [128, D], BF16) for i in range(2)]
        sq = bump("sq", [128, D], BF16)
        st = bump("st", [128, 2, 4], F32)

        def load(ti):
            if ti < len(tiles):
                t0, n = tiles[ti]
                dma(xt[ti % 2][0:n, :], src[t0:t0 + n, :], ("x", ti % 2), w=[("xt", tag, ti % 2)])

        def stage1(ti):
            t0, n = tiles[ti]
            b = ti % 2
            kx, kn = ("xt", tag, b), ("xn", tag, b)
            if add_fn is not None:
                add_fn(ti, t0, n, xt[b], kx)
            if store_h is not None:
                dma(store_h[t0:t0 + n, :], xt[b][0:n, :], ("hs", b), r=[kx])
            act(sq[0:n, :], xt[b][0:n, :], AF.Square, [kx], ["sq" + tag, ("st", tag, b)], accum=st[0:n, b, 0:1])
            act(st[0:n, b, 1:2], st[0:n, b, 0:1], AF.Ln, [("st", tag, b)], [("st1", tag, b)], scale=1.0 / D, bias=EPS)
            act(st[0:n, b, 2:3], st[0:n, b, 1:2], AF.Exp, [("st1", tag, b)], [("st2", tag, b)], scale=-0.5)
            ts(xn[b][0:n, :], xt[b][0:n, :], st[0:n, b, 2:3], ALU.mult, [kx, ("st2", tag, b)], [kn])

        def stage2(ti):
            t0, n = tiles[ti]
            b = ti % 2
            kn = ("xn", tag, b)
            for half in range(2):
                bank = 6 + half
                for k8 in range(8):
                    kc = half * 8 + k8
                    tr(psb[:, bank, k8 * 128:k8 * 128 + n], xn[b][0:n, kc * 128:(kc + 1) * 128],
                       identb[0:n, 0:n], [kn, "identb"], [PSB(bank)])
                gview = cst[:, gcol0 + half * 8:gcol0 + half * 8 + 8].unsqueeze(2).to_broadcast([128, 8, n])
                pview = psb[:, bank, 0:1024].rearrange("p (k t) -> p k t", k=8)[:, :, 0:n]
                tt(dstT[:, half * 8:half * 8 + 8, t0:t0 + n], pview, gview, ALU.mult,
                   [PSB(bank), "cst"], [dst_key])

        load(0)
        load(1)
        for ti in range(len(tiles)):
            stage1(ti)
            if ti >= 1:
                stage2(ti - 1)
            load(ti + 2)
            yield
        stage2(len(tiles) - 1)
        yield

    P.phase = "P1a"
    if True:
        npT = AT("npT", [128, KC, TP], BF16, R1)
        norm_tiles(xp, CH_PRE, npT, "npT", C_GMIX, "p", Bump(O_P23))
        P.barrier()

        P.phase = "P2"
        if True:
            b2 = Bump(O_P23)
            alr = b2("alrp", [32, TP], F32)
            rm = b2("rmp", [128, TP], BF16)
            A = b2("Ap", [128, 2, TP], F32)
            B = b2("Bp", [128, 2, TP], F32)
            KD = b2("KDp", [128, 2, TP], BF16)
            vt = b2("vtp", [128, 9, 512], BF16)
            kdta = b2("kdtp", [128, 9, 256], BF16)
            vTs = b2("vTsp", [128, 4, TP], BF16)
            nT = AT("nT", [128, KC, T], BF16, R0)
            g1b = norm_tiles_gen(xo, TILES_OWN, nT, "nT", C_GMIX, "o", Bump(b2.off))
            dma(rm[:], rmp_d, None, w=["rm"], eng="pool")
            memset(alr[:], 1.0, ["alr"])
            blk, rk = next_block()
            mm_a(blk, rk, 0, 16, 16, npT, "npT", NT_PRE, [0, 1, 2])
            prefetch()
            for i, (t0, tn) in enumerate(NT_PRE):
                cp(alr[0:16, t0:t0 + tn], ps[0:16, i, 0:tn], [PSB(i)], ["alr"])
            for h in range(4):
                for dk in range(2):
                    banks = next_banks()
                    for i, (t0, tn) in enumerate(NT_PRE):
                        mm(ps[:, banks[i], 0:tn], wg[0:17, h * 256 + dk * 128:h * 256 + (dk + 1) * 128],
                           alr[0:17, t0:t0 + tn], True, True, ["wg", "alr"], [PSB(banks[i])])
                        act(A[:, dk, t0:t0 + tn], ps[:, banks[i], 0:tn], AF.Exp, [PSB(banks[i])], [("A", dk)], scale=-1.0)

                def batch(dk, A=A, B=B, rm=rm):
                    act(A[:, dk, :], A[:, dk, :], AF.Ln, [("A", dk)], [("A", dk)], bias=1.0)
                    P.op("dve", lambda e: e.tensor_tensor_scan(
                        out=B[:, dk, :], data0=rm[:], data1=A[:, dk, :], initial=0.0, op0=ALU.mult, op1=ALU.add),
                        reads=[("A", dk), "rm"], writes=[("B", dk)])
                    full = B[:, dk, 0:1024].rearrange("p (c t) -> p c t", c=8)
                    tt(A[:, dk, 0:1024].rearrange("p (c t) -> p c t", c=8), full,
                       full[:, :, 127:128].to_broadcast([128, 8, 128]), ALU.subtract, [("B", dk)], [("A", dk)])
                    tt(A[:, dk, 1024:TP], B[:, dk, 1024:TP], B[:, dk, TP - 1:TP].to_broadcast([128, 4]),
                       ALU.subtract, [("B", dk)], [("A", dk)])
                    act(A[:, dk, :], A[:, dk, :], AF.Exp, [("A", dk)], [("A", dk)], scale=1.0 / 16)
                    ends = full[:, :, 127:128]
                    act(ends, ends, AF.Exp, [("B", dk)], [("B", dk)], scale=-1.0 / 16)
                    act(B[:, dk, TP - 1:TP], B[:, dk, TP - 1:TP], AF.Exp, [("B", dk)], [("B", dk)], scale=-1.0 / 16)

                for g2 in range(2):
                    vblk, vk = next_block()
                    for j in range(2):
                        c = 2 * g2 + j
                        banks = next_banks()
                        mm_a(vblk, vk, j, 128, 16, npT, "npT", NT_PRE, banks)
                        for i, (t0, tn) in enumerate(NT_PRE):
                            act(vTs[:, c, t0:t0 + tn], ps[:, banks[i], 0:tn], AF.Copy, [PSB(banks[i])], ["vTs"])
                        if c == 0:
                            batch(0)
                        if c == 1:
                            batch(1)
                    prefetch()
                for ci, (t0, n) in enumerate(CH_PRE):
                    bank = 6 + (ci % 2)
                    for c in range(4):
                        tr(psb[0:n, bank, c * 128:(c + 1) * 128], vTs[:, c, t0:t0 + n], identb[:, :],
                           ["vTs", "identb"], [PSB(bank)])
                    act(vt[0:n, ci, :], psb[0:n, bank, 0:512], AF.Copy, [PSB(bank)], [("vt", ci)])
                for _ in range(3 if h < 3 else 2):
                    next(g1b, None)
                blk, rk = next_block()
                for dk in range(2):
                    banks = next_banks()
                    mm_a(blk, rk, dk, 128, 16, npT, "npT", NT_PRE, banks)
                    for i, (t0, tn) in enumerate(NT_PRE):
                        tt(KD[:, dk, t0:t0 + tn], ps[:, banks[i], 0:tn], A[:, dk, t0:t0 + tn], ALU.mult,
                           [PSB(banks[i]), ("A", dk)], [("KD", dk)])
                prefetch()
                for ci, (t0, n) in enumerate(CH_PRE):
                    bank = 6 + (ci // 4) % 2
                    q4 = ci % 4
                    for dk in range(2):
                        tr(psb[0:n, bank, q4 * 256 + dk * 128:q4 * 256 + (dk + 1) * 128], KD[:, dk, t0:t0 + n], identb[:, :],
                           [("KD", dk), "identb"], [PSB(bank)])
                    act(kdta[0:n, ci, :], psb[0:n, bank, q4 * 256:(q4 + 1) * 256], AF.Copy, [PSB(bank)], [("kdt", ci)])
                for ci, (t0, n) in enumerate(CH_PRE):
                    b = ci % 2
                    for dk in range(2):
                        pb = 2 * b + dk
                        mm(ps[:, pb, :], kdta[0:n, ci, dk * 128:(dk + 1) * 128], vt[0:n, ci, :], True, True,
                           [("kdt", ci), ("vt", ci)], [PSB(pb)])
                        stt(Sst[:, h, dk, :], Sst[:, h, dk, :], B[:, dk, t0 + n - 1:t0 + n], ps[:, pb, :],
                            ALU.mult, ALU.add, [("S", h, dk), ("B", dk), PSB(pb)], [("S", h, dk)])
            for _ in g1b:
                pass
            P.barrier()
    P.phase = "P1b"
    b1 = Bump(O_P23)
    if True:
        stl = b1("stl", [32, DCV], F32)
        dma(stl[:, :], sconv, None, w=["stl"])
        for c in range(8):
            tr(ps[:, 0, c * 32:(c + 1) * 32], stl[0:32, c * 128:(c + 1) * 128], identf[0:32, 0:32],
               ["stl", "identf"], [PSB(0)])
        cp(sprev_u[:], ps[:, 0, 0:256].rearrange("p (c s) -> p c s", c=8), [PSB(0)], ["sprev_u"])
        P.barrier()

    P.phase = "P3"
    ogT = AT("ogT", [128, KC, T], BF16, R1)
    if True:
        t3 = Bump(O_P23)
        alr = t3("alro", [32, T])
        rm = t3("rmo", [128, T], BF16)
        seqmb = t3("seqmb", [128, 16, 128], BF16)
        o_A = (t3.off + 63) // 64 * 64
        A = t3("Ao", [128, 2, T])
        sgh = AT("sgh", [128, 4, T], BF16, o_A)
        B = t3("Bo", [128, 2, T])
        QE = t3("QE", [128, 2, T], BF16)
        o_KE = (t3.off + 63) // 64 * 64
        KE = t3("KE", [128, 2, T], BF16)
        vTs = AT("vTs", [128, 4, T], BF16, o_KE)
        KD = t3("KD", [128, 2, T], BF16)
        vt = t3("vto", [128, 10, 512], BF16)
        kdta = t3("kdta", [128, 10, 256], BF16)
        attma = t3("attma", [128, 10, 128], BF16)
        Sbf = [t3("Sbf%d" % i, [128, 2, 512], BF16) for i in range(2)]
        og = [t3("og%d" % i, [128, 512], BF16) for i in range(2)]
        ost = t3("ost", [128, 2, 4])
        S0 = [t3("S0_%d" % i, [128, 2, 512]) for i in range(4)]
        S0b = [t3("S0b_%d" % i, [128, 2, 512], BF16) for i in range(4)]
        QXs = [t3("QXs%d" % i, [128, 2, 128], BF16) for i in range(2)]
        KXs = [t3("KXs%d" % i, [128, 256], BF16) for i in range(2)]
        dma(rm[:], rmo_d, None, w=["rm"], eng="pool")
        dma(seqmb[:].rearrange("p s t -> p (s t)"), seqmb_d, None, w=["seqmb"], eng="pool")
        memset(alr[:], 1.0, ["alr"])
        blk, rk = next_block()
        mm_a(blk, rk, 0, 16, 16, nT, "nT", NT_OWN, [0, 1, 2])
        prefetch()
        for i, (t0, tn) in enumerate(NT_OWN):
            cp(alr[0:16, t0:t0 + tn], ps[0:16, i, 0:tn], [PSB(i)], ["alr"])

        def chunk_views(buf, dk):
            return (buf[:, dk, 0:1024].rearrange("p (c t) -> p c t", c=8), buf[:, dk, 1024:NPO],
                    buf[:, dk, NPO:T].rearrange("p (s t) -> p s t", s=16))
        AK = [("A", 0), ("A", 1)]

        for h in range(4):
            for dk in range(2):
                banks = next_banks()
                for i, (t0, tn) in enumerate(NT_OWN):
                    mm(ps[:, banks[i], 0:tn], wg[0:17, h * 256 + dk * 128:h * 256 + (dk + 1) * 128],
                       alr[0:17, t0:t0 + tn], True, True, ["wg", "alr"], [PSB(banks[i])])
                    act(A[:, dk, t0:t0 + tn], ps[:, banks[i], 0:tn], AF.Exp, [PSB(banks[i])], [("A", dk)], scale=-1.0)

            def batch(dk, A=A, B=B, rm=rm):
                act(A[:, dk, :], A[:, dk, :], AF.Ln, [("A", dk)], [("A", dk)], bias=1.0)
                P.op("dve", lambda e: e.tensor_tensor_scan(
                    out=B[:, dk, :], data0=rm[:], data1=A[:, dk, :], initial=0.0, op0=ALU.mult, op1=ALU.add),
                    reads=[("A", dk), "rm"], writes=[("B", dk)])
                act(A[:, dk, :], B[:, dk, :], AF.Exp, [("B", dk)], [("A", dk)], scale=1.0 / 16)

            KK = [("KE", 0), ("KE", 1), ("KD", 0), ("KD", 1)]

            def v_units():
                for g2 in range(2):
                    vblk, vk = next_block()
                    for j in range(2):
                        c = 2 * g2 + j
                        banks = next_banks()
                        mm_a(vblk, vk, j, 128, 16, nT, "nT", NT_OWN, banks)
                        for i, (t0, tn) in enumerate(NT_OWN):
                            act(vTs[:, c, t0:t0 + tn], ps[:, banks[i], 0:tn], AF.Copy, [PSB(banks[i])] + KK, KK)
                    prefetch()
            if h == 0:
                v_units()
            batch(0)
            batch(1)
            for ci, (t0, n) in enumerate(TILES_OWN):
                bank = 6 + (ci % 2)
                for c in range(4):
                    tr(psb[0:n, bank, c * 128:(c + 1) * 128], vTs[:, c, t0:t0 + n], identb[:, :],
                       KK + ["identb"], [PSB(bank)])
                act(vt[0:n, ci, :], psb[0:n, bank, 0:512], AF.Copy, [PSB(bank)], [("vt", ci)])
            blk, rk = next_block()
            for dk in range(2):
                banks = next_banks()
                mm_a(blk, rk, dk, 128, 16, nT, "nT", NT_OWN, banks)
                for i, (t0, tn) in enumerate(NT_OWN):
                    tt(KE[:, dk, t0:t0 + tn], ps[:, banks[i], 0:tn], A[:, dk, t0:t0 + tn], ALU.mult,
                       [PSB(banks[i]), ("A", dk)] + KK, [("KE", dk)])
                fb, sb_, smb = chunk_views(B, dk)
                fa, sa, sma = chunk_views(A, dk)
                tt(fa, fb, fb[:, :, 127:128].to_broadcast([128, 8, 128]), ALU.subtract, [("B", dk)], [("A", dk)])
                tt(sa, sb_, B[:, dk, NPO - 1:NPO].to_broadcast([128, 12]), ALU.subtract, [("B", dk)], [("A", dk)])
                tt(sma, smb, smb[:, :, 7:8].to_broadcast([128, 16, 8]), ALU.subtract, [("B", dk)], [("A", dk)])
                act(A[:, dk, :], A[:, dk, :], AF.Exp, [("A", dk)], [("A", dk)], scale=1.0 / 16)
                for i, (t0, tn) in enumerate(NT_OWN):
                    tt(KD[:, dk, t0:t0 + tn], ps[:, banks[i], 0:tn], A[:, dk, t0:t0 + tn], ALU.mult,
                       [PSB(banks[i]), ("A", dk)] + KK, [("KD", dk)])
                act(A[:, dk, :], B[:, dk, :], AF.Exp, [("B", dk), ("KD", dk)], [("A", dk)], scale=-1.0 / 16)
            prefetch()
            for ci, (t0, n) in enumerate(TILES_OWN):
                bank = 6 + (ci // 4) % 2
                q4 = ci % 4
                for dk in range(2):
                    tr(psb[0:n, bank, q4 * 256 + dk * 128:q4 * 256 + (dk + 1) * 128], KD[:, dk, t0:t0 + n], identb[:, :],
                       [("KD", dk), "identb"], [PSB(bank)])
                act(kdta[0:n, ci, :], psb[0:n, bank, q4 * 256:(q4 + 1) * 256], AF.Copy, [PSB(bank)], [("kdt", ci)])
            blk, rk = next_block()
            for dk in range(2):
                banks = next_banks()
                mm_a(blk, rk, dk, 128, 16, nT, "nT", NT_OWN, banks)
                for i, (t0, tn) in enumerate(NT_OWN):
                    stt(QE[:, dk, t0:t0 + tn], ps[:, banks[i], 0:tn], 1.0 / 16, A[:, dk, t0:t0 + tn], ALU.mult, ALU.mult,
                        [PSB(banks[i]), ("A", dk)], [("QE", dk)])
                fb, sb_, smb = chunk_views(B, dk)
                act(fb[:, :, 127:128], fb[:, :, 127:128], AF.Exp, [("B", dk), ("A", dk)], [("B", dk)], scale=-1.0 / 16)
                act(B[:, dk, NPO - 1:NPO], B[:, dk, NPO - 1:NPO], AF.Exp, [("B", dk)], [("B", dk)], scale=-1.0 / 16)
                act(smb[:, :, 7:8], smb[:, :, 7:8], AF.Exp, [("B", dk)], [("B", dk)], scale=-1.0 / 16)
            prefetch()
            for ci, (t0, n) in enumerate(TILES_OWN):
                bank = 6 + (ci // 4) % 2
                q4 = ci % 4
                for dk in range(2):
                    mm(ps[0:n, bank, q4 * 128:q4 * 128 + n], KE[:, dk, t0:t0 + n], QE[:, dk, t0:t0 + n], dk == 0, dk == 1,
                       [("KE", dk), ("QE", dk)], [PSB(bank)])
                msk = masks if ci == 9 else maskp
                tt(attma[0:n, ci, 0:n], ps[0:n, bank, q4 * 128:q4 * 128 + n], msk[0:n, 0:n], ALU.mult,
                   [PSB(bank), "maskp", "masks"], [("attm", ci)])
            for dk in range(2):
                act(Sbf[0][:, dk, :], Sst[:, h, dk, :], AF.Copy, [("S", h, dk)], [("Sbf", 0, dk)])

            def ep_a(ci, n, obank):
                b = ci % 2
                act(og[b][0:n, :], ps[0:n, obank, :], AF.Square, [PSB(obank)], [("og", b), ("ost", b)], accum=ost[0:n, b, 0:1])
                act(ost[0:n, b, 1:2], ost[0:n, b, 0:1], AF.Ln, [("ost", b)], [("ost1", b)], scale=1.0 / 512, bias=EPS)
                act(ost[0:n, b, 2:3], ost[0:n, b, 1:2], AF.Exp, [("ost1", b)], [("ost2", b)], scale=-0.5)
                ts(og[b][0:n, :], ps[0:n, obank, :], ost[0:n, b, 2:3], ALU.mult, [PSB(obank), ("ost2", b)], [("og", b)])

            def ep_b(ci, t0, n, h=h):
                b = ci % 2
                tb = 6 + b
                for c in range(4):
                    tr(psb[:, tb, 512 + c * 128:512 + c * 128 + n], og[b][0:n, c * 128:(c + 1) * 128], identb[0:n, 0:n],
                       [("og", b), "identb"], [PSB(tb)])
                pview = psb[:, tb, 512:1024].rearrange("p (c t) -> p c t", c=4)[:, :, 0:n]
                gview = cst[:, C_GGLA:C_GGLA + 4].unsqueeze(2).to_broadcast([128, 4, n])
                tt(ogT[:, 4 * h:4 * h + 4, t0:t0 + n], pview, gview, ALU.mult, [PSB(tb), "cst"], [("ogT", h)])

            def P_mm(ci):
                t0, n = CH_OWN[ci]
                for dk in range(2):
                    pb = 2 + 2 * (ci % 2) + dk
                    mm(ps[:, pb, :], kdta[0:n, ci, dk * 128:(dk + 1) * 128], vt[0:n, ci, :], True, True,
                       [("kdt", ci), ("vt", ci)], [PSB(pb)])

            def load_state(s, h=h):
                sbi = s % 4
                for dk in range(2):
                    dma(S0[sbi][:, dk, :], sgla[s, h, dk], ("s0", sbi, dk), w=[("S0", sbi, dk)])
            for s_ in range(3):
                load_state(s_)

            P_mm(0)
            for ci, (t0, n) in enumerate(CH_OWN):
                b = ci % 2
                sb_cur, sb_nxt = ci % 2, (ci + 1) % 2
                if ci + 1 < len(CH_OWN):
                    P_mm(ci + 1)
                ob = b
                mm(ps[0:n, ob, :], attma[0:n, ci, 0:n], vt[0:n, ci, :], True, False, [("attm", ci), ("vt", ci)], [PSB(ob)])
                for dk in range(2):
                    mm(ps[0:n, ob, :], QE[:, dk, t0:t0 + n], Sbf[sb_cur][:, dk, :], False, dk == 1,
                       [("QE", dk), ("Sbf", sb_cur, dk)], [PSB(ob)])
                for dk in range(2):
                    pb = 2 + 2 * b + dk
                    dl = B[:, dk, t0 + n - 1:t0 + n]
                    stt(Sbf[sb_nxt][:, dk, :], Sst[:, h, dk, :], dl, ps[:, pb, :], ALU.mult, ALU.add,
                        [("S", h, dk), ("B", dk), PSB(pb)], [("Sbf", sb_nxt, dk)])
                    stt(Sst[:, h, dk, :], Sst[:, h, dk, :], dl, ps[:, pb, :], ALU.mult, ALU.add,
                        [("S", h, dk), ("B", dk), PSB(pb)], [("S", h, dk)])
                ep_a(ci, n, ob)
                if ci >= 1:
                    ep_b(ci - 1, CH_OWN[ci - 1][0], CH_OWN[ci - 1][1])
            ep_b(len(CH_OWN) - 1, CH_OWN[-1][0], CH_OWN[-1][1])
            for dk in range(2):
                dma(o_glap[h, dk], Sst[:, h, dk, :], ("gp", dk), r=[("S", h, dk)])

            def g_stream():
                gbanks = [5, 6, 7]
                for g2 in range(2):
                    blk, rk = next_block()
                    for j in range(2):
                        c = 2 * g2 + j
                        cnt_ = 0
                        for kc in range(KC):
                            for i, (t0_, tn) in enumerate(NT_OWN):
                                mm(ps[:, gbanks[i], 0:tn], blk[:, kc, j * 128:(j + 1) * 128], nT[:, kc, t0_:t0_ + tn],
                                   start=(kc == 0), stop=(kc == KC - 1), r=[rk, "nT"], w=[PSB(gbanks[i])])
                                cnt_ += 1
                                if cnt_ % 12 == 0 and cnt_ < 48:
                                    yield
                        for i, (t0_, tn) in enumerate(NT_OWN):
                            act(sgh[:, c, t0_:t0_ + tn], ps[:, gbanks[i], 0:tn], AF.Silu, [PSB(gbanks[i])] + AK, AK)
                        yield
                    prefetch()

            gs = g_stream()
            t0, n = SAMPLE
            ci = 9
            ob = 0
            mm(ps[:, ob, :], attma[:, ci, :], vt[:, ci, :], True, False, [("attm", ci), ("vt", ci)], [PSB(ob)])

            def prep(s):
                sbi, xb = s % 4, s % 2
                tt(QXs[xb][:, :, :], QE[:, :, t0:t0 + n], seqmb[:, s, :].unsqueeze(1).to_broadcast([128, 2, 128]),
                   ALU.mult, [("QE", 0), ("QE", 1), "seqmb"], [("QXs", xb)])
                ts(KXs[xb][:, :], kdta[:, ci, :], seqmt[:, s:s + 1], ALU.mult, [("kdt", ci), "seqmt"], [("KXs", xb)])
                act(S0b[sbi][:, :, :], S0[sbi][:, :, :], AF.Copy, [("S0", sbi, 0), ("S0", sbi, 1)], [("S0b", sbi)])
            prep(0)
            for s in range(16):
                sbi = s % 4
                xb = s % 2
                for dk in range(2):
                    mm(ps[:, ob, :], QXs[xb][:, dk, :], S0b[sbi][:, dk, :], False, (s == 15 and dk == 1),
                       [("QXs", xb), ("S0b", sbi)], [PSB(ob)])
                for dk in range(2):
                    pb = 1 + 2 * xb + dk
                    mm(ps[:, pb, :], KXs[xb][:, dk * 128:(dk + 1) * 128], vt[:, ci, :], True, True,
                       [("KXs", xb), ("vt", ci)], [PSB(pb)])
                if s + 1 < 16:
                    prep(s + 1)
                for dk in range(2):
                    pb = 1 + 2 * xb + dk
                    stt(S0[sbi][:, dk, :], S0[sbi][:, dk, :], B[:, dk, t0 + 8 * s + 7:t0 + 8 * s + 8], ps[:, pb, :],
                        ALU.mult, ALU.add, [("S0", sbi, dk), ("B", dk), PSB(pb)], [("S0", sbi, dk)])
                if s + 3 < 16:
                    load_state(s + 3)
                for dk in range(2):
                    dma(o_glas[s, h, dk], S0[sbi][:, dk, :], ("so", sbi, dk), r=[("S0", sbi, dk)])
                next(gs, None)
            for _ in gs:
                pass
            ep_a(ci, n, ob)
            if h + 1 < 4:
                v_units()
            ep_b(ci, t0, n)
            for c in range(4):
                tt(ogT[:, 4 * h + c, :], ogT[:, 4 * h + c, :], sgh[:, c, :], ALU.mult, [("ogT", h)] + AK, [("ogT", h)])
        P.barrier()

    P.phase = "P4"
    merged = AT("merged", [128, KC, T], BF16, R2)
    O_WOB = (AR - 32768) // 64 * 64
    woB = AT("woB", [128, 2, 16 * 512], BF16, O_WOB)
    t5 = Bump(3 * BIGB, O_WOB)
    sgt = [t5("sgt", [128, T], BF16) for i in range(4)]
    for c2 in range(8):
        gb, gbk = next_block()
        for j in range(2):
            banks = next_banks()
            mm_a(gb, gbk, j, 128, 16, nT, "nT", NT_OWN, banks)
            for i, (t0, tn) in enumerate(NT_OWN):
                act(sgt[j][:, t0:t0 + tn], ps[:, banks[i], 0:tn], AF.Sigmoid, [PSB(banks[i])], [("sgt", j)])
        prefetch()
        go, gok = next_block()
        for j in range(2):
            c = 2 * c2 + j
            banks = next_banks()
            mm_a(go, gok, j, 128, 16, ogT, "ogTall", NT_OWN, banks)
            for i, (t0, tn) in enumerate(NT_OWN):
                tt(merged[:, c, t0:t0 + tn], ps[:, banks[i], 0:tn], sgt[j][:, t0:t0 + tn], ALU.mult,
                   [PSB(banks[i]), ("sgt", j)], [("merged", c)])
        prefetch()
        g_, q_ = 2 + c2 // 4, c2 % 4
        dma(woB[:, g_ - 2, q_ * 2048:(q_ + 1) * 2048], wo_r[g_][:, q_ * 2048:(q_ + 1) * 2048], ("wo", g_),
            w=[("wo", g_)], eng="pool")
    P.barrier()

    P.phase = "P5"
    def conv3(dst, src, wcol, key_src, key_dst, tmp, key_tmp):
        regs = [(src[:, 0:EXT - 160], dst[:, 0:NPO], tmp[:, 0:NPO], NPO, None),
                (src[:, EXT - 160:EXT].rearrange("p (s t) -> p s t", s=16),
                 dst[:, NPO:T].rearrange("p (s t) -> p s t", s=16),
                 tmp[:, NPO:T].rearrange("p (s t) -> p s t", s=16), 8, 16)]
        for (sv, dv, tv, L, ns) in regs:
            def sl(k):
                return sv[:, k:k + L] if ns is None else sv[:, :, k:k + L]
            ts(tv, sl(0), cst[:, wcol:wcol + 1], ALU.mult, [key_src, "cst"], [key_tmp])
            stt(tv, sl(1), cst[:, wcol + 1:wcol + 2], tv, ALU.mult, ALU.add, [key_src, "cst", key_tmp], [key_tmp])
            stt(dv, sl(2), cst[:, wcol + 2:wcol + 3], tv, ALU.mult, ALU.add, [key_src, "cst", key_tmp], [key_dst])

    if True:
        cbuc = AT("cbuc", [128, 8, T], BF16, R1)
        ccs = [t5("ccs%d" % i, [128, T]) for i in range(2)]
        ue = [t5("ue%d" % i, [128, EXT]) for i in range(2)]
        uc = [t5("uc%d" % i, [128, T]) for i in range(2)]
        ctmp = t5("ctmp", [128, T])
        uo = t5("uo", [NCOL, DCV])
        for b in range(2):
            memset(ue[b][:, 0:2], 0.0, [("ue", b)])
        for c2 in range(4):
            blk, rk = next_block()
            for j in range(2):
                banks = next_banks()
                mm_a(blk, rk, j, 128, 16, nT, "nT", NT_OWN, banks)
                for i, (t0, tn) in enumerate(NT_OWN):
                    act(ccs[j][:, t0:t0 + tn], ps[:, banks[i], 0:tn], AF.Copy, [PSB(banks[i])], [("ccs", j)])
            prefetch()
            blk, rk = next_block()
            for j in range(2):
                c = 2 * c2 + j
                banks = next_banks()
                mm_a(blk, rk, j, 128, 16, nT, "nT", NT_OWN, banks)
                ues = ue[j][:, EXT - 160:EXT].rearrange("p (s t) -> p s t", s=16)
                cp(ues[:, :, 0:2], sprev_u[:, c, :].rearrange("p (s r) -> p s r", s=16), ["sprev_u"], [("ue", j)])
                for i, (t0, tn) in enumerate(NT_OWN):
                    pn = min(tn, NPO - t0)
                    tt(ue[j][:, 2 + t0:2 + t0 + pn], ps[:, banks[i], 0:pn], ccs[j][:, t0:t0 + pn], ALU.mult,
                       [PSB(banks[i]), ("ccs", j)], [("ue", j)])
                    if pn < tn:
                        tt(ues[:, :, 2:10], ps[:, banks[i], pn:tn].rearrange("p (s t) -> p s t", s=16),
                           ccs[j][:, NPO:T].rearrange("p (s t) -> p s t", s=16), ALU.mult,
                           [PSB(banks[i]), ("ccs", j)], [("ue", j)])
                conv3(uc[j], ue[j], C_CW + 3 * c, ("ue", j), ("uc", j), ctmp, "ctmp")
                cp(ucol[:, c, 0:2], ue[j][:, NPO:NPO + 2], [("ue", j)], ["ucol"])
                cp(ucol[:, c, 2:NCOL].rearrange("p (s r) -> p s r", s=16), ues[:, :, 8:10], [("ue", j)], ["ucol"])
            prefetch()
            blk, rk = next_block()
            for j in range(2):
                c = 2 * c2 + j
                banks = next_banks()
                mm_a(blk, rk, j, 128, 16, nT, "nT", NT_OWN, banks)
                for i, (t0, tn) in enumerate(NT_OWN):
                    tt(cbuc[:, c, t0:t0 + tn], ps[:, banks[i], 0:tn], uc[j][:, t0:t0 + tn], ALU.mult,
                       [PSB(banks[i]), ("uc", j)], ["cbuc"])
            prefetch()
        for c in range(8):
            tr(ps[0:NCOL, 6 + c // 4, (c % 4) * 128:(c % 4 + 1) * 128], ucol[:, c, :], identf[:, :],
               ["ucol", "identf"], [PSB(6 + c // 4)])
        for hb in range(2):
            cp(uo[:, hb * 512:(hb + 1) * 512], ps[0:NCOL, 6 + hb, :], [PSB(6 + hb)], ["uo"])
        dma(o_convp, uo[0:2, :], "oc", r=["uo"])
        dma(o_convs, uo[2:NCOL, :], "oc", r=["uo"])
        for c4 in range(4):
            for gh in range(2):
                gab, gak = next_block()
                for jj in range(2):
                    j4 = 2 * gh + jj
                    banks = next_banks()
                    mm_a(gab, gak, jj, 128, 16, nT, "nT", NT_OWN, banks)
                    for i, (t0, tn) in enumerate(NT_OWN):
                        act(sgt[j4][:, t0:t0 + tn], ps[:, banks[i], 0:tn], AF.Sigmoid, [PSB(banks[i])], [("sgt", j4)])
                prefetch()
            co, cok = next_block()
            for j4 in range(4):
                c = 4 * c4 + j4
                banks = next_banks()
                mm_a(co, cok, j4, 128, 8, cbuc, "cbuc", NT_OWN, banks)
                for i, (t0, tn) in enumerate(NT_OWN):
                    tt(ctmp[:, t0:t0 + tn], ps[:, banks[i], 0:tn], sgt[j4][:, t0:t0 + tn], ALU.mult,
                       [PSB(banks[i]), ("sgt", j4)], ["ctmp"])
                    tt(merged[:, c, t0:t0 + tn], merged[:, c, t0:t0 + tn], ctmp[:, t0:t0 + tn], ALU.add,
                       [("merged", c), "ctmp"], [("merged", c)])
            prefetch()
        P.barrier()

    P.phase = "P6"
    n2T = AT("n2T", [128, KC, T], BF16, R0)
    if True:
        woA = AT("woA", [128, 2, 16 * 512], BF16, R1)

        def wo_g(g):
            return woA[:, g, :] if g < 2 else woB[:, g - 2, :]
        for g in range(2):
            for q in range(4):
                dma(wo_g(g)[:, q * 2048:(q + 1) * 2048], wo_r[g][:, q * 2048:(q + 1) * 2048], ("wo", g),
                    w=[("wo", g)], eng="pool")

        def add_mo(ti, t0, n, xt_t, kx):
            for g in (2, 3, 0, 1):
                bank = (4 * ti + g) % 6
                for kc in range(KC):
                    mm(ps[0:n, bank, :], merged[:, kc, t0:t0 + n], wo_g(g)[:, kc * 512:(kc + 1) * 512],
                       kc == 0, kc == KC - 1, ["mergedall", ("wo", g)], [PSB(bank)])
                tt(xt_t[0:n, g * 512:(g + 1) * 512], xt_t[0:n, g * 512:(g + 1) * 512], ps[0:n, bank, :], ALU.add,
                   [kx, PSB(bank)], [kx])
        norm_tiles(xo, TILES_OWN, n2T, "n2T", C_GFFN, "h", Bump(3 * BIGB, O_WOB), add_fn=add_mo, store_h=h_scr)
        P.barrier()

    P.phase = "P7"
    O_ACT = BIGB
    O_GCOL = O_ACT + FC * T * 2
    O_P7 = (O_GCOL + FC * NCOL * 4 + 63) // 64 * 64
    actb = AT("actb", [128, FC, T], BF16, O_ACT)
    gcol = AT("gcol", [128, FC, NCOL], F32, O_GCOL)
    if True:
        t7 = Bump(O_P7)
        sprev_g = t7("sprev_g", [128, FC, 32])
        a_sb = [t7("a_sb%d" % i, [128, T], BF16) for i in range(2)]
        gte = [t7("gte%d" % i, [128, EXT]) for i in range(2)]
        o_shared = t7.off
        stl = t7("stl2", [32, 1408])
        for pc in range(4):
            dma(stl[:, :], sffn[:, pc * 1408:(pc + 1) * 1408], None, w=["stl2"])
            for cc in range(11):
                c = pc * 11 + cc
                bnk, off = c // 16, (c % 16) * 32
                tr(ps[:, bnk, off:off + 32], stl[0:32, cc * 128:(cc + 1) * 128], identf[0:32, 0:32],
                   ["stl2", "identf"], [PSB(bnk)])
        for bnk in range(3):
            c0 = bnk * 16
            cn = min(16, FC - c0)
            cp(sprev_g[:, c0:c0 + cn, :], ps[:, bnk, 0:cn * 32].rearrange("p (c s) -> p c s", c=cn),
               [PSB(bnk)], ["sprev_g"])
        P.barrier()
        t7 = Bump(o_shared)
        gc1 = t7("gc", [128, T])
        gc = [gc1, gc1]
        ctmp = t7("ctmp7", [128, T])
        for b in range(2):
            memset(gte[b][:, 0:2], 0.0, [("gte", b)])
        for c2 in range(22):
            ba, bak = next_block()
            for j in range(2):
                banks = next_banks()
                mm_a(ba, bak, j, 128, 16, n2T, "n2T", NT_OWN, banks)
                for i, (t0, tn) in enumerate(NT_OWN):
                    act(a_sb[j][:, t0:t0 + tn], ps[:, banks[i], 0:tn], AF.Copy, [PSB(banks[i])], [("a_sb", j)])
            prefetch()
            bg, bgk = next_block()
            for j in range(2):
                c = 2 * c2 + j
                banks = next_banks()
                mm_a(bg, bgk, j, 128, 16, n2T, "n2T", NT_OWN, banks)
                gts = gte[j][:, EXT - 160:EXT].rearrange("p (s t) -> p s t", s=16)
                cp(gts[:, :, 0:2], sprev_g[:, c, :].rearrange("p (s r) -> p s r", s=16), ["sprev_g"], [("gte", j)])
                for i, (t0, tn) in enumerate(NT_OWN):
                    pn = min(tn, NPO - t0)
                    act(gte[j][:, 2 + t0:2 + t0 + pn], ps[:, banks[i], 0:pn], AF.Copy, [PSB(banks[i])], [("gte", j)])
                    if pn < tn:
                        act(gts[:, :, 2:10], ps[:, banks[i], pn:tn].rearrange("p (s t) -> p s t", s=16), AF.Copy,
                            [PSB(banks[i])], [("gte", j)])
                conv3(gc[j], gte[j], C_FW + 3 * c, ("gte", j), "gc", ctmp, "ctmp7")
                cp(gcol[:, c, 0:2], gte[j][:, NPO:NPO + 2], [("gte", j)], ["gcol"])
                cp(gcol[:, c, 2:NCOL].rearrange("p (s r) -> p s r", s=16), gts[:, :, 8:10], [("gte", j)], ["gcol"])
                act(gc[j][:, :], gc[j][:, :], AF.Silu, ["gc", "cst"], ["gc"], bias=cst[:, C_FB + c:C_FB + c + 1])
                tt(actb[:, c, :], gc[j][:, :], a_sb[j][:, :], ALU.mult, ["gc", ("a_sb", j)], ["actb"])
            prefetch()
        P.barrier()

    P.phase = "P8"
    if True:
        t8 = Bump(0, BIGB)
        go = t8("go", [NCOL, DFF])
        fft = [t8("fft%d" % i, [128, T]) for i in range(2)]
        t8b = Bump(O_P7)
        ftm = [t8b("ftm%d" % i, [128, 10, 128]) for i in range(2)]
        hpc = [t8b("hpc%d" % i, [128, 10, 128]) for i in range(2)]

        def p8_hload(c):
            b = c % 2
            kh = ("hpc", b)
            dma(hpc[b][:, 0:8, :], h_scr[0:1024, c * 128:(c + 1) * 128].rearrange("(i p) m -> p i m", p=128),
                ("hp", b), w=[kh])
            dma(hpc[b][0:12, 8, :], h_scr[1024:NPO, c * 128:(c + 1) * 128], ("hp", b), w=[kh])
            dma(hpc[b][:, 9, :], h_scr[NPO:T, c * 128:(c + 1) * 128], ("hp", b), w=[kh])
        for c in range(FC):
            bnk = 6 + (c // 4) % 2
            tr(ps[0:NCOL, bnk, (c % 4) * 128:(c % 4 + 1) * 128], gcol[:, c, :], identf[:, :], ["gcol", "identf"], [PSB(bnk)])
            if c % 4 == 3:
                cp(go[:, (c - 3) * 128:(c + 1) * 128], ps[0:NCOL, bnk, :], [PSB(bnk)], ["go"])
        dma(o_ffnp, go[0:2, :], "oc", r=["go"])
        dma(o_ffns, go[2:NCOL, :], "oc", r=["go"])
        def p8_out(c):
            b = c % 2
            for ti, (t0, n) in enumerate(TILES_OWN):
                bnk = 6 + (ti // 4) % 2
                tr(ps[0:n, bnk, (ti % 4) * 128:(ti % 4 + 1) * 128], fft[b][:, t0:t0 + n], identf[:, :],
                   [("fft", b), "identf"], [PSB(bnk)])
                if ti % 4 == 3 or ti == 9:
                    t_lo = ti - (ti % 4)
                    for tj in range(t_lo, ti + 1):
                        nn = TILES_OWN[tj][1]
                        tt(ftm[b][0:nn, tj, :], ps[0:nn, bnk, (tj % 4) * 128:(tj % 4 + 1) * 128], hpc[b][0:nn, tj, :], ALU.add,
                           [PSB(bnk), ("hpc", b)], [("ftm", b)])
            dma(ff_scr[0:1024, c * 128:(c + 1) * 128].rearrange("(i p) m -> p i m", p=128), ftm[b][:, 0:8, :],
                ("fo", b), r=[("ftm", b)])
            dma(ff_scr[1024:NPO, c * 128:(c + 1) * 128], ftm[b][0:12, 8, :], ("fo", b), r=[("ftm", b)])
            dma(ff_scr[NPO:T, c * 128:(c + 1) * 128], ftm[b][:, 9, :], ("fo", b), r=[("ftm", b)])

        for c in range(16):
            b = c % 2
            p8_hload(c)
            bA, bAk = next_block()
            banks = next_banks()
            mm_a(bA, bAk, 0, 128, 22, actb, "actb", NT_OWN, banks, first=True, last=False, kc_off=0)
            prefetch()
            bB, bBk = next_block()
            mm_a(bB, bBk, 0, 128, 22, actb, "actb", NT_OWN, banks, first=False, last=True, kc_off=22)
            prefetch()
            for i, (t0, tn) in enumerate(NT_OWN):
                act(fft[b][:, t0:t0 + tn], ps[:, banks[i], 0:tn], AF.Copy, [PSB(banks[i])], [("fft", b)])
            if c >= 1:
                p8_out(c - 1)
        p8_out(15)
        P.barrier()

    P.phase = "P9"
    if True:
        t9 = Bump(0)
        gfin = t9("gfin", [128, D])
        fa = [t9("fa%d" % i, [128, D]) for i in range(2)]
        ha = [t9("ha%d" % i, [128, D]) for i in range(2)]
        ya = [t9("ya%d" % i, [128, D]) for i in range(2)]
        sq9 = t9("sq9", [128, D], BF16)
        st9 = t9("st9", [128, 2, 4])
        dma(gfin[:], gfin_d, None, w=["gfin"])

        def load9(ti):
            if ti < len(TILES_OWN):
                t0, n = TILES_OWN[ti]
                b = ti % 2
                dma(fa[b][0:n, :], ff_scr[t0:t0 + n, :], ("f9", b), w=[("fa", b)])
        load9(0)
        load9(1)
        for ti, (t0, n) in enumerate(TILES_OWN):
            b = ti % 2
            act(sq9[0:n, :], fa[b][0:n, :], AF.Square, [("fa", b)], ["sq9", ("st9", b)], accum=st9[0:n, b, 0:1])
            act(st9[0:n, b, 1:2], st9[0:n, b, 0:1], AF.Ln, [("st9", b)], [("st91", b)], scale=1.0 / D, bias=EPS)
            act(st9[0:n, b, 2:3], st9[0:n, b, 1:2], AF.Exp, [("st91", b)], [("st92", b)], scale=-0.5)
            act(ya[b][0:n, :], fa[b][0:n, :], AF.Copy, [("fa", b), ("st92", b)], [("ya", b)], scale=st9[0:n, b, 2:3])
            tt(ya[b][0:n, :], ya[b][0:n, :], gfin[0:n, :], ALU.mult, [("ya", b), "gfin"], [("ya", b)])
            dma(y[t0:t0 + n, :], ya[b][0:n, :], ("y9", b), r=[("ya", b)], eng="act")
            load9(ti + 2)
    P.emit()
    es.close()
    return nc


_NC_CACHE = {}


def _layout_block(Wm, kc0, kcn, col0, ncol):
    sub = Wm[kc0 * 128:(kc0 + kcn) * 128, col0:col0 + ncol]
    return np.ascontiguousarray(sub.reshape(kcn, 128, ncol).transpose(1, 0, 2)).reshape(128, kcn * ncol)


def kernel(x_prompt, x_sample, state_conv, state_gla, state_ffn_conv, meta_tokens,
           norm_mix_g, w_in, conv_mix_w, w_conv_out, w_gate_up, b_gate, gla_norm_g,
           w_gla_out, w_o, norm_ffn_g, w_ffn_up, ffn_conv_w, ffn_conv_b, w_ffn_down,
           final_norm_g):
    f32 = np.float32
    A_ = lambda a: np.asarray(a, dtype=f32)
    x_prompt, x_sample = A_(x_prompt), A_(x_sample)
    mats = {"w_in": A_(w_in)[0], "w_conv_out": A_(w_conv_out)[0], "w_gla_out": A_(w_gla_out)[0],
            "w_ffn_up": A_(w_ffn_up)[0], "w_ffn_down": A_(w_ffn_down)[0]}
    pl = weight_plan()
    offs, nw = plan_offsets(pl)
    wflat = np.empty(nw, dtype=f32)
    for (name, kc0, kcn, col0, ncol), o in zip(pl, offs):
        wflat[o:o + 128 * kcn * ncol] = _layout_block(mats[name], kc0, kcn, col0, ncol).reshape(-1)
    wo = A_(w_o)[0]
    wo_r = np.stack([_layout_block(wo, 0, 16, 512 * g, 512) for g in range(4)])
    wg = np.concatenate([A_(w_gate_up)[0], A_(b_gate)[0][None, :]], axis=0)

    def pp(v, nch):
        return np.ascontiguousarray(A_(v).reshape(nch, 128).T)
    cst = np.zeros((128, NCST), dtype=f32)
    cst[:, C_GMIX:C_GMIX + 16] = pp(norm_mix_g[0], 16)
    cst[:, C_GFFN:C_GFFN + 16] = pp(norm_ffn_g[0], 16)
    cst[:, C_GGLA:C_GGLA + 4] = pp(gla_norm_g[0], 4)
    cw = A_(conv_mix_w)[0]
    fw = A_(ffn_conv_w)[0]
    for tap in range(3):
        cst[:, C_CW + tap:C_CW + 24:3] = pp(cw[tap], 8)
        cst[:, C_FW + tap:C_FW + 132:3] = pp(fw[tap], 44)
    cst[:, C_FB:C_FB + 44] = pp(ffn_conv_b[0], 44)
    gfin = np.ascontiguousarray(np.broadcast_to(A_(final_norm_g)[None, :], (128, D)))
    ident = np.eye(128, dtype=f32)
    maskp = np.triu(np.ones((128, 128), dtype=f32))
    seq = np.arange(128) // 8
    same = (seq[:, None] == seq[None, :]).astype(f32)
    masks = maskp * same
    seqmt = (seq[:, None] == np.arange(16)[None, :]).astype(f32)
    seqmb = np.ascontiguousarray(np.broadcast_to(seqmt.T.reshape(1, 16 * 128), (128, 16 * 128)))
    rmo = np.ones((128, T), dtype=f32)
    for (t0, n) in CH_OWN:
        rmo[:, t0] = 0.0
    rmo[:, NPO:T:8] = 0.0
    rmp = np.ones((128, TP), dtype=f32)
    for (t0, n) in CH_PRE:
        rmp[:, t0] = 0.0

    meta = A_(meta_tokens)
    in_maps = []
    for c in range(8):
        b, half = c // 2, c % 2
        hp_pad = np.concatenate([np.zeros((1032, D), f32), meta, x_prompt[b]], axis=0)
        base = half * 1032
        xp_c = hp_pad[base:base + TP]
        own = hp_pad[base + TP:base + TP + NPO]
        xo_c = np.concatenate([own, x_sample[16 * c:16 * c + 16].reshape(NS, D)], axis=0)
        in_maps.append({
            "xo": np.ascontiguousarray(xo_c), "xp": np.ascontiguousarray(xp_c), "wflat": wflat, "wo_r": wo_r,
            "wg": wg, "sgla": np.ascontiguousarray(A_(state_gla)[0, 16 * c:16 * c + 16].reshape(16, 4, 2, 128, 512)),
            "sconv": np.ascontiguousarray(A_(state_conv)[0, 16 * c:16 * c + 16].reshape(32, DCV)),
            "sffn": np.ascontiguousarray(A_(state_ffn_conv)[0, 16 * c:16 * c + 16].reshape(32, DFF)),
            "cst": cst, "gfin": gfin, "ident": ident, "maskp": maskp, "masks": masks,
            "rmo": rmo, "rmp": rmp, "seqmb": seqmb, "seqmt": seqmt,
        })
    if _NC_CACHE.get("prep_only"):
        return in_maps
    if "nc" not in _NC_CACHE:
        _NC_CACHE["nc"] = build_nc()
    res = run_bass_kernel_spmd(_NC_CACHE["nc"], in_maps, core_ids=list(range(8))).results

    y_prompt = np.empty((4, 2048, D), f32)
    y_sample = np.empty((128, 8, D), f32)
    conv_p = np.empty((1, 4, 2, DCV), f32)
    gla_p = np.empty((1, 4, 4, 256, 512), f32)
    ffn_p = np.empty((1, 4, 2, DFF), f32)
    conv_s = np.empty((1, 128, 2, DCV), f32)
    gla_s = np.empty((1, 128, 4, 256, 512), f32)
    ffn_s = np.empty((1, 128, 2, DFF), f32)
    for c in range(8):
        r = res[c]
        b, half = c // 2, c % 2
        if half == 0:
            y_prompt[b, 0:1016] = r["y"][20:NPO]
        else:
            y_prompt[b, 1016:2048] = r["y"][4:NPO]
            conv_p[0, b] = r["o_convp"]
            ffn_p[0, b] = r["o_ffnp"]
            gla_p[0, b] = r["o_glap"].reshape(4, 256, 512)
        y_sample[16 * c:16 * c + 16] = r["y"][NPO:T].reshape(16, 8, D)
        conv_s[0, 16 * c:16 * c + 16] = r["o_convs"].reshape(16, 2, DCV)
        ffn_s[0, 16 * c:16 * c + 16] = r["o_ffns"].reshape(16, 2, DFF)
        gla_s[0, 16 * c:16 * c + 16] = r["o_glas"].reshape(16, 4, 256, 512)
    return (y_prompt, y_sample, conv_p, gla_p, ffn_p, conv_s, gla_s, ffn_s)
```

```python
import contextlib
import numpy as np
import concourse.bass as bass
import concourse.mybir as mybir
from concourse.bass_utils import run_bass_kernel_spmd

F32 = mybir.dt.float32
BF16 = mybir.dt.bfloat16
AF = mybir.ActivationFunctionType
ALU = mybir.AluOpType
AX = mybir.AxisListType

D = 2048
KC = 16
DFF = 5632
FC = 44
DCV = 1024
NPO = 1036
NS = 128
T = NPO + NS
TP = 1028
EPS = 1e-6
NT_OWN = [(0, 388), (388, 388), (776, 388)]
NT_PRE = [(0, 343), (343, 343), (686, 342)]
CH_OWN = [(i * 128, 128) for i in range(8)] + [(1024, 12)]
CH_PRE = [(i * 128, 128) for i in range(8)] + [(1024, 4)]
SAMPLE = (NPO, NS)
TILES_OWN = CH_OWN + [SAMPLE]
EXT = 2 + NPO + 160
NSLOT = 2
SLOT = 4096
NCOL = 34

O_CB, O_CC, O_CH, O_Q, O_K, O_V, O_G, O_ALR, O_GA, O_GB = 0, 1024, 2048, 3072, 4096, 5120, 7168, 9216, 9232, 11280

C_GMIX, C_GFFN, C_GGLA, C_CW, C_FW, C_FB = 0, 16, 32, 36, 60, 192
NCST = 236


NOSYNC_SAME = ("pe",)


class _Op:
    __slots__ = ("eng", "fn", "deps", "sig", "sigcnt", "lane", "lanecnt", "is_dma", "ph")

    def __init__(self, eng, fn, is_dma=False, lane=None):
        self.eng = eng
        self.fn = fn
        self.deps = []
        self.sig = False
        self.sigcnt = 0
        self.lane = lane
        self.lanecnt = 0
        self.is_dma = is_dma


class Prog:
    ENGS = ("pe", "act", "dve", "pool", "sp")
    COMPUTE = ("pe", "act", "dve", "pool")

    def __init__(self, nc):
        self.nc = nc
        self.ops = {e: [] for e in self.ENGS}
        self.last_w = {}
        self.readers = {}
        self.lane_cnt = {}
        self.lane_last = {}
        self.phase = "init"
        self.scopes = False

    def _track(self, o, reads, writes):
        o.ph = self.phase
        deps = {}

        def add(d):
            if d is not None and d is not o:
                deps[id(d)] = d
        for r in reads:
            add(self.last_w.get(r))
        for w in writes:
            add(self.last_w.get(w))
            for rd in self.readers.get(w, {}).values():
                add(rd)
        o.deps = list(deps.values())
        for r in reads:
            key = ("dma", id(o)) if o.is_dma else o.eng
            self.readers.setdefault(r, {})[key] = o
        for w in writes:
            self.last_w[w] = o
            self.readers[w] = {}

    def op(self, eng, fn, reads=(), writes=()):
        o = _Op(eng, fn)
        self._track(o, reads, writes)
        self.ops[eng].append(o)
        return o

    def dma(self, eng, fn, lane, reads=(), writes=()):
        o = _Op(eng, fn, is_dma=True, lane=lane)
        self.lane_cnt[lane] = self.lane_cnt.get(lane, 0) + 1
        o.lanecnt = self.lane_cnt[lane]
        self.lane_last[lane] = o
        self._track(o, reads, writes)
        self.ops[eng].append(o)
        return o

    def barrier(self):
        lasts = []
        for e in self.ENGS:
            for o in reversed(self.ops[e]):
                if not o.is_dma and o.fn is not None:
                    lasts.append(o)
                    break
        dmas = list(self.lane_last.values())
        for e in self.ENGS:
            o = _Op(e, None)
            o.ph = self.phase
            o.deps = lasts + dmas
            self.ops[e].append(o)
        self.last_w = {}
        self.readers = {}

    def emit(self):
        nc = self.nc
        for e in self.ENGS:
            for o in self.ops[e]:
                for d in o.deps:
                    if d.is_dma:
                        continue
                    if d.eng == o.eng and not o.is_dma and d.eng in NOSYNC_SAME:
                        continue
                    d.sig = True
        for e in self.ENGS:
            c = 0
            for o in self.ops[e]:
                if o.sig:
                    c += 1
                    o.sigcnt = c
        lanes = sorted(self.lane_cnt.keys(), key=str)
        with contextlib.ExitStack() as st:
            sem_e = {e: st.enter_context(nc.semaphore("s_" + e)) for e in self.COMPUTE}
            sem_l = {l: st.enter_context(nc.semaphore("l%d" % i)) for i, l in enumerate(lanes)}
            block = st.enter_context(nc.Block())
            handles = {"pe": block.tensor, "act": block.scalar, "dve": block.vector,
                       "pool": block.gpsimd, "sp": block.sync}

            def make(e):
                ops = self.ops[e]

                def body(eng):
                    known = {}
                    cur = [None, None]
                    for o in ops:
                        if self.scopes and o.ph != cur[0]:
                            if cur[0] is not None:
                                nc.leave_named_scope(cur[0], cur[1], False)
                            cur[0] = o.ph
                            cur[1] = nc.enter_named_scope(o.ph, False)[0]
                        need = {}
                        for d in o.deps:
                            if d.is_dma:
                                k, v = ("l", d.lane), 16 * d.lanecnt
                            else:
                                if d.eng == e and not o.is_dma and e in NOSYNC_SAME:
                                    continue
                                if not d.sig:
                                    continue
                                k, v = ("e", d.eng), d.sigcnt
                            if need.get(k, 0) < v:
                                need[k] = v
                        for k, v in need.items():
                            if known.get(k, 0) >= v:
                                continue
                            known[k] = v
                            eng.wait_ge(sem_l[k[1]] if k[0] == "l" else sem_e[k[1]], v)
                        if o.fn is None:
                            continue
                        ins = o.fn(eng)
                        if o.is_dma:
                            ins.then_inc(sem_l[o.lane], 16)
                        elif o.sig:
                            ins.then_inc(sem_e[e], 1)
                    if e == "sp":
                        for l in lanes:
                            eng.wait_ge(sem_l[l], 16 * self.lane_cnt[l])
                    if self.scopes and cur[0] is not None:
                        nc.leave_named_scope(cur[0], cur[1], False)
                return body

            for e in self.ENGS:
                handles[e](make(e))


def weight_plan():
    pl = []
    pl.append(("w_in", 0, 16, O_ALR, 16))
    for h in range(4):
        pl.append(("w_in", 0, 16, O_V + 512 * h, 256))
        pl.append(("w_in", 0, 16, O_V + 512 * h + 256, 256))
        pl.append(("w_in", 0, 16, O_K + 256 * h, 256))
    pl.append(("w_in", 0, 16, O_ALR, 16))
    for h in range(4):
        pl.append(("w_in", 0, 16, O_V + 512 * h, 256))
        pl.append(("w_in", 0, 16, O_V + 512 * h + 256, 256))
        pl.append(("w_in", 0, 16, O_K + 256 * h, 256))
        pl.append(("w_in", 0, 16, O_Q + 256 * h, 256))
        pl.append(("w_in", 0, 16, O_G + 512 * h, 256))
        pl.append(("w_in", 0, 16, O_G + 512 * h + 256, 256))
    for c2 in range(8):
        pl.append(("w_in", 0, 16, O_GB + 256 * c2, 256))
        pl.append(("w_gla_out", 0, 16, 256 * c2, 256))
    for c2 in range(4):
        pl.append(("w_in", 0, 16, O_CC + 256 * c2, 256))
        pl.append(("w_in", 0, 16, O_CH + 256 * c2, 256))
        pl.append(("w_in", 0, 16, O_CB + 256 * c2, 256))
    for c4 in range(4):
        pl.append(("w_in", 0, 16, O_GA + 512 * c4, 256))
        pl.append(("w_in", 0, 16, O_GA + 512 * c4 + 256, 256))
        pl.append(("w_conv_out", 0, 8, 512 * c4, 512))
    for c2 in range(22):
        pl.append(("w_ffn_up", 0, 16, 256 * c2, 256))
        pl.append(("w_ffn_up", 0, 16, DFF + 256 * c2, 256))
    for c in range(16):
        pl.append(("w_ffn_down", 0, 22, 128 * c, 128))
        pl.append(("w_ffn_down", 22, 22, 128 * c, 128))
    return pl


def plan_offsets(pl):
    offs, o = [], 0
    for (_, _, kcn, _, ncol) in pl:
        offs.append(o)
        o += 128 * kcn * ncol
    return offs, o


def build_nc(scopes=False):
    nc = bass.Bass("TRN2", target_bir_lowering=False)
    pl = weight_plan()
    offs, nw = plan_offsets(pl)

    def din(name, shape):
        return nc.dram_tensor(name, shape, F32, kind="ExternalInput").ap()

    def dout(name, shape):
        return nc.dram_tensor(name, shape, F32, kind="ExternalOutput").ap()

    xo = din("xo", [T, D])
    xp = din("xp", [TP, D])
    wflat = din("wflat", [nw])
    wo_r = din("wo_r", [4, 128, 16 * 512])
    wg_d = din("wg", [17, 1024])
    sgla = din("sgla", [16, 4, 2, 128, 512])
    sconv = din("sconv", [32, DCV])
    sffn = din("sffn", [32, DFF])
    cst_d = din("cst", [128, NCST])
    gfin_d = din("gfin", [128, D])
    ident_d = din("ident", [128, 128])
    maskp_d = din("maskp", [128, 128])
    masks_d = din("masks", [128, 128])
    rmo_d = din("rmo", [128, T])
    seqmb_d = din("seqmb", [128, 16 * 128])
    seqmt_d = din("seqmt", [128, 16])
    rmp_d = din("rmp", [128, TP])

    y = dout("y", [T, D])
    o_convp = dout("o_convp", [2, DCV])
    o_ffnp = dout("o_ffnp", [2, DFF])
    o_glap = dout("o_glap", [4, 2, 128, 512])
    o_convs = dout("o_convs", [32, DCV])
    o_ffns = dout("o_ffns", [32, DFF])
    o_glas = dout("o_glas", [16, 4, 2, 128, 512])
    h_scr = nc.dram_tensor("h_scr", [T, D], F32).ap()
    ff_scr = nc.dram_tensor("ff_scr", [T, D], F32).ap()

    P = Prog(nc)
    P.scopes = scopes
    es = contextlib.ExitStack()

    def sb(name, shape, dt=F32):
        return es.enter_context(nc.sbuf_tensor("sb_sb_" + name, shape, dt))

    def mm(out, lhsT, rhs, start, stop, r, w):
        P.op("pe", lambda e: e.matmul(out, lhsT=lhsT, rhs=rhs, start=start, stop=stop), reads=r, writes=w)

    def tr(out, in_, ident, r, w):
        P.op("pe", lambda e: e.transpose(out, in_, ident), reads=r, writes=w)

    def act(out, in_, func, r, w, bias=None, scale=None, accum=None):
        kw = {}
        if bias is not None:
            kw["bias"] = bias
        if scale is not None:
            kw["scale"] = scale
        if accum is not None:
            kw["accum_out"] = accum
        P.op("act", lambda e: e.activation(out=out, in_=in_, func=func, **kw), reads=r, writes=w)

    def tt(out, in0, in1, op, r, w, eng="dve"):
        P.op(eng, lambda e: e.tensor_tensor(out=out, in0=in0, in1=in1, op=op), reads=r, writes=w)

    def ts(out, in0, s1, op0, r, w, s2=None, op1=None, eng="dve"):
        if op1 is None:
            P.op(eng, lambda e: e.tensor_scalar(out=out, in0=in0, scalar1=s1, scalar2=None, op0=op0), reads=r, writes=w)
        else:
            P.op(eng, lambda e: e.tensor_scalar(out=out, in0=in0, scalar1=s1, scalar2=s2, op0=op0, op1=op1),
                 reads=r, writes=w)

    def stt(out, in0, scalar, in1, op0, op1, r, w):
        P.op("dve", lambda e: e.scalar_tensor_tensor(out=out, in0=in0, scalar=scalar, in1=in1, op0=op0, op1=op1),
             reads=r, writes=w)

    def cp(out, in_, r, w, eng="dve"):
        P.op(eng, lambda e: e.tensor_copy(out=out, in_=in_), reads=r, writes=w)

    def memset(ap, val, w, eng="pool"):
        P.op(eng, lambda e: e.memset(ap, val), writes=w)

    ulane = {"n": 0}

    def dma(out, in_, lane, r=(), w=(), eng="sp"):
        if lane is None:
            ulane["n"] += 1
            lane = ("u", ulane["n"])
        P.dma(eng, lambda e: e.dma_start(out=out, in_=in_), lane=lane, reads=r, writes=w)

    ps = es.enter_context(nc.psum_tensor("ps", [128, 8, 512], F32))
    psb = ps[:].bitcast(BF16)
    ring = sb("ring", [128, NSLOT, SLOT], BF16)
    cst = sb("cst", [128, NCST])
    identf = sb("identf", [128, 128])
    identb = sb("identb", [128, 128], BF16)
    maskp = sb("maskp", [128, 128])
    masks = sb("masks", [128, 128])
    ucol = sb("ucol", [128, 8, NCOL])
    sprev_u = sb("sprev_u", [128, 8, 32])

    seqmt = sb("seqmt", [128, 16])
    PSB = lambda b: ("ps", b)

    arena_start = (nc.sbuf_base + 63) // 64 * 64
    AR = nc.sbuf_top - arena_start - 128
    es.enter_context(nc.sbuf_tensor("sb_fence", [128, AR // 4], F32))
    cnt = {"n": 0}

    def AT(name, shape, dt, off):
        nbytes = int(np.prod(shape[1:])) * (4 if dt == F32 else 2)
        assert off % 32 == 0 and off + nbytes <= AR, (name, off, nbytes, AR)
        cnt["n"] += 1
        return nc.alloc_sbuf_tensor_at("a%d_%s" % (cnt["n"], name), shape, dt, offset=arena_start + off)

    class Bump:
        def __init__(self, off, limit=None):
            self.off = off
            self.limit = limit

        def __call__(self, name, shape, dt=F32):
            nbytes = int(np.prod(shape[1:])) * (4 if dt == F32 else 2)
            o = (self.off + 63) // 64 * 64
            self.off = o + nbytes
            if self.limit is not None:
                assert self.off <= self.limit, (name, self.off, self.limit)
            return AT(name, shape, dt, o)

    BIGB = 16 * T * 2
    R0, R1, R2 = 0, BIGB, 2 * BIGB
    O_SST = R2
    O_WG = O_SST + 16384
    O_P23 = O_WG + 4096
    Sst = AT("Sst", [128, 4, 2, 512], F32, O_SST)
    wg = AT("wg", [32, 1024], F32, O_WG)

    dma(cst[:], cst_d, None, w=["cst"])
    dma(identf[:], ident_d, None, w=["identf"])
    dma(maskp[:], maskp_d, None, w=["maskp"])
    dma(masks[:], masks_d, None, w=["masks"])
    dma(wg[0:17, :], wg_d, None, w=["wg"])
    cp(identb[:], identf[:], ["identf"], ["identb"])
    memset(Sst[:], 0.0, [("S", h_, d_) for h_ in range(4) for d_ in range(2)])
    dma(seqmt[:], seqmt_d, None, w=["seqmt"])

    wstate = {"next_dma": 0, "next_use": 0, "released": 0}

    def issue_block_dma():
        i = wstate["next_dma"]
        if i >= len(pl):
            return
        assert i - NSLOT < wstate["released"], "weight ring overrun"
        wstate["next_dma"] += 1
        (_, _, kcn, _, ncol) = pl[i]
        n = kcn * ncol
        s = i % NSLOT
        src = wflat[offs[i]:offs[i] + 128 * n].rearrange("(p n) -> p n", p=128)
        for c0 in range(0, n, 2048):
            c1 = min(n, c0 + 2048)
            dma(ring[:, s, c0:c1], src[:, c0:c1], ("w", s), w=[("ring", s)], eng="pool")

    def next_block():
        i = wstate["next_use"]
        wstate["next_use"] += 1
        while wstate["next_dma"] <= i:
            issue_block_dma()
        (_, _, kcn, _, ncol) = pl[i]
        s = i % NSLOT
        v = ring[:, s, 0:kcn * ncol].rearrange("p (k n) -> p k n", k=kcn)
        return v, ("ring", s)

    def prefetch():
        wstate["released"] = wstate["next_use"]
        while wstate["next_dma"] < min(len(pl), wstate["next_use"] + NSLOT):
            issue_block_dma()

    for _ in range(NSLOT):
        issue_block_dma()

    def mm_a(blk, rk, j, M, kcn, actT, act_key, ntiles, banks, first=True, last=True, kc_off=0):
        for kc in range(kcn):
            for i, (t0, tn) in enumerate(ntiles):
                mm(ps[0:M, banks[i], 0:tn], blk[:, kc, j * M:(j + 1) * M], actT[:, kc_off + kc, t0:t0 + tn],
                   start=(first and kc == 0), stop=(last and kc == kcn - 1),
                   r=[rk, act_key], w=[PSB(banks[i])])

    bank_set = {"i": 0}

    def next_banks():
        b = bank_set["i"]
        bank_set["i"] ^= 1
        return [3 * b, 3 * b + 1, 3 * b + 2]

    def norm_tiles(*a, **kw):
        for _ in norm_tiles_gen(*a, **kw):
            pass

    def norm_tiles_gen(src, tiles, dstT, dst_key, gcol0, tag, bump, add_fn=None, store_h=None):
        xt = [bump("xt", [128, D], F32) for i in range(2)]
        xn = [bump("xn", [128, D], BF16) for i in range(2)]
        sq = bump("sq", [128, D], BF16)
        st = bump("st", [128, 2, 4], F32)

        def load(ti):
            if ti < len(tiles):
                t0, n = tiles[ti]
                dma(xt[ti % 2][0:n, :], src[t0:t0 + n, :], ("x", ti % 2), w=[("xt", tag, ti % 2)])

        def stage1(ti):
            t0, n = tiles[ti]
            b = ti % 2
            kx, kn = ("xt", tag, b), ("xn", tag, b)
            if add_fn is not None:
                add_fn(ti, t0, n, xt[b], kx)
            if store_h is not None:
                dma(store_h[t0:t0 + n, :], xt[b][0:n, :], ("hs", b), r=[kx])
            act(sq[0:n, :], xt[b][0:n, :], AF.Square, [kx], ["sq" + tag, ("st", tag, b)], accum=st[0:n, b, 0:1])
            act(st[0:n, b, 1:2], st[0:n, b, 0:1], AF.Ln, [("st", tag, b)], [("st1", tag, b)], scale=1.0 / D, bias=EPS)
            act(st[0:n, b, 2:3], st[0:n, b, 1:2], AF.Exp, [("st1", tag, b)], [("st2", tag, b)], scale=-0.5)
            ts(xn[b][0:n, :], xt[b][0:n, :], st[0:n, b, 2:3], ALU.mult, [kx, ("st2", tag, b)], [kn])

        def stage2(ti):
            t0, n = tiles[ti]
            b = ti % 2
            kn = ("xn", tag, b)
            for half in range(2):
                bank = 6 + half
                for k8 in range(8):
                    kc = half * 8 + k8
                    tr(psb[:, bank, k8 * 128:k8 * 128 + n], xn[b][0:n, kc * 128:(kc + 1) * 128],
                       identb[0:n, 0:n], [kn, "identb"], [PSB(bank)])
                gview = cst[:, gcol0 + half * 8:gcol0 + half * 8 + 8].unsqueeze(2).to_broadcast([128, 8, n])
                pview = psb[:, bank, 0:1024].rearrange("p (k t) -> p k t", k=8)[:, :, 0:n]
                tt(dstT[:, half * 8:half * 8 + 8, t0:t0 + n], pview, gview, ALU.mult,
                   [PSB(bank), "cst"], [dst_key])

        load(0)
        load(1)
        for ti in range(len(tiles)):
            stage1(ti)
            if ti >= 1:
                stage2(ti - 1)
            load(ti + 2)
            yield
        stage2(len(tiles) - 1)
        yield

    P.phase = "P1a"
    if True:
        npT = AT("npT", [128, KC, TP], BF16, R1)
        norm_tiles(xp, CH_PRE, npT, "npT", C_GMIX, "p", Bump(O_P23))
        P.barrier()

        P.phase = "P2"
        if True:
            b2 = Bump(O_P23)
            alr = b2("alrp", [32, TP], F32)
            rm = b2("rmp", [128, TP], BF16)
            A = b2("Ap", [128, 2, TP], F32)
            B = b2("Bp", [128, 2, TP], F32)
            KD = b2("KDp", [128, 2, TP], BF16)
            vt = b2("vtp", [128, 9, 512], BF16)
            kdta = b2("kdtp", [128, 9, 256], BF16)
            vTs = b2("vTsp", [128, 4, TP], BF16)
            nT = AT("nT", [128, KC, T], BF16, R0)
            b2b = Bump(b2.off)
            g1b = norm_tiles_gen(xo, TILES_OWN, nT, "nT", C_GMIX, "o", b2b)
            dma(rm[:], rmp_d, None, w=["rm"], eng="pool")
            memset(alr[:], 1.0, ["alr"])
            blk, rk = next_block()
            mm_a(blk, rk, 0, 16, 16, npT, "npT", NT_PRE, [0, 1, 2])
            prefetch()
            for i, (t0, tn) in enumerate(NT_PRE):
                cp(alr[0:16, t0:t0 + tn], ps[0:16, i, 0:tn], [PSB(i)], ["alr"])
            def logit_p(h, A=A, alr=alr):
                for dk in range(2):
                    banks = next_banks()
                    for i, (t0, tn) in enumerate(NT_PRE):
                        mm(ps[:, banks[i], 0:tn], wg[0:17, h * 256 + dk * 128:h * 256 + (dk + 1) * 128],
                           alr[0:17, t0:t0 + tn], True, True, ["wg", "alr"], [PSB(banks[i])])
                        act(A[:, dk, t0:t0 + tn], ps[:, banks[i], 0:tn], AF.Exp, [PSB(banks[i])], [("A", dk)], scale=-1.0)
            logit_p(0)
            for h in range(4):

                def batch(dk, A=A, B=B, rm=rm):
                    act(A[:, dk, :], A[:, dk, :], AF.Ln, [("A", dk)], [("A", dk)], bias=1.0)
                    P.op("dve", lambda e: e.tensor_tensor_scan(
                        out=B[:, dk, :], data0=rm[:], data1=A[:, dk, :], initial=0.0, op0=ALU.mult, op1=ALU.add),
                        reads=[("A", dk), "rm"], writes=[("B", dk)])
                    full = B[:, dk, 0:1024].rearrange("p (c t) -> p c t", c=8)
                    tt(A[:, dk, 0:1024].rearrange("p (c t) -> p c t", c=8), full,
                       full[:, :, 127:128].to_broadcast([128, 8, 128]), ALU.subtract, [("B", dk)], [("A", dk)])
                    tt(A[:, dk, 1024:TP], B[:, dk, 1024:TP], B[:, dk, TP - 1:TP].to_broadcast([128, 4]),
                       ALU.subtract, [("B", dk)], [("A", dk)])
                    act(A[:, dk, :], A[:, dk, :], AF.Exp, [("A", dk)], [("A", dk)], scale=1.0 / 16)
                    ends = full[:, :, 127:128]
                    act(ends, ends, AF.Exp, [("B", dk)], [("B", dk)], scale=-1.0 / 16)
                    act(B[:, dk, TP - 1:TP], B[:, dk, TP - 1:TP], AF.Exp, [("B", dk)], [("B", dk)], scale=-1.0 / 16)

                for g2 in range(2):
                    vblk, vk = next_block()
                    for j in range(2):
                        c = 2 * g2 + j
                        banks = next_banks()
                        mm_a(vblk, vk, j, 128, 16, npT, "npT", NT_PRE, banks)
                        for i, (t0, tn) in enumerate(NT_PRE):
                            act(vTs[:, c, t0:t0 + tn], ps[:, banks[i], 0:tn], AF.Copy, [PSB(banks[i])], ["vTs"])
                        if c == 0:
                            batch(0)
                        if c == 1:
                            batch(1)
                    prefetch()
                for ci, (t0, n) in enumerate(CH_PRE):
                    bank = 6 + (ci % 2)
                    for c in range(4):
                        tr(psb[0:n, bank, c * 128:(c + 1) * 128], vTs[:, c, t0:t0 + n], identb[:, :],
                           ["vTs", "identb"], [PSB(bank)])
                    act(vt[0:n, ci, :], psb[0:n, bank, 0:512], AF.Copy, [PSB(bank)], [("vt", ci)])
                for _ in range(3 if h < 3 else 2):
                    next(g1b, None)
                blk, rk = next_block()
                for dk in range(2):
                    banks = next_banks()
                    mm_a(blk, rk, dk, 128, 16, npT, "npT", NT_PRE, banks)
                    for i, (t0, tn) in enumerate(NT_PRE):
                        tt(KD[:, dk, t0:t0 + tn], ps[:, banks[i], 0:tn], A[:, dk, t0:t0 + tn], ALU.mult,
                           [PSB(banks[i]), ("A", dk)], [("KD", dk)])
                prefetch()
                if h + 1 < 4:
                    logit_p(h + 1)
                for ci, (t0, n) in enumerate(CH_PRE):
                    bank = 6 + (ci // 4) % 2
                    q4 = ci % 4
                    for dk in range(2):
                        tr(psb[0:n, bank, q4 * 256 + dk * 128:q4 * 256 + (dk + 1) * 128], KD[:, dk, t0:t0 + n], identb[:, :],
                           [("KD", dk), "identb"], [PSB(bank)])
                    act(kdta[0:n, ci, :], psb[0:n, bank, q4 * 256:(q4 + 1) * 256], AF.Copy, [PSB(bank)], [("kdt", ci)])
                for ci, (t0, n) in enumerate(CH_PRE):
                    b = ci % 2
                    for dk in range(2):
                        pb = 2 * b + dk
                        mm(ps[:, pb, :], kdta[0:n, ci, dk * 128:(dk + 1) * 128], vt[0:n, ci, :], True, True,
                           [("kdt", ci), ("vt", ci)], [PSB(pb)])
                        stt(Sst[:, h, dk, :], Sst[:, h, dk, :], B[:, dk, t0 + n - 1:t0 + n], ps[:, pb, :],
                            ALU.mult, ALU.add, [("S", h, dk), ("B", dk), PSB(pb)], [("S", h, dk)])
            for _ in g1b:
                pass
            stl = b2b("stl", [32, DCV], F32)
            dma(stl[:, :], sconv, None, w=["stl"])
            for c in range(8):
                tr(ps[:, 0, c * 32:(c + 1) * 32], stl[0:32, c * 128:(c + 1) * 128], identf[0:32, 0:32],
                   ["stl", "identf"], [PSB(0)])
            cp(sprev_u[:], ps[:, 0, 0:256].rearrange("p (c s) -> p c s", c=8), [PSB(0)], ["sprev_u"])
            P.barrier()

    P.phase = "P3"
    ogT = AT("ogT", [128, KC, T], BF16, R1)
    if True:
        t3 = Bump(O_P23)
        alr = t3("alro", [32, T])
        rm = t3("rmo", [128, T], BF16)
        seqmb = t3("seqmb", [128, 16, 128], BF16)
        o_A = (t3.off + 63) // 64 * 64
        A = t3("Ao", [128, 2, T])
        sgh = AT("sgh", [128, 4, T], BF16, o_A)
        B = t3("Bo", [128, 2, T])
        QE = t3("QE", [128, 2, T], BF16)
        o_KE = (t3.off + 63) // 64 * 64
        KE = t3("KE", [128, 2, T], BF16)
        vTs = AT("vTs", [128, 4, T], BF16, o_KE)
        KD = t3("KD", [128, 2, T], BF16)
        vt = t3("vto", [128, 10, 512], BF16)
        kdta = t3("kdta", [128, 10, 256], BF16)
        attma = t3("attma", [128, 10, 128], BF16)
        Sbf = [t3("Sbf%d" % i, [128, 2, 512], BF16) for i in range(2)]
        og = [t3("og%d" % i, [128, 512], BF16) for i in range(2)]
        ost = t3("ost", [128, 2, 4])
        S0 = [t3("S0_%d" % i, [128, 2, 512]) for i in range(4)]
        S0b = [t3("S0b_%d" % i, [128, 2, 512], BF16) for i in range(4)]
        QXs = [t3("QXs%d" % i, [128, 2, 128], BF16) for i in range(2)]
        KXs = [t3("KXs%d" % i, [128, 256], BF16) for i in range(2)]
        dma(rm[:], rmo_d, None, w=["rm"], eng="pool")
        dma(seqmb[:].rearrange("p s t -> p (s t)"), seqmb_d, None, w=["seqmb"], eng="pool")
        memset(alr[:], 1.0, ["alr"])
        blk, rk = next_block()
        mm_a(blk, rk, 0, 16, 16, nT, "nT", NT_OWN, [0, 1, 2])
        prefetch()
        for i, (t0, tn) in enumerate(NT_OWN):
            cp(alr[0:16, t0:t0 + tn], ps[0:16, i, 0:tn], [PSB(i)], ["alr"])

        def chunk_views(buf, dk):
            return (buf[:, dk, 0:1024].rearrange("p (c t) -> p c t", c=8), buf[:, dk, 1024:NPO],
                    buf[:, dk, NPO:T].rearrange("p (s t) -> p s t", s=16))
        AK = [("A", 0), ("A", 1)]

        for h in range(4):
            for dk in range(2):
                banks = next_banks()
                for i, (t0, tn) in enumerate(NT_OWN):
                    mm(ps[:, banks[i], 0:tn], wg[0:17, h * 256 + dk * 128:h * 256 + (dk + 1) * 128],
                       alr[0:17, t0:t0 + tn], True, True, ["wg", "alr"], [PSB(banks[i])])
                    act(A[:, dk, t0:t0 + tn], ps[:, banks[i], 0:tn], AF.Exp, [PSB(banks[i])], [("A", dk)], scale=-1.0)

            def batch(dk, A=A, B=B, rm=rm):
                act(A[:, dk, :], A[:, dk, :], AF.Ln, [("A", dk)], [("A", dk)], bias=1.0)
                P.op("dve", lambda e: e.tensor_tensor_scan(
                    out=B[:, dk, :], data0=rm[:], data1=A[:, dk, :], initial=0.0, op0=ALU.mult, op1=ALU.add),
                    reads=[("A", dk), "rm"], writes=[("B", dk)])
                act(A[:, dk, :], B[:, dk, :], AF.Exp, [("B", dk)], [("A", dk)], scale=1.0 / 16)

            KK = [("KE", 0), ("KE", 1), ("KD", 0), ("KD", 1)]
            for g2 in range(2):
                vblk, vk = next_block()
                for j in range(2):
                    c = 2 * g2 + j
                    banks = next_banks()
                    mm_a(vblk, vk, j, 128, 16, nT, "nT", NT_OWN, banks)
                    for i, (t0, tn) in enumerate(NT_OWN):
                        act(vTs[:, c, t0:t0 + tn], ps[:, banks[i], 0:tn], AF.Copy, [PSB(banks[i])] + KK, KK)
                    if c == 0:
                        batch(0)
                    if c == 1:
                        batch(1)
                prefetch()
            for ci, (t0, n) in enumerate(TILES_OWN):
                bank = 6 + (ci % 2)
                for c in range(4):
                    tr(psb[0:n, bank, c * 128:(c + 1) * 128], vTs[:, c, t0:t0 + n], identb[:, :],
                       KK + ["identb"], [PSB(bank)])
                act(vt[0:n, ci, :], psb[0:n, bank, 0:512], AF.Copy, [PSB(bank)], [("vt", ci)])
            blk, rk = next_block()
            for dk in range(2):
                banks = next_banks()
                mm_a(blk, rk, dk, 128, 16, nT, "nT", NT_OWN, banks)
                for i, (t0, tn) in enumerate(NT_OWN):
                    tt(KE[:, dk, t0:t0 + tn], ps[:, banks[i], 0:tn], A[:, dk, t0:t0 + tn], ALU.mult,
                       [PSB(banks[i]), ("A", dk)] + KK, [("KE", dk)])
                fb, sb_, smb = chunk_views(B, dk)
                fa, sa, sma = chunk_views(A, dk)
                tt(fa, fb, fb[:, :, 127:128].to_broadcast([128, 8, 128]), ALU.subtract, [("B", dk)], [("A", dk)])
                tt(sa, sb_, B[:, dk, NPO - 1:NPO].to_broadcast([128, 12]), ALU.subtract, [("B", dk)], [("A", dk)])
                tt(sma, smb, smb[:, :, 7:8].to_broadcast([128, 16, 8]), ALU.subtract, [("B", dk)], [("A", dk)])
                act(A[:, dk, :], A[:, dk, :], AF.Exp, [("A", dk)], [("A", dk)], scale=1.0 / 16)
                for i, (t0, tn) in enumerate(NT_OWN):
                    tt(KD[:, dk, t0:t0 + tn], ps[:, banks[i], 0:tn], A[:, dk, t0:t0 + tn], ALU.mult,
                       [PSB(banks[i]), ("A", dk)] + KK, [("KD", dk)])
                act(A[:, dk, :], B[:, dk, :], AF.Exp, [("B", dk), ("KD", dk)], [("A", dk)], scale=-1.0 / 16)
            prefetch()
            for ci, (t0, n) in enumerate(TILES_OWN):
                bank = 6 + (ci // 4) % 2
                q4 = ci % 4
                for dk in range(2):
                    tr(psb[0:n, bank, q4 * 256 + dk * 128:q4 * 256 + (dk + 1) * 128], KD[:, dk, t0:t0 + n], identb[:, :],
                       [("KD", dk), "identb"], [PSB(bank)])
                act(kdta[0:n, ci, :], psb[0:n, bank, q4 * 256:(q4 + 1) * 256], AF.Copy, [PSB(bank)], [("kdt", ci)])
            blk, rk = next_block()
            for dk in range(2):
                banks = next_banks()
                mm_a(blk, rk, dk, 128, 16, nT, "nT", NT_OWN, banks)
                for i, (t0, tn) in enumerate(NT_OWN):
                    stt(QE[:, dk, t0:t0 + tn], ps[:, banks[i], 0:tn], 1.0 / 16, A[:, dk, t0:t0 + tn], ALU.mult, ALU.mult,
                        [PSB(banks[i]), ("A", dk)], [("QE", dk)])
                fb, sb_, smb = chunk_views(B, dk)
                act(fb[:, :, 127:128], fb[:, :, 127:128], AF.Exp, [("B", dk), ("A", dk)], [("B", dk)], scale=-1.0 / 16)
                act(B[:, dk, NPO - 1:NPO], B[:, dk, NPO - 1:NPO], AF.Exp, [("B", dk)], [("B", dk)], scale=-1.0 / 16)
                act(smb[:, :, 7:8], smb[:, :, 7:8], AF.Exp, [("B", dk)], [("B", dk)], scale=-1.0 / 16)
            prefetch()
            for ci, (t0, n) in enumerate(TILES_OWN):
                bank = 6 + (ci // 4) % 2
                q4 = ci % 4
                for dk in range(2):
                    mm(ps[0:n, bank, q4 * 128:q4 * 128 + n], KE[:, dk, t0:t0 + n], QE[:, dk, t0:t0 + n], dk == 0, dk == 1,
                       [("KE", dk), ("QE", dk)], [PSB(bank)])
                msk = masks if ci == 9 else maskp
                tt(attma[0:n, ci, 0:n], ps[0:n, bank, q4 * 128:q4 * 128 + n], msk[0:n, 0:n], ALU.mult,
                   [PSB(bank), "maskp", "masks"], [("attm", ci)])
            for dk in range(2):
                act(Sbf[0][:, dk, :], Sst[:, h, dk, :], AF.Copy, [("S", h, dk)], [("Sbf", 0, dk)])

            def ep_a(ci, n, obank):
                b = ci % 2
                act(og[b][0:n, :], ps[0:n, obank, :], AF.Square, [PSB(obank)], [("og", b), ("ost", b)], accum=ost[0:n, b, 0:1])
                act(ost[0:n, b, 1:2], ost[0:n, b, 0:1], AF.Ln, [("ost", b)], [("ost1", b)], scale=1.0 / 512, bias=EPS)
                act(ost[0:n, b, 2:3], ost[0:n, b, 1:2], AF.Exp, [("ost1", b)], [("ost2", b)], scale=-0.5)
                ts(og[b][0:n, :], ps[0:n, obank, :], ost[0:n, b, 2:3], ALU.mult, [PSB(obank), ("ost2", b)], [("og", b)])

            def ep_b(ci, t0, n, h=h):
                b = ci % 2
                tb = 6 + b
                for c in range(4):
                    tr(psb[:, tb, 512 + c * 128:512 + c * 128 + n], og[b][0:n, c * 128:(c + 1) * 128], identb[0:n, 0:n],
                       [("og", b), "identb"], [PSB(tb)])
                pview = psb[:, tb, 512:1024].rearrange("p (c t) -> p c t", c=4)[:, :, 0:n]
                gview = cst[:, C_GGLA:C_GGLA + 4].unsqueeze(2).to_broadcast([128, 4, n])
                tt(ogT[:, 4 * h:4 * h + 4, t0:t0 + n], pview, gview, ALU.mult, [PSB(tb), "cst"], [("ogT", h)])

            def P_mm(ci):
                t0, n = CH_OWN[ci]
                for dk in range(2):
                    pb = 2 + 2 * (ci % 2) + dk
                    mm(ps[:, pb, :], kdta[0:n, ci, dk * 128:(dk + 1) * 128], vt[0:n, ci, :], True, True,
                       [("kdt", ci), ("vt", ci)], [PSB(pb)])

            def load_state(s, h=h):
                sbi = s % 4
                for dk in range(2):
                    dma(S0[sbi][:, dk, :], sgla[s, h, dk], ("s0", sbi, dk), w=[("S0", sbi, dk)])
            for s_ in range(3):
                load_state(s_)

            P_mm(0)
            for ci, (t0, n) in enumerate(CH_OWN):
                b = ci % 2
                sb_cur, sb_nxt = ci % 2, (ci + 1) % 2
                if ci + 1 < len(CH_OWN):
                    P_mm(ci + 1)
                ob = b
                mm(ps[0:n, ob, :], attma[0:n, ci, 0:n], vt[0:n, ci, :], True, False, [("attm", ci), ("vt", ci)], [PSB(ob)])
                for dk in range(2):
                    mm(ps[0:n, ob, :], QE[:, dk, t0:t0 + n], Sbf[sb_cur][:, dk, :], False, dk == 1,
                       [("QE", dk), ("Sbf", sb_cur, dk)], [PSB(ob)])
                for dk in range(2):
                    pb = 2 + 2 * b + dk
                    dl = B[:, dk, t0 + n - 1:t0 + n]
                    stt(Sbf[sb_nxt][:, dk, :], Sst[:, h, dk, :], dl, ps[:, pb, :], ALU.mult, ALU.add,
                        [("S", h, dk), ("B", dk), PSB(pb)], [("Sbf", sb_nxt, dk)])
                    stt(Sst[:, h, dk, :], Sst[:, h, dk, :], dl, ps[:, pb, :], ALU.mult, ALU.add,
                        [("S", h, dk), ("B", dk), PSB(pb)], [("S", h, dk)])
                ep_a(ci, n, ob)
                if ci >= 1:
                    ep_b(ci - 1, CH_OWN[ci - 1][0], CH_OWN[ci - 1][1])
            ep_b(len(CH_OWN) - 1, CH_OWN[-1][0], CH_OWN[-1][1])
            for dk in range(2):
                dma(o_glap[h, dk], Sst[:, h, dk, :], ("gp", dk), r=[("S", h, dk)])

            def g_stream():
                gbanks = [5, 6, 7]
                for g2 in range(2):
                    blk, rk = next_block()
                    for j in range(2):
                        c = 2 * g2 + j
                        cnt_ = 0
                        for kc in range(KC):
                            for i, (t0_, tn) in enumerate(NT_OWN):
                                mm(ps[:, gbanks[i], 0:tn], blk[:, kc, j * 128:(j + 1) * 128], nT[:, kc, t0_:t0_ + tn],
                                   start=(kc == 0), stop=(kc == KC - 1), r=[rk, "nT"], w=[PSB(gbanks[i])])
                                cnt_ += 1
                                if cnt_ % 12 == 0 and cnt_ < 48:
                                    yield
                        for i, (t0_, tn) in enumerate(NT_OWN):
                            act(sgh[:, c, t0_:t0_ + tn], ps[:, gbanks[i], 0:tn], AF.Silu, [PSB(gbanks[i])] + AK, AK)
                        yield
                    prefetch()

            gs = g_stream()
            t0, n = SAMPLE
            ci = 9
            ob = 0
            mm(ps[:, ob, :], attma[:, ci, :], vt[:, ci, :], True, False, [("attm", ci), ("vt", ci)], [PSB(ob)])

            def prep(s):
                sbi, xb = s % 4, s % 2
                tt(QXs[xb][:, :, :], QE[:, :, t0:t0 + n], seqmb[:, s, :].unsqueeze(1).to_broadcast([128, 2, 128]),
                   ALU.mult, [("QE", 0), ("QE", 1), "seqmb"], [("QXs", xb)])
                ts(KXs[xb][:, :], kdta[:, ci, :], seqmt[:, s:s + 1], ALU.mult, [("kdt", ci), "seqmt"], [("KXs", xb)])
                act(S0b[sbi][:, :, :], S0[sbi][:, :, :], AF.Copy, [("S0", sbi, 0), ("S0", sbi, 1)], [("S0b", sbi)])
            prep(0)
            for s in range(16):
                sbi = s % 4
                xb = s % 2
                for dk in range(2):
                    mm(ps[:, ob, :], QXs[xb][:, dk, :], S0b[sbi][:, dk, :], False, (s == 15 and dk == 1),
                       [("QXs", xb), ("S0b", sbi)], [PSB(ob)])
                for dk in range(2):
                    pb = 1 + 2 * xb + dk
                    mm(ps[:, pb, :], KXs[xb][:, dk * 128:(dk + 1) * 128], vt[:, ci, :], True, True,
                       [("KXs", xb), ("vt", ci)], [PSB(pb)])
                if s + 1 < 16:
                    prep(s + 1)
                for dk in range(2):
                    pb = 1 + 2 * xb + dk
                    stt(S0[sbi][:, dk, :], S0[sbi][:, dk, :], B[:, dk, t0 + 8 * s + 7:t0 + 8 * s + 8], ps[:, pb, :],
                        ALU.mult, ALU.add, [("S0", sbi, dk), ("B", dk), PSB(pb)], [("S0", sbi, dk)])
                if s + 3 < 16:
                    load_state(s + 3)
                for dk in range(2):
                    dma(o_glas[s, h, dk], S0[sbi][:, dk, :], ("so", sbi, dk), r=[("S0", sbi, dk)])
                next(gs, None)
            for _ in gs:
                pass
            ep_a(ci, n, ob)
            ep_b(ci, t0, n)
            for c in range(4):
                tt(ogT[:, 4 * h + c, :], ogT[:, 4 * h + c, :], sgh[:, c, :], ALU.mult, [("ogT", h)] + AK, [("ogT", h)])
        P.barrier()

    P.phase = "P4"
    merged = AT("merged", [128, KC, T], BF16, R2)
    O_WOB = (AR - 32768) // 64 * 64
    woB = AT("woB", [128, 2, 16 * 512], BF16, O_WOB)
    t5 = Bump(3 * BIGB, O_WOB)
    sgt = [t5("sgt", [128, T], BF16) for i in range(4)]
    for c2 in range(8):
        gb, gbk = next_block()
        for j in range(2):
            banks = next_banks()
            mm_a(gb, gbk, j, 128, 16, nT, "nT", NT_OWN, banks)
            for i, (t0, tn) in enumerate(NT_OWN):
                act(sgt[j][:, t0:t0 + tn], ps[:, banks[i], 0:tn], AF.Sigmoid, [PSB(banks[i])], [("sgt", j)])
        prefetch()
        go, gok = next_block()
        for j in range(2):
            c = 2 * c2 + j
            banks = next_banks()
            mm_a(go, gok, j, 128, 16, ogT, "ogTall", NT_OWN, banks)
            for i, (t0, tn) in enumerate(NT_OWN):
                tt(merged[:, c, t0:t0 + tn], ps[:, banks[i], 0:tn], sgt[j][:, t0:t0 + tn], ALU.mult,
                   [PSB(banks[i]), ("sgt", j)], [("merged", c)])
        prefetch()
        g_, q_ = 2 + c2 // 4, c2 % 4
        dma(woB[:, g_ - 2, q_ * 2048:(q_ + 1) * 2048], wo_r[g_][:, q_ * 2048:(q_ + 1) * 2048], ("wo", g_),
            w=[("wo", g_)], eng="pool")
    P.barrier()

    P.phase = "P5"
    def conv3(dst, src, wcol, key_src, key_dst, tmp, key_tmp):
        regs = [(src[:, 0:EXT - 160], dst[:, 0:NPO], tmp[:, 0:NPO], NPO, None),
                (src[:, EXT - 160:EXT].rearrange("p (s t) -> p s t", s=16),
                 dst[:, NPO:T].rearrange("p (s t) -> p s t", s=16),
                 tmp[:, NPO:T].rearrange("p (s t) -> p s t", s=16), 8, 16)]
        for (sv, dv, tv, L, ns) in regs:
            def sl(k):
                return sv[:, k:k + L] if ns is None else sv[:, :, k:k + L]
            ts(tv, sl(0), cst[:, wcol:wcol + 1], ALU.mult, [key_src, "cst"], [key_tmp])
            stt(tv, sl(1), cst[:, wcol + 1:wcol + 2], tv, ALU.mult, ALU.add, [key_src, "cst", key_tmp], [key_tmp])
            stt(dv, sl(2), cst[:, wcol + 2:wcol + 3], tv, ALU.mult, ALU.add, [key_src, "cst", key_tmp], [key_dst])

    if True:
        cbuc = AT("cbuc", [128, 8, T], BF16, R1)
        ccs = [t5("ccs%d" % i, [128, T]) for i in range(2)]
        ue = [t5("ue%d" % i, [128, EXT]) for i in range(2)]
        uc = [t5("uc%d" % i, [128, T]) for i in range(2)]
        ctmp = t5("ctmp", [128, T])
        uo = t5("uo", [NCOL, DCV])
        for b in range(2):
            memset(ue[b][:, 0:2], 0.0, [("ue", b)])
        for c2 in range(4):
            blk, rk = next_block()
            for j in range(2):
                banks = next_banks()
                mm_a(blk, rk, j, 128, 16, nT, "nT", NT_OWN, banks)
                for i, (t0, tn) in enumerate(NT_OWN):
                    act(ccs[j][:, t0:t0 + tn], ps[:, banks[i], 0:tn], AF.Copy, [PSB(banks[i])], [("ccs", j)])
            prefetch()
            blk, rk = next_block()
            for j in range(2):
                c = 2 * c2 + j
                banks = next_banks()
                mm_a(blk, rk, j, 128, 16, nT, "nT", NT_OWN, banks)
                ues = ue[j][:, EXT - 160:EXT].rearrange("p (s t) -> p s t", s=16)
                cp(ues[:, :, 0:2], sprev_u[:, c, :].rearrange("p (s r) -> p s r", s=16), ["sprev_u"], [("ue", j)])
                for i, (t0, tn) in enumerate(NT_OWN):
                    pn = min(tn, NPO - t0)
                    tt(ue[j][:, 2 + t0:2 + t0 + pn], ps[:, banks[i], 0:pn], ccs[j][:, t0:t0 + pn], ALU.mult,
                       [PSB(banks[i]), ("ccs", j)], [("ue", j)])
                    if pn < tn:
                        tt(ues[:, :, 2:10], ps[:, banks[i], pn:tn].rearrange("p (s t) -> p s t", s=16),
                           ccs[j][:, NPO:T].rearrange("p (s t) -> p s t", s=16), ALU.mult,
                           [PSB(banks[i]), ("ccs", j)], [("ue", j)])
                conv3(uc[j], ue[j], C_CW + 3 * c, ("ue", j), ("uc", j), ctmp, "ctmp")
                cp(ucol[:, c, 0:2], ue[j][:, NPO:NPO + 2], [("ue", j)], ["ucol"])
                cp(ucol[:, c, 2:NCOL].rearrange("p (s r) -> p s r", s=16), ues[:, :, 8:10], [("ue", j)], ["ucol"])
            prefetch()
            blk, rk = next_block()
            for j in range(2):
                c = 2 * c2 + j
                banks = next_banks()
                mm_a(blk, rk, j, 128, 16, nT, "nT", NT_OWN, banks)
                for i, (t0, tn) in enumerate(NT_OWN):
                    tt(cbuc[:, c, t0:t0 + tn], ps[:, banks[i], 0:tn], uc[j][:, t0:t0 + tn], ALU.mult,
                       [PSB(banks[i]), ("uc", j)], ["cbuc"])
            prefetch()
        for c in range(8):
            tr(ps[0:NCOL, 6 + c // 4, (c % 4) * 128:(c % 4 + 1) * 128], ucol[:, c, :], identf[:, :],
               ["ucol", "identf"], [PSB(6 + c // 4)])
        for hb in range(2):
            cp(uo[:, hb * 512:(hb + 1) * 512], ps[0:NCOL, 6 + hb, :], [PSB(6 + hb)], ["uo"])
        dma(o_convp, uo[0:2, :], "oc", r=["uo"])
        dma(o_convs, uo[2:NCOL, :], "oc", r=["uo"])
        for c4 in range(4):
            for gh in range(2):
                gab, gak = next_block()
                for jj in range(2):
                    j4 = 2 * gh + jj
                    banks = next_banks()
                    mm_a(gab, gak, jj, 128, 16, nT, "nT", NT_OWN, banks)
                    for i, (t0, tn) in enumerate(NT_OWN):
                        act(sgt[j4][:, t0:t0 + tn], ps[:, banks[i], 0:tn], AF.Sigmoid, [PSB(banks[i])], [("sgt", j4)])
                prefetch()
            co, cok = next_block()
            for j4 in range(4):
                c = 4 * c4 + j4
                banks = next_banks()
                mm_a(co, cok, j4, 128, 8, cbuc, "cbuc", NT_OWN, banks)
                for i, (t0, tn) in enumerate(NT_OWN):
                    tt(ctmp[:, t0:t0 + tn], ps[:, banks[i], 0:tn], sgt[j4][:, t0:t0 + tn], ALU.mult,
                       [PSB(banks[i]), ("sgt", j4)], ["ctmp"])
                    tt(merged[:, c, t0:t0 + tn], merged[:, c, t0:t0 + tn], ctmp[:, t0:t0 + tn], ALU.add,
                       [("merged", c), "ctmp"], [("merged", c)])
            prefetch()
        P.barrier()

    P.phase = "P6"
    n2T = AT("n2T", [128, KC, T], BF16, R0)
    if True:
        woA = AT("woA", [128, 2, 16 * 512], BF16, R1)

        def wo_g(g):
            return woA[:, g, :] if g < 2 else woB[:, g - 2, :]
        for g in range(2):
            for q in range(4):
                dma(wo_g(g)[:, q * 2048:(q + 1) * 2048], wo_r[g][:, q * 2048:(q + 1) * 2048], ("wo", g),
                    w=[("wo", g)], eng="pool")

        def add_mo(ti, t0, n, xt_t, kx):
            for g in (2, 3, 0, 1):
                bank = (4 * ti + g) % 6
                for kc in range(KC):
                    mm(ps[0:n, bank, :], merged[:, kc, t0:t0 + n], wo_g(g)[:, kc * 512:(kc + 1) * 512],
                       kc == 0, kc == KC - 1, ["mergedall", ("wo", g)], [PSB(bank)])
                tt(xt_t[0:n, g * 512:(g + 1) * 512], xt_t[0:n, g * 512:(g + 1) * 512], ps[0:n, bank, :], ALU.add,
                   [kx, PSB(bank)], [kx])
        norm_tiles(xo, TILES_OWN, n2T, "n2T", C_GFFN, "h", Bump(3 * BIGB, O_WOB), add_fn=add_mo, store_h=h_scr)
        P.barrier()

    P.phase = "P7"
    O_ACT = BIGB
    O_GCOL = O_ACT + FC * T * 2
    O_P7 = (O_GCOL + FC * NCOL * 4 + 63) // 64 * 64
    actb = AT("actb", [128, FC, T], BF16, O_ACT)
    gcol = AT("gcol", [128, FC, NCOL], F32, O_GCOL)
    if True:
        t7 = Bump(O_P7)
        sprev_g = t7("sprev_g", [128, FC, 32])
        a_sb = [t7("a_sb%d" % i, [128, T], BF16) for i in range(2)]
        gte = [t7("gte%d" % i, [128, EXT]) for i in range(2)]
        o_shared = t7.off
        stl = t7("stl2", [32, 1408])
        for pc in range(4):
            dma(stl[:, :], sffn[:, pc * 1408:(pc + 1) * 1408], None, w=["stl2"])
            for cc in range(11):
                c = pc * 11 + cc
                bnk, off = c // 16, (c % 16) * 32
                tr(ps[:, bnk, off:off + 32], stl[0:32, cc * 128:(cc + 1) * 128], identf[0:32, 0:32],
                   ["stl2", "identf"], [PSB(bnk)])
        for bnk in range(3):
            c0 = bnk * 16
            cn = min(16, FC - c0)
            cp(sprev_g[:, c0:c0 + cn, :], ps[:, bnk, 0:cn * 32].rearrange("p (c s) -> p c s", c=cn),
               [PSB(bnk)], ["sprev_g"])
        P.barrier()
        t7 = Bump(o_shared)
        gc1 = t7("gc", [128, T])
        gc = [gc1, gc1]
        ctmp = t7("ctmp7", [128, T])
        for b in range(2):
            memset(gte[b][:, 0:2], 0.0, [("gte", b)])
        for c2 in range(22):
            ba, bak = next_block()
            for j in range(2):
                banks = next_banks()
                mm_a(ba, bak, j, 128, 16, n2T, "n2T", NT_OWN, banks)
                for i, (t0, tn) in enumerate(NT_OWN):
                    act(a_sb[j][:, t0:t0 + tn], ps[:, banks[i], 0:tn], AF.Copy, [PSB(banks[i])], [("a_sb", j)])
            prefetch()
            bg, bgk = next_block()
            for j in range(2):
                c = 2 * c2 + j
                banks = next_banks()
                mm_a(bg, bgk, j, 128, 16, n2T, "n2T", NT_OWN, banks)
                gts = gte[j][:, EXT - 160:EXT].rearrange("p (s t) -> p s t", s=16)
                cp(gts[:, :, 0:2], sprev_g[:, c, :].rearrange("p (s r) -> p s r", s=16), ["sprev_g"], [("gte", j)])
                for i, (t0, tn) in enumerate(NT_OWN):
                    pn = min(tn, NPO - t0)
                    act(gte[j][:, 2 + t0:2 + t0 + pn], ps[:, banks[i], 0:pn], AF.Copy, [PSB(banks[i])], [("gte", j)])
                    if pn < tn:
                        act(gts[:, :, 2:10], ps[:, banks[i], pn:tn].rearrange("p (s t) -> p s t", s=16), AF.Copy,
                            [PSB(banks[i])], [("gte", j)])
                conv3(gc[j], gte[j], C_FW + 3 * c, ("gte", j), "gc", ctmp, "ctmp7")
                cp(gcol[:, c, 0:2], gte[j][:, NPO:NPO + 2], [("gte", j)], ["gcol"])
                cp(gcol[:, c, 2:NCOL].rearrange("p (s r) -> p s r", s=16), gts[:, :, 8:10], [("gte", j)], ["gcol"])
                act(gc[j][:, :], gc[j][:, :], AF.Silu, ["gc", "cst"], ["gc"], bias=cst[:, C_FB + c:C_FB + c + 1])
                tt(actb[:, c, :], gc[j][:, :], a_sb[j][:, :], ALU.mult, ["gc", ("a_sb", j)], ["actb"])
            prefetch()
        P.barrier()

    P.phase = "P8"
    if True:
        t8 = Bump(0, BIGB)
        go = t8("go", [NCOL, DFF])
        fft = [t8("fft%d" % i, [128, T]) for i in range(2)]
        t8b = Bump(O_P7)
        ftm = [t8b("ftm%d" % i, [128, 10, 128]) for i in range(2)]
        hpc = [t8b("hpc%d" % i, [128, 10, 128]) for i in range(2)]

        def p8_hload(c):
            b = c % 2
            kh = ("hpc", b)
            dma(hpc[b][:, 0:8, :], h_scr[0:1024, c * 128:(c + 1) * 128].rearrange("(i p) m -> p i m", p=128),
                ("hp", b), w=[kh])
            dma(hpc[b][0:12, 8, :], h_scr[1024:NPO, c * 128:(c + 1) * 128], ("hp", b), w=[kh])
            dma(hpc[b][:, 9, :], h_scr[NPO:T, c * 128:(c + 1) * 128], ("hp", b), w=[kh])
        for c in range(FC):
            bnk = 6 + (c // 4) % 2
            tr(ps[0:NCOL, bnk, (c % 4) * 128:(c % 4 + 1) * 128], gcol[:, c, :], identf[:, :], ["gcol", "identf"], [PSB(bnk)])
            if c % 4 == 3:
                cp(go[:, (c - 3) * 128:(c + 1) * 128], ps[0:NCOL, bnk, :], [PSB(bnk)], ["go"])
        dma(o_ffnp, go[0:2, :], "oc", r=["go"])
        dma(o_ffns, go[2:NCOL, :], "oc", r=["go"])
        def p8_out(c):
            b = c % 2
            for ti, (t0, n) in enumerate(TILES_OWN):
                bnk = 6 + (ti // 4) % 2
                tr(ps[0:n, bnk, (ti % 4) * 128:(ti % 4 + 1) * 128], fft[b][:, t0:t0 + n], identf[:, :],
                   [("fft", b), "identf"], [PSB(bnk)])
                if ti % 4 == 3 or ti == 9:
                    t_lo = ti - (ti % 4)
                    for tj in range(t_lo, ti + 1):
                        nn = TILES_OWN[tj][1]
                        tt(ftm[b][0:nn, tj, :], ps[0:nn, bnk, (tj % 4) * 128:(tj % 4 + 1) * 128], hpc[b][0:nn, tj, :], ALU.add,
                           [PSB(bnk), ("hpc", b)], [("ftm", b)])
            dma(ff_scr[0:1024, c * 128:(c + 1) * 128].rearrange("(i p) m -> p i m", p=128), ftm[b][:, 0:8, :],
                ("fo", b), r=[("ftm", b)])
            dma(ff_scr[1024:NPO, c * 128:(c + 1) * 128], ftm[b][0:12, 8, :], ("fo", b), r=[("ftm", b)])
            dma(ff_scr[NPO:T, c * 128:(c + 1) * 128], ftm[b][:, 9, :], ("fo", b), r=[("ftm", b)])

        for c in range(16):
            b = c % 2
            p8_hload(c)
            bA, bAk = next_block()
            banks = next_banks()
            mm_a(bA, bAk, 0, 128, 22, actb, "actb", NT_OWN, banks, first=True, last=False, kc_off=0)
            prefetch()
            bB, bBk = next_block()
            mm_a(bB, bBk, 0, 128, 22, actb, "actb", NT_OWN, banks, first=False, last=True, kc_off=22)
            prefetch()
            for i, (t0, tn) in enumerate(NT_OWN):
                act(fft[b][:, t0:t0 + tn], ps[:, banks[i], 0:tn], AF.Copy, [PSB(banks[i])], [("fft", b)])
            if c >= 1:
                p8_out(c - 1)
        p8_out(15)
        P.barrier()

    P.phase = "P9"
    if True:
        t9 = Bump(0)
        gfin = t9("gfin", [128, D])
        fa = [t9("fa%d" % i, [128, D]) for i in range(2)]
        ha = [t9("ha%d" % i, [128, D]) for i in range(2)]
        ya = [t9("ya%d" % i, [128, D]) for i in range(2)]
        sq9 = t9("sq9", [128, D], BF16)
        st9 = t9("st9", [128, 2, 4])
        dma(gfin[:], gfin_d, None, w=["gfin"])

        def load9(ti):
            if ti < len(TILES_OWN):
                t0, n = TILES_OWN[ti]
                b = ti % 2
                dma(fa[b][0:n, :], ff_scr[t0:t0 + n, :], ("f9", b), w=[("fa", b)])
        load9(0)
        load9(1)
        for ti, (t0, n) in enumerate(TILES_OWN):
            b = ti % 2
            act(sq9[0:n, :], fa[b][0:n, :], AF.Square, [("fa", b)], ["sq9", ("st9", b)], accum=st9[0:n, b, 0:1])
            act(st9[0:n, b, 1:2], st9[0:n, b, 0:1], AF.Ln, [("st9", b)], [("st91", b)], scale=1.0 / D, bias=EPS)
            act(st9[0:n, b, 2:3], st9[0:n, b, 1:2], AF.Exp, [("st91", b)], [("st92", b)], scale=-0.5)
            act(ya[b][0:n, :], fa[b][0:n, :], AF.Copy, [("fa", b), ("st92", b)], [("ya", b)], scale=st9[0:n, b, 2:3])
            tt(ya[b][0:n, :], ya[b][0:n, :], gfin[0:n, :], ALU.mult, [("ya", b), "gfin"], [("ya", b)])
            dma(y[t0:t0 + n, :], ya[b][0:n, :], ("y9", b), r=[("ya", b)], eng="act")
            load9(ti + 2)
    P.emit()
    es.close()
    return nc


_NC_CACHE = {}


def _layout_block(Wm, kc0, kcn, col0, ncol):
    sub = Wm[kc0 * 128:(kc0 + kcn) * 128, col0:col0 + ncol]
    return np.ascontiguousarray(sub.reshape(kcn, 128, ncol).transpose(1, 0, 2)).reshape(128, kcn * ncol)


def kernel(x_prompt, x_sample, state_conv, state_gla, state_ffn_conv, meta_tokens,
           norm_mix_g, w_in, conv_mix_w, w_conv_out, w_gate_up, b_gate, gla_norm_g,
           w_gla_out, w_o, norm_ffn_g, w_ffn_up, ffn_conv_w, ffn_conv_b, w_ffn_down,
           final_norm_g):
    f32 = np.float32
    A_ = lambda a: np.asarray(a, dtype=f32)
    x_prompt, x_sample = A_(x_prompt), A_(x_sample)
    mats = {"w_in": A_(w_in)[0], "w_conv_out": A_(w_conv_out)[0], "w_gla_out": A_(w_gla_out)[0],
            "w_ffn_up": A_(w_ffn_up)[0], "w_ffn_down": A_(w_ffn_down)[0]}
    pl = weight_plan()
    offs, nw = plan_offsets(pl)
    wflat = np.empty(nw, dtype=f32)
    for (name, kc0, kcn, col0, ncol), o in zip(pl, offs):
        wflat[o:o + 128 * kcn * ncol] = _layout_block(mats[name], kc0, kcn, col0, ncol).reshape(-1)
    wo = A_(w_o)[0]
    wo_r = np.stack([_layout_block(wo, 0, 16, 512 * g, 512) for g in range(4)])
    wg = np.concatenate([A_(w_gate_up)[0], A_(b_gate)[0][None, :]], axis=0)

    def pp(v, nch):
        return np.ascontiguousarray(A_(v).reshape(nch, 128).T)
    cst = np.zeros((128, NCST), dtype=f32)
    cst[:, C_GMIX:C_GMIX + 16] = pp(norm_mix_g[0], 16)
    cst[:, C_GFFN:C_GFFN + 16] = pp(norm_ffn_g[0], 16)
    cst[:, C_GGLA:C_GGLA + 4] = pp(gla_norm_g[0], 4)
    cw = A_(conv_mix_w)[0]
    fw = A_(ffn_conv_w)[0]
    for tap in range(3):
        cst[:, C_CW + tap:C_CW + 24:3] = pp(cw[tap], 8)
        cst[:, C_FW + tap:C_FW + 132:3] = pp(fw[tap], 44)
    cst[:, C_FB:C_FB + 44] = pp(ffn_conv_b[0], 44)
    gfin = np.ascontiguousarray(np.broadcast_to(A_(final_norm_g)[None, :], (128, D)))
    ident = np.eye(128, dtype=f32)
    maskp = np.triu(np.ones((128, 128), dtype=f32))
    seq = np.arange(128) // 8
    same = (seq[:, None] == seq[None, :]).astype(f32)
    masks = maskp * same
    seqmt = (seq[:, None] == np.arange(16)[None, :]).astype(f32)
    seqmb = np.ascontiguousarray(np.broadcast_to(seqmt.T.reshape(1, 16 * 128), (128, 16 * 128)))
    rmo = np.ones((128, T), dtype=f32)
    for (t0, n) in CH_OWN:
        rmo[:, t0] = 0.0
    rmo[:, NPO:T:8] = 0.0
    rmp = np.ones((128, TP), dtype=f32)
    for (t0, n) in CH_PRE:
        rmp[:, t0] = 0.0

    meta = A_(meta_tokens)
    in_maps = []
    for c in range(8):
        b, half = c // 2, c % 2
        hp_pad = np.concatenate([np.zeros((1032, D), f32), meta, x_prompt[b]], axis=0)
        base = half * 1032
        xp_c = hp_pad[base:base + TP]
        own = hp_pad[base + TP:base + TP + NPO]
        xo_c = np.concatenate([own, x_sample[16 * c:16 * c + 16].reshape(NS, D)], axis=0)
        in_maps.append({
            "xo": np.ascontiguousarray(xo_c), "xp": np.ascontiguousarray(xp_c), "wflat": wflat, "wo_r": wo_r,
            "wg": wg, "sgla": np.ascontiguousarray(A_(state_gla)[0, 16 * c:16 * c + 16].reshape(16, 4, 2, 128, 512)),
            "sconv": np.ascontiguousarray(A_(state_conv)[0, 16 * c:16 * c + 16].reshape(32, DCV)),
            "sffn": np.ascontiguousarray(A_(state_ffn_conv)[0, 16 * c:16 * c + 16].reshape(32, DFF)),
            "cst": cst, "gfin": gfin, "ident": ident, "maskp": maskp, "masks": masks,
            "rmo": rmo, "rmp": rmp, "seqmb": seqmb, "seqmt": seqmt,
        })
    if _NC_CACHE.get("prep_only"):
        return in_maps
    if "nc" not in _NC_CACHE:
        _NC_CACHE["nc"] = build_nc()
    res = run_bass_kernel_spmd(_NC_CACHE["nc"], in_maps, core_ids=list(range(8))).results

    y_prompt = np.empty((4, 2048, D), f32)
    y_sample = np.empty((128, 8, D), f32)
    conv_p = np.empty((1, 4, 2, DCV), f32)
    gla_p = np.empty((1, 4, 4, 256, 512), f32)
    ffn_p = np.empty((1, 4, 2, DFF), f32)
    conv_s = np.empty((1, 128, 2, DCV), f32)
    gla_s = np.empty((1, 128, 4, 256, 512), f32)
    ffn_s = np.empty((1, 128, 2, DFF), f32)
    for c in range(8):
        r = res[c]
        b, half = c // 2, c % 2
        if half == 0:
            y_prompt[b, 0:1016] = r["y"][20:NPO]
        else:
            y_prompt[b, 1016:2048] = r["y"][4:NPO]
            conv_p[0, b] = r["o_convp"]
            ffn_p[0, b] = r["o_ffnp"]
            gla_p[0, b] = r["o_glap"].reshape(4, 256, 512)
        y_sample[16 * c:16 * c + 16] = r["y"][NPO:T].reshape(16, 8, D)
        conv_s[0, 16 * c:16 * c + 16] = r["o_convs"].reshape(16, 2, DCV)
        ffn_s[0, 16 * c:16 * c + 16] = r["o_ffns"].reshape(16, 2, DFF)
        gla_s[0, 16 * c:16 * c + 16] = r["o_glas"].reshape(16, 4, 256, 512)
    return (y_prompt, y_sample, conv_p, gla_p, ffn_p, conv_s, gla_s, ffn_s)
```
